# Optimizing a Trainium2 kernel written in Bass

```python
import jax, jax.numpy as jnp
from jax import lax
import numpy as np

D_MODEL = 1024
BATCH = 4
SEQ = 8192
DEPTH = 2

CHUNK = 64
LEFT_CHUNKS = 8
BAND = (LEFT_CHUNKS + 1) * CHUNK
MAX_REL = 128
Q_BLOCK = 128

HEAD_DIM = 64
N_HEADS_A = D_MODEL // (2 * HEAD_DIM)
N_HEADS_B = D_MODEL // (2 * HEAD_DIM)
AB_COLS = 3 * (N_HEADS_A + N_HEADS_B) * HEAD_DIM
AB_OUT = (N_HEADS_A + N_HEADS_B) * HEAD_DIM
N_HEADS_C = D_MODEL // (2 * HEAD_DIM)
QK_NOPE = HEAD_DIM
QK_ROPE = HEAD_DIM // 2
V_DIM_C = HEAD_DIM
Q_RANK = 3 * D_MODEL // 8
KV_RANK = D_MODEL // 4
N_HEADS_D = D_MODEL // (2 * HEAD_DIM)
CD_SPLITS = [Q_RANK, KV_RANK, QK_ROPE, N_HEADS_D * HEAD_DIM, N_HEADS_D * HEAD_DIM, N_HEADS_D * HEAD_DIM, N_HEADS_D]
CD_COLS = sum(CD_SPLITS)
CD_OUT = N_HEADS_C * V_DIM_C + N_HEADS_D * HEAD_DIM
ROPE_THETA = 10000.0
D_FF = ((8 * D_MODEL // 3) + 127) // 128 * 128
CONV_WIDTH = 3
RMS_EPS = 1e-6
N_EVEN = (DEPTH + 1) // 2
N_ODD = DEPTH // 2

kernel_name = 'hybrid_chunk_stick_mla_fox_convffn'


def rmsnorm(x, g):
    x32 = x.astype(jnp.float32)
    y = x32 * lax.rsqrt(jnp.mean(x32 * x32, axis=-1, keepdims=True) + RMS_EPS)
    return (y * g.astype(jnp.float32)).astype(x.dtype)


def rope(x, positions):
    half = x.shape[-1] // 2
    inv = ROPE_THETA ** (-jnp.arange(half, dtype=jnp.float32) / half)
    ang = positions.astype(jnp.float32)[:, None] * inv[None, :]
    ang = ang.reshape((ang.shape[0],) + (1,) * (x.ndim - 3) + (half,))
    cos, sin = jnp.cos(ang), jnp.sin(ang)
    x32 = x.astype(jnp.float32)
    x1, x2 = x32[..., :half], x32[..., half:]
    return jnp.concatenate([x1 * cos - x2 * sin, x2 * cos + x1 * sin], axis=-1).astype(x.dtype)


def sweep_query_blocks(block_fn, seq):
    out = lax.map(block_fn, jnp.arange(seq // Q_BLOCK))
    nb, b, qb, h, dv = out.shape
    return jnp.transpose(out, (1, 0, 2, 3, 4)).reshape(b, nb * qb, h, dv)


def chunked_relpos_attention(q, k, v, rel_bias):
    b, s, h, d = q.shape
    pad = LEFT_CHUNKS * CHUNK
    kp = jnp.pad(k, ((0, 0), (pad, 0), (0, 0), (0, 0)))
    vp = jnp.pad(v, ((0, 0), (pad, 0), (0, 0), (0, 0)))
    qi = np.arange(CHUNK)[:, None]
    kj = np.arange(BAND)[None, :]
    rel_idx = np.clip(qi + pad - kj, -MAX_REL, MAX_REL) + MAX_REL
    bias = rel_bias[:, rel_idx].astype(jnp.float32)
    scale = d ** -0.5
    band_pos = jnp.arange(BAND)

    def chunk_fn(c):
        start = c * CHUNK
        qc = lax.dynamic_slice_in_dim(q, start, CHUNK, axis=1)
        kb = lax.dynamic_slice_in_dim(kp, start, BAND, axis=1)
        vb = lax.dynamic_slice_in_dim(vp, start, BAND, axis=1)
        logits = jnp.einsum('bqhd,bkhd->bhqk', qc, kb).astype(jnp.float32) * scale + bias
        valid = (start - pad + band_pos) >= 0
        logits = jnp.where(valid, logits, -jnp.inf)
        p = jax.nn.softmax(logits, axis=-1)
        return jnp.einsum('bhqk,bkhd->bqhd', p.astype(vb.dtype), vb)

    out = lax.map(chunk_fn, jnp.arange(s // CHUNK))
    return jnp.transpose(out, (1, 0, 2, 3, 4)).reshape(b, s, h, d)


def stick_breaking_attention(q, k, v):
    b, s, h, d = q.shape
    scale = d ** -0.5
    key_pos = jnp.arange(s)

    def block_fn(blk):
        start = blk * Q_BLOCK
        qb = lax.dynamic_slice_in_dim(q, start, Q_BLOCK, axis=1)
        z = jnp.einsum('bqhd,bkhd->bhqk', qb, k).astype(jnp.float32) * scale
        q_pos = start + jnp.arange(Q_BLOCK)
        mask = key_pos[None, :] < q_pos[:, None]
        log_keep = jnp.where(mask, jax.nn.log_sigmoid(-z), 0.0)
        later = lax.cumsum(log_keep, axis=3, reverse=True) - log_keep
        w = jnp.where(mask, jnp.exp(jax.nn.log_sigmoid(z) + later), 0.0)
        return jnp.einsum('bhqk,bkhd->bqhd', w.astype(v.dtype), v)

    return sweep_query_blocks(block_fn, s)


def mla_attention(q_nope, q_rope, k_nope, k_rope, v):
    s = q_nope.shape[1]
    scale = (QK_NOPE + QK_ROPE) ** -0.5
    key_chunk = jnp.arange(s) // CHUNK

    def block_fn(blk):
        start = blk * Q_BLOCK
        qn = lax.dynamic_slice_in_dim(q_nope, start, Q_BLOCK, axis=1)
        qr = lax.dynamic_slice_in_dim(q_rope, start, Q_BLOCK, axis=1)
        logits = (jnp.einsum('bqhd,bkhd->bhqk', qn, k_nope)
                  + jnp.einsum('bqhr,bkr->bhqk', qr, k_rope)).astype(jnp.float32) * scale
        q_chunk = (start + jnp.arange(Q_BLOCK)) // CHUNK
        mask = key_chunk[None, :] <= q_chunk[:, None]
        p = jax.nn.softmax(jnp.where(mask, logits, -jnp.inf), axis=-1)
        return jnp.einsum('bhqk,bkhd->bqhd', p.astype(v.dtype), v)

    return sweep_query_blocks(block_fn, s)


def forgetting_attention(q, k, v, log_f):
    b, s, h, d = q.shape
    scale = d ** -0.5
    cum = jnp.transpose(jnp.cumsum(log_f, axis=1), (0, 2, 1))
    key_pos = jnp.arange(s)

    def block_fn(blk):
        start = blk * Q_BLOCK
        qb = lax.dynamic_slice_in_dim(q, start, Q_BLOCK, axis=1)
        cum_q = lax.dynamic_slice_in_dim(cum, start, Q_BLOCK, axis=2)
        logits = (jnp.einsum('bqhd,bkhd->bhqk', qb, k).astype(jnp.float32) * scale
                  + cum_q[..., None] - cum[:, :, None, :])
        q_pos = start + jnp.arange(Q_BLOCK)
        mask = key_pos[None, :] <= q_pos[:, None]
        p = jax.nn.softmax(jnp.where(mask, logits, -jnp.inf), axis=-1)
        return jnp.einsum('bhqk,bkhd->bqhd', p.astype(v.dtype), v)

    return sweep_query_blocks(block_fn, s)


def chunk_stick_layer(x, norm_g, w_in, rel_bias, w_o):
    b, s, _ = x.shape
    h = rmsnorm(x, norm_g)
    proj = h @ w_in
    qa, ka, va, qb, kb, vb = jnp.split(proj, 6, axis=-1)
    shp = (b, s, N_HEADS_A, HEAD_DIM)
    oa = chunked_relpos_attention(qa.reshape(shp), ka.reshape(shp), va.reshape(shp), rel_bias)
    shp_b = (b, s, N_HEADS_B, HEAD_DIM)
    ob = stick_breaking_attention(qb.reshape(shp_b), kb.reshape(shp_b), vb.reshape(shp_b))
    o = jnp.concatenate([oa, ob], axis=2).reshape(b, s, AB_OUT)
    return x + o @ w_o


def mla_fox_layer(x, norm_g, w_in, q_norm, w_uq, kv_norm, w_ukv, b_f, w_o):
    b, s, _ = x.shape
    h = rmsnorm(x, norm_g)
    proj = h @ w_in
    c_q, c_kv, k_rope, q_d, k_d, v_d, f_logit = jnp.split(proj, list(np.cumsum(CD_SPLITS)[:-1]), axis=-1)
    positions = jnp.arange(s)
    q_c = (rmsnorm(c_q, q_norm) @ w_uq).reshape(b, s, N_HEADS_C, QK_NOPE + QK_ROPE)
    kv_c = (rmsnorm(c_kv, kv_norm) @ w_ukv).reshape(b, s, N_HEADS_C, QK_NOPE + V_DIM_C)
    q_nope, q_rope = q_c[..., :QK_NOPE], rope(q_c[..., QK_NOPE:], positions)
    k_nope, v_c = kv_c[..., :QK_NOPE], kv_c[..., QK_NOPE:]
    oc = mla_attention(q_nope, q_rope, k_nope, rope(k_rope, positions), v_c)
    log_f = jax.nn.log_sigmoid((f_logit + b_f).astype(jnp.float32))
    shp = (b, s, N_HEADS_D, HEAD_DIM)
    od = forgetting_attention(q_d.reshape(shp), k_d.reshape(shp), v_d.reshape(shp), log_f)
    o = jnp.concatenate([oc, od], axis=2).reshape(b, s, CD_OUT)
    return x + o @ w_o


def conv_gated_mlp(x, norm_g, w_gate, w_up, conv_w, conv_b, w_down):
    h = rmsnorm(x, norm_g)
    g = h @ w_gate
    g = lax.conv_general_dilated(g, conv_w[:, None, :], window_strides=(1,),
                                 padding=[(CONV_WIDTH - 1, 0)],
                                 dimension_numbers=('NWC', 'WIO', 'NWC'),
                                 feature_group_count=g.shape[-1]) + conv_b
    y = jax.nn.silu(g) * (h @ w_up)
    return x + y @ w_down


def setup_inputs(seed: int = 0) -> dict:
    key = jax.random.key(seed)
    ks = iter(jax.random.split(key, 32))

    def nrm(shape, scale):
        return jax.random.normal(next(ks), shape, jnp.float32) * scale

    def gain(shape):
        return 1.0 + nrm(shape, 0.05)

    return {
        'x': nrm((BATCH, SEQ, D_MODEL), 1.0),
        'ab_norm': gain((N_EVEN, D_MODEL)),
        'ab_w_in': nrm((N_EVEN, D_MODEL, AB_COLS), D_MODEL ** -0.5),
        'ab_rel_bias': nrm((N_EVEN, N_HEADS_A, 2 * MAX_REL + 1), 0.5),
        'ab_w_o': nrm((N_EVEN, AB_OUT, D_MODEL), AB_OUT ** -0.5),
        'cd_norm': gain((N_ODD, D_MODEL)),
        'cd_w_in': nrm((N_ODD, D_MODEL, CD_COLS), D_MODEL ** -0.5),
        'cd_q_norm': gain((N_ODD, Q_RANK)),
        'cd_w_uq': nrm((N_ODD, Q_RANK, N_HEADS_C * (QK_NOPE + QK_ROPE)), Q_RANK ** -0.5),
        'cd_kv_norm': gain((N_ODD, KV_RANK)),
        'cd_w_ukv': nrm((N_ODD, KV_RANK, N_HEADS_C * (QK_NOPE + V_DIM_C)), KV_RANK ** -0.5),
        'cd_b_f': 3.0 + nrm((N_ODD, N_HEADS_D), 0.5),
        'cd_w_o': nrm((N_ODD, CD_OUT, D_MODEL), CD_OUT ** -0.5),
        'ffn_norm': gain((DEPTH, D_MODEL)),
        'ffn_w_gate': nrm((DEPTH, D_MODEL, D_FF), D_MODEL ** -0.5),
        'ffn_w_up': nrm((DEPTH, D_MODEL, D_FF), D_MODEL ** -0.5),
        'ffn_conv_w': nrm((DEPTH, CONV_WIDTH, D_FF), CONV_WIDTH ** -0.5),
        'ffn_conv_b': nrm((DEPTH, D_FF), 0.02),
        'ffn_w_down': nrm((DEPTH, D_FF, D_MODEL), D_FF ** -0.5),
        'final_norm': gain((D_MODEL,)),
    }


def reference(x, ab_norm, ab_w_in, ab_rel_bias, ab_w_o,
              cd_norm, cd_w_in, cd_q_norm, cd_w_uq, cd_kv_norm, cd_w_ukv, cd_b_f, cd_w_o,
              ffn_norm, ffn_w_gate, ffn_w_up, ffn_conv_w, ffn_conv_b, ffn_w_down,
              final_norm):
    for layer in range(DEPTH):
        i = layer // 2
        if layer % 2 == 0:
            x = chunk_stick_layer(x, ab_norm[i], ab_w_in[i], ab_rel_bias[i], ab_w_o[i])
        else:
            x = mla_fox_layer(x, cd_norm[i], cd_w_in[i], cd_q_norm[i], cd_w_uq[i], cd_kv_norm[i],
                              cd_w_ukv[i], cd_b_f[i], cd_w_o[i])
        x = conv_gated_mlp(x, ffn_norm[layer], ffn_w_gate[layer], ffn_w_up[layer],
                           ffn_conv_w[layer], ffn_conv_b[layer], ffn_w_down[layer])
    return rmsnorm(x, final_norm)
```

```python
import contextlib
import os
import numpy as np
import ml_dtypes
import concourse.bass as bass
import concourse.mybir as mybir
from concourse.bass_utils import run_bass_kernel_spmd

F32 = mybir.dt.float32
BF16 = mybir.dt.bfloat16
AF = mybir.ActivationFunctionType
ALU = mybir.AluOpType

D = 1024
DFF = 2816
NFF = DFF // 128
EPS = 1e-6
NEG = -30000.0
ENGS = ['sp', 'act', 'dve', 'pool', 'pe']


class Buf:
    __slots__ = ('w', 'r', 'rd')

    def __init__(self):
        self.w = None
        self.r = {}
        self.rd = []


class Op:
    __slots__ = ('eng', 'fn', 'deps', 'flag', 'sem', 'val', 'dma', 'slot')


class Sched:
    def __init__(self, nc, nd=40):
        self.nc = nc
        self.q = {e: [] for e in ENGS}
        self.dmas = []
        self.ND = nd
        self.pending_dma = []
        self.last_real = {}

    def op(self, eng, fn, r=(), w=()):
        o = Op()
        o.eng = eng
        o.fn = fn
        o.flag = False
        o.dma = False
        o.sem = None
        o.val = 0
        deps = {}

        def add(d, strong):
            if d is None or d is o:
                return
            k = id(d)
            if k in deps:
                deps[k] = (d, deps[k][1] or strong)
            else:
                deps[k] = (d, strong)
        for b in r:
            add(b.w, True)
        for b in w:
            add(b.w, True)
            for d in b.r.values():
                add(d, False)
            for d in b.rd:
                add(d, True)
        o.deps = list(deps.values())
        self.q[eng].append(o)
        self.last_real[eng] = o
        return o

    def _post(self, o, r, w):
        for b in r:
            if o.dma:
                b.rd.append(o)
            else:
                b.r[o.eng] = o
        for b in w:
            b.w = o
            b.r = {}
            b.rd = []

    def cop(self, eng, fn, r=(), w=()):
        o = self.op(eng, fn, r, w)
        self._post(o, r, w)
        return o

    def dma(self, out, in_, r=(), w=(), eng='sp'):
        o = self.op(eng, lambda e: e.dma_start(out=out, in_=in_), r, w)
        o.dma = True
        o.flag = True
        n = len(self.dmas)
        o.slot = n % self.ND
        o.val = 16 * (n // self.ND + 1)
        if n >= self.ND:
            o.deps.append((self.dmas[n - self.ND], True))
        self.dmas.append(o)
        self.pending_dma.append(o)
        self._post(o, r, w)
        return o

    def barrier(self):
        last = list(self.last_real.values())
        self.last_real = {}
        pend = self.pending_dma
        self.pending_dma = []
        for e in ENGS:
            o = Op()
            o.eng = e
            o.fn = None
            o.flag = False
            o.dma = False
            o.deps = [(d, True) for d in last if d.eng != e or d.dma] + [(d, True) for d in pend]
            self.q[e].append(o)

    def mm(self, out, lhsT, rhs, start, stop, r=(), w=(), skip=False):
        return self.cop('pe', lambda e: e.matmul(out, lhsT, rhs, start=start, stop=stop, skip_group_check=skip), r, w)

    def tr(self, out, in_, ident, r=(), w=()):
        return self.cop('pe', lambda e: e.transpose(out, in_, ident), r, w)

    def act(self, out, in_, func, r=(), w=(), **kw):
        return self.cop('act', lambda e: e.activation(out, in_, func, **kw), r, w)

    def ts(self, eng, out, in0, s1, s2, op0, op1=None, r=(), w=()):
        if op1 is None and eng == 'pool' and op0 == ALU.mult:
            return self.cop(eng, lambda e: e.tensor_scalar(out, in0, s1, 0.0, ALU.mult, ALU.add), r, w)
        if op1 is None:
            return self.cop(eng, lambda e: e.tensor_scalar(out, in0, s1, None, op0), r, w)
        return self.cop(eng, lambda e: e.tensor_scalar(out, in0, s1, s2, op0, op1), r, w)

    def tt(self, eng, out, in0, in1, op, r=(), w=()):
        return self.cop(eng, lambda e: e.tensor_tensor(out, in0, in1, op), r, w)

    def stt(self, eng, out, in0, scalar, in1, op0, op1, r=(), w=()):
        return self.cop(eng, lambda e: e.scalar_tensor_tensor(out, in0, scalar, in1, op0, op1), r, w)

    def copy(self, eng, out, in_, r=(), w=()):
        if eng == 'act':
            return self.cop('act', lambda e: e.copy(out, in_), r, w)
        return self.cop(eng, lambda e: e.tensor_copy(out, in_), r, w)

    def memset(self, eng, ap, val, w=()):
        return self.cop(eng, lambda e: e.memset(ap, val), (), w)

    @staticmethod
    def _needs(o, d, strong):
        if d.dma:
            return True
        if d.eng != o.eng:
            return True
        if o.eng == 'pe':
            return False
        return strong

    def emit(self, es):
        nc = self.nc
        for e in ENGS:
            for o in self.q[e]:
                for (d, strong) in o.deps:
                    if self._needs(o, d, strong):
                        d.flag = True
        esem = {e: es.enter_context(nc.semaphore('S_' + e)) for e in ENGS}
        dsem = [es.enter_context(nc.semaphore('D%d' % i)) for i in range(self.ND)]
        for e in ENGS:
            cnt = 0
            for o in self.q[e]:
                if o.dma:
                    o.sem = dsem[o.slot]
                elif o.flag:
                    cnt += 1
                    o.sem = esem[e]
                    o.val = cnt
        block = es.enter_context(nc.Block())

        def run(e, eng):
            waited = {}
            for o in self.q[e]:
                for (d, strong) in o.deps:
                    if not self._needs(o, d, strong):
                        continue
                    k = id(d.sem)
                    if waited.get(k, 0) >= d.val:
                        continue
                    eng.wait_ge(d.sem, d.val)
                    waited[k] = d.val
                if o.fn is None:
                    continue
                inst = o.fn(eng)
                if o.flag:
                    inst.then_inc(o.sem, 16 if o.dma else 1)

        block.sync(lambda eng: run('sp', eng))
        block.scalar(lambda eng: run('act', eng))
        block.vector(lambda eng: run('dve', eng))
        block.gpsimd(lambda eng: run('pool', eng))
        block.tensor(lambda eng: run('pe', eng))


def build(S, stop_after=None):
    nc = bass.Bass("TRN2", target_bir_lowering=False)
    NB = S // 512
    NT = S // 128

    def din(name, shape, dt=F32):
        return nc.dram_tensor(name, list(shape), dt, kind="ExternalInput").ap()

    def dscr(name, shape, dt):
        return nc.dram_tensor(name, list(shape), dt).ap()

    x_in = din("x", [S, D])
    ab_w_in = din("ab_w_in", [D, 3072])
    ab_w_o = din("ab_w_o", [D, D])
    ab_norm = din("ab_norm", [128, 8])
    relE = din("relE", [8, 128, 1408])
    cd_w_in = din("cd_w_in", [D, 2216])
    cd_w_o = din("cd_w_o", [D, D])
    cd_norm = din("cd_norm", [128, 8])
    cd_q_norm = din("cd_q_norm", [128, 3])
    cd_kv_norm = din("cd_kv_norm", [128, 2])
    cd_w_uq = din("cd_w_uq", [384, 768])
    cd_w_ukv = din("cd_w_ukv", [256, 1024])
    cd_b_f = din("cd_b_f", [8, 1])
    ffn_norm = din("ffn_norm", [2, 128, 8])
    ffn_w_gate = din("ffn_w_gate", [2, D, DFF])
    ffn_w_up = din("ffn_w_up", [2, D, DFF])
    ffn_w_down = din("ffn_w_down", [2, DFF, D])
    ffn_conv = din("ffn_conv", [2, 128, NFF, 4])
    final_norm = din("final_norm", [1, D])
    c_ident = din("c_ident", [128, 128], BF16)
    c_tri = din("c_tri", [2, 128, 128], BF16)
    c_maskA = din("c_maskA", [8, 128, 512])
    c_maskD = din("c_maskD", [3, 4, 128, 512], BF16)
    c_m01 = din("c_m01", [2, 128, 128], BF16)
    c_sel = din("c_sel", [128, 64])
    c_rope = din("c_rope", [2, 32, S])
    out = nc.dram_tensor("out", [S, D], F32, kind="ExternalOutput").ap()

    QT = dscr("QT_scr", [16, 96, S], BF16)
    KT = dscr("KT_scr", [16, 96, S], BF16)
    VS = dscr("V_scr", [16, 128, NT, 65], BF16)
    OS = dscr("O_scr", [NT, 128, D], BF16)
    KR = dscr("KR_scr", [32, S], BF16)
    OTS = dscr("OT_scr", [16, 64, S], BF16)
    xa = dscr("xa", [S, D], F32)
    xb = dscr("xb", [S, D], F32)

    K = Sched(nc)
    es_top = contextlib.ExitStack()

    uid = [0]

    def sb(es, name, shape, dt):
        uid[0] += 1
        return es.enter_context(nc.sbuf_tensor(f"{name}_{uid[0]}", list(shape), dt))

    def ps(es, name, shape, dt):
        uid[0] += 1
        return es.enter_context(nc.psum_tensor(f"{name}_{uid[0]}", list(shape), dt))

    ident = sb(es_top, "ident", [128, 128], BF16)
    b_ident = Buf()
    K.dma(ident[:], c_ident[:, :], w=[b_ident])

    def load_weight_folded(es, Wt, wbuf, src, ncols, g_sb, b_g, kchunks, colchunk, name):
        st = [sb(es, f"{name}_st{i}", [128, colchunk], F32) for i in range(2)]
        bst = [Buf(), Buf()]
        i = 0
        for kc in range(kchunks):
            for c0 in range(0, ncols, colchunk):
                cw = min(colchunk, ncols - c0)
                j = i % 2
                K.dma(st[j][:, 0:cw], src[kc * 128:(kc + 1) * 128, c0:c0 + cw], w=[bst[j]])
                eng = 'dve' if i % 2 == 0 else 'pool'
                if g_sb is None:
                    K.copy(eng, Wt[:, kc, c0:c0 + cw], st[j][:, 0:cw], r=[bst[j]], w=[wbuf])
                else:
                    K.ts(eng, Wt[:, kc, c0:c0 + cw], st[j][:, 0:cw], g_sb[:, kc:kc + 1], None, ALU.mult,
                         r=[bst[j], b_g], w=[wbuf])
                i += 1

    def norm_block(xt, bx, ntile, ss, lnv, rstd, junk, bjunk, bss, width=D):
        for t in range(ntile):
            K.act(junk[:, 0:width], xt[:, t, 0:width], AF.Square, r=[bx[t]], w=[bjunk, bss],
                  accum_out=ss[:, t:t + 1])
        K.act(lnv[:, 0:ntile], ss[:, 0:ntile], AF.Ln, r=[bss], w=[bss], bias=EPS, scale=1.0 / width)
        K.act(rstd[:, 0:ntile], lnv[:, 0:ntile], AF.Exp, r=[bss], w=[bss], scale=-0.5)

    def phase_inproj_ab(xsrc):
        with contextlib.ExitStack() as es:
            W = sb(es, "W_ab", [128, 8, 3072], BF16)
            bW = Buf()
            g_sb = sb(es, "g_ab", [128, 8], F32)
            b_g = Buf()
            K.dma(g_sb[:], ab_norm[:, :], w=[b_g])
            load_weight_folded(es, W, bW, ab_w_in, 3072, g_sb, b_g, 8, 1536, "wab")
            xt = [sb(es, f"xt{i}", [128, 4, D], F32) for i in range(2)]
            bxt = [[Buf() for _ in range(4)] for _ in range(2)]
            junk = sb(es, "junk", [128, D], BF16)
            bjunk = Buf()
            ss = sb(es, "ss", [128, 4], F32)
            lnv = sb(es, "lnv", [128, 4], F32)
            rstd = sb(es, "rstd", [128, 4], F32)
            bss = Buf()
            xn = sb(es, "xn", [128, 4, D], BF16)
            bxn = [Buf() for _ in range(4)]
            hT = [sb(es, f"hT{i}", [128, 8, 512], BF16) for i in range(2)]
            bhT = [Buf(), Buf()]
            qst = [sb(es, f"qst{i}", [128, 512], BF16) for i in range(4)]
            bqst = [Buf() for _ in range(4)]
            vst = [sb(es, f"vst{i}", [128, 16, 4, 65], BF16) for i in range(2)]
            bvst = [Buf(), Buf()]
            tp = [ps(es, f"tp{i}", [128, 512], BF16) for i in range(2)]
            btp = [Buf(), Buf()]
            pm = [ps(es, f"pm{i}", [128, 512], F32) for i in range(4)]
            bpm = [Buf() for _ in range(4)]
            for i in range(2):
                K.memset('pool', vst[i][:, :, :, 64:65], 1.0, w=[bvst[i]])
            ipm = 0
            iq = 0
            for b in range(NB):
                xb_ = b % 2
                src = xsrc[b * 512:(b + 1) * 512, :].rearrange("(t p) c -> p t c", p=128)
                for t in range(4):
                    K.dma(xt[xb_][:, t, :], src[:, t, :], w=[bxt[xb_][t]])
                norm_block(xt[xb_], bxt[xb_], 4, ss, lnv, rstd, junk, bjunk, bss)
                for t in range(4):
                    K.ts('dve' if t % 2 == 0 else 'pool', xn[:, t, :], xt[xb_][:, t, :], rstd[:, t:t + 1], None,
                         ALU.mult, r=[bxt[xb_][t], bss], w=[bxn[t]])
                hb = b % 2
                for kc in range(8):
                    j = kc % 2
                    for t in range(4):
                        K.tr(tp[j][:, t * 128:(t + 1) * 128], xn[:, t, kc * 128:(kc + 1) * 128], ident[:],
                             r=[bxn[t], b_ident], w=[btp[j]])
                    K.copy('act' if kc % 2 == 0 else 'dve', hT[hb][:, kc, :], tp[j][:, :], r=[btp[j]], w=[bhT[hb]])
                for (c0, dst, h0) in ((0, QT, 0), (512, KT, 0), (1536, QT, 8), (2048, KT, 8)):
                    for hp in range(4):
                        bk = ipm % 4
                        ipm += 1
                        for kc in range(8):
                            K.mm(pm[bk][:, :], W[:, kc, c0 + hp * 128:c0 + (hp + 1) * 128], hT[hb][:, kc, :],
                                 kc == 0, kc == 7, r=[bW, bhT[hb]], w=[bpm[bk]])
                        qi = iq % 4
                        iq += 1
                        K.copy('act' if qi % 2 == 0 else 'dve', qst[qi][:, :], pm[bk][:, :], r=[bpm[bk]], w=[bqst[qi]])
                        for hh in range(2):
                            K.dma(dst[h0 + hp * 2 + hh, 0:64, b * 512:(b + 1) * 512], qst[qi][hh * 64:(hh + 1) * 64, :],
                                  r=[bqst[qi]])
                vb_ = b % 2
                for (c0, h0) in ((1024, 0), (2560, 8)):
                    for t in range(4):
                        bk = ipm % 4
                        ipm += 1
                        for kc in range(8):
                            K.mm(pm[bk][:, :], hT[hb][:, kc, t * 128:(t + 1) * 128], W[:, kc, c0:c0 + 512],
                                 kc == 0, kc == 7, r=[bW, bhT[hb]], w=[bpm[bk]])
                        K.copy('act' if t % 2 == 0 else 'dve', vst[vb_][:, h0:h0 + 8, t, 0:64],
                               pm[bk][:, :].rearrange("p (h d) -> p h d", d=64), r=[bpm[bk]], w=[bvst[vb_]])
                K.dma(VS[:, :, b * 4:(b + 1) * 4, :].rearrange("h p t c -> p h t c"), vst[vb_][:, :, :, :], r=[bvst[vb_]])
            K.barrier()

    def attn_softmax_heads(es, heads, krows, scale, kind, maskset=None, bias_src=None, kr_rows=False):
        kt_sb = [sb(es, f"kt{i}", [96, S], BF16) for i in range(2)]
        qt_sb = [sb(es, f"qt{i}", [96, S], BF16) for i in range(2)]
        v_sb = [sb(es, f"v{i}", [128, NT, 65], BF16) for i in range(2)]
        o_sb = [sb(es, f"o{i}", [128, NT, 64], BF16) for i in range(2)]
        bkt = [Buf(), Buf()]
        bqt = [Buf(), Buf()]
        bv = [Buf(), Buf()]
        bo = [Buf(), Buf()]
        pt = [sb(es, f"pt{i}", [128, 512], BF16) for i in range(4)]
        bpt = [Buf() for _ in range(4)]
        rc = sb(es, "rc", [128, 4], F32)
        brc = Buf()
        z = [ps(es, f"z{i}", [128, 512], F32) for i in range(3)]
        bz = [Buf() for _ in range(3)]
        ob = [ps(es, f"ob{i}", [128, 512], F32) for i in range(4)]
        bob = [Buf() for _ in range(4)]
        if kind == 'A':
            bia = [sb(es, f"bia{i}", [128, 8, 512], BF16) for i in range(2)]
            bbia = [Buf(), Buf()]
            bst = [sb(es, f"bst{i}", [128, 512], F32) for i in range(2)]
            bbst = [Buf(), Buf()]
            mA = sb(es, "mA", [128, 8, 512], F32)
            bmA = Buf()
            for r_ in range(8):
                K.dma(mA[:, r_, :], c_maskA[r_, :, :], w=[bmA])
        else:
            m01 = sb(es, "m01", [128, 128], BF16)
            bm01 = Buf()
            K.dma(m01[:, :], c_m01[maskset - 1, :, :], w=[bm01])
        if kr_rows:
            for i in range(2):
                K.dma(kt_sb[i][64:96, :], KR[:, :], w=[bkt[i]])
        items = []
        for hi, h in enumerate(heads):
            for qb in range(NB):
                if kind == 'A':
                    tiles = [(qb * 4 - 4 + r_, r_) for r_ in range(8) if qb * 4 - 4 + r_ >= 0]
                else:
                    tiles = [(kt_, (kt_ - qb * 4) if kt_ >= qb * 4 else None) for kt_ in range(qb * 4 + 4)]
                for ti, (kt_, r_) in enumerate(tiles):
                    items.append((hi, h, qb, kt_, r_, ti, len(tiles)))
        kk = 96 if kr_rows else krows

        def head_prologue(hi, h):
            hb = hi % 2
            K.dma(kt_sb[hb][0:krows, :], KT[h, 0:krows, :], w=[bkt[hb]])
            K.dma(qt_sb[hb][0:kk, :], QT[h, 0:kk, :], w=[bqt[hb]])
            K.dma(v_sb[hb][:, :, :], VS[h, :, :, :], w=[bv[hb]])
            if kind == 'A':
                for r_ in range(8):
                    j = r_ % 2
                    K.dma(bst[j][:, :], bias_src[hi, :, 896 - 128 * r_:896 - 128 * r_ + 512], w=[bbst[j]])
                    K.stt('dve', bia[hb][:, r_, :], bst[j][:, :], 1.0 / scale, mA[:, r_, :],
                          ALU.mult, ALU.add, r=[bbst[j], bmA], w=[bbia[hb]])

        def stage1(i):
            hi, h, qb, kt_, r_, ti, nt_ = items[i]
            hb = hi % 2
            if qb == 0 and ti == 0:
                head_prologue(hi, h)
            zi = i % 3
            if kind == 'A':
                K.mm(z[zi][:, :], kt_sb[hb][0:kk, kt_ * 128:(kt_ + 1) * 128], qt_sb[hb][0:kk, qb * 512:(qb + 1) * 512],
                     True, False, r=[bkt[hb], bqt[hb]], w=[bz[zi]])
                K.mm(z[zi][:, :], ident[:], bia[hb][:, r_, :], False, True, r=[b_ident, bbia[hb]], w=[bz[zi]])
                K.act(pt[i % 4][:, :], z[zi][:, :], AF.Exp, r=[bz[zi]], w=[bpt[i % 4]], scale=scale)
            else:
                c0 = 0 if r_ is None else 128 * r_
                K.mm(z[zi][:, c0:512], kt_sb[hb][0:kk, kt_ * 128:(kt_ + 1) * 128],
                     qt_sb[hb][0:kk, qb * 512 + c0:(qb + 1) * 512], True, True, r=[bkt[hb], bqt[hb]], w=[bz[zi]])
                K.act(pt[i % 4][:, c0:512], z[zi][:, c0:512], AF.Exp, r=[bz[zi]], w=[bpt[i % 4]], scale=scale)
                if r_ is not None:
                    K.tt('dve', pt[i % 4][:, c0:c0 + 128], pt[i % 4][:, c0:c0 + 128], m01[:, :], ALU.mult,
                         r=[bpt[i % 4], bm01], w=[bpt[i % 4]])

        def stage2(i):
            hi, h, qb, kt_, r_, ti, nt_ = items[i]
            hb = hi % 2
            pi = i % 4
            for qs in range(4):
                if kind != 'A' and r_ is not None and qs < r_:
                    continue
                last = (ti == nt_ - 1) if kind == 'A' else (r_ is not None and r_ == qs)
                K.mm(ob[qs][:, 0:65], pt[pi][:, qs * 128:(qs + 1) * 128], v_sb[hb][:, kt_, 0:65],
                     ti == 0, last, r=[bpt[pi], bv[hb]], w=[bob[qs]])
            if ti == nt_ - 1:
                for qs in range(4):
                    K.cop('dve', lambda e, o_=rc[:, qs:qs + 1], i_=ob[qs][:, 64:65]: e.reciprocal(o_, i_),
                          r=[bob[qs]], w=[brc])
                    K.ts('dve', o_sb[hb][:, qb * 4 + qs, :], ob[qs][:, 0:64], rc[:, qs:qs + 1], None, ALU.mult,
                         r=[bob[qs], brc], w=[bo[hb]])
                if qb == NB - 1:
                    K.dma(OS[:, :, h * 64:(h + 1) * 64].rearrange("t p d -> p t d"), o_sb[hb][:, :, :], r=[bo[hb]])

        n_it = len(items)
        for i in range(n_it + 2):
            if i < n_it:
                stage1(i)
            if i >= 2:
                stage2(i - 2)

    def attn_softmax_T(es, heads, krows, scale, maskidx, kr_rows):
        kt_sb = [sb(es, f"tkt{i}", [96, S], BF16) for i in range(2)]
        qt_sb = [sb(es, f"tqt{i}", [96, S], BF16) for i in range(2)]
        v2 = [sb(es, f"tv{i}", [128, NT, 128], BF16) for i in range(2)]
        oT_sb = [sb(es, f"toT{i}", [64, S], BF16) for i in range(2)]
        bkt = [Buf(), Buf()]
        bqt = [Buf(), Buf()]
        bv = [Buf(), Buf()]
        bo = [Buf(), Buf()]
        pt = [sb(es, f"tpt{i}", [128, 512], BF16) for i in range(4)]
        bpt = [Buf() for _ in range(4)]
        X = [sb(es, f"tX{i}", [128, 512], F32) for i in range(2)]
        bX = [Buf(), Buf()]
        rcp = sb(es, "trcp", [64, 512], F32)
        brcp = Buf()
        sel = sb(es, "tsel", [128, 64], F32)
        bsel = Buf()
        K.dma(sel[:, :], c_sel[:, :], w=[bsel])
        m01 = sb(es, "tm01", [128, 128], BF16)
        bm01 = Buf()
        K.dma(m01[:, :], c_m01[maskidx, :, :], w=[bm01])
        z = [ps(es, f"tz{i}", [128, 512], F32) for i in range(3)]
        bz = [Buf() for _ in range(3)]
        obT = [ps(es, f"tob{i}", [128, 512], F32) for i in range(2)]
        bobT = [Buf(), Buf()]
        dn = ps(es, "tdn", [128, 512], F32)
        bdn = Buf()
        for i in range(2):
            K.memset('pool', v2[i][:, :, 64:128], 1.0, w=[bv[i]])
        if kr_rows:
            for i in range(2):
                K.dma(kt_sb[i][64:96, :], KR[:, :], w=[bkt[i]])
        kk = 96 if kr_rows else krows
        items = []
        for hi, h in enumerate(heads):
            for qb in range(NB):
                tiles = [(kt_, (kt_ - qb * 4) if kt_ >= qb * 4 else None) for kt_ in range(qb * 4 + 4)]
                for ti, (kt_, r_) in enumerate(tiles):
                    items.append((hi, h, qb, kt_, r_, ti, len(tiles)))

        def stage1(i):
            hi, h, qb, kt_, r_, ti, nt_ = items[i]
            hb = hi % 2
            if qb == 0 and ti == 0:
                K.dma(kt_sb[hb][0:krows, :], KT[h, 0:krows, :], w=[bkt[hb]])
                K.dma(qt_sb[hb][0:kk, :], QT[h, 0:kk, :], w=[bqt[hb]])
                K.dma(v2[hb][:, :, 0:65], VS[h, :, :, :], w=[bv[hb]])
            zi = i % 3
            c0 = 0 if r_ is None else 128 * r_
            K.mm(z[zi][:, c0:512], kt_sb[hb][0:kk, kt_ * 128:(kt_ + 1) * 128],
                 qt_sb[hb][0:kk, qb * 512 + c0:(qb + 1) * 512], True, True, r=[bkt[hb], bqt[hb]], w=[bz[zi]])
            K.act(pt[i % 4][:, c0:512], z[zi][:, c0:512], AF.Exp, r=[bz[zi]], w=[bpt[i % 4]], scale=scale)
            if r_ is not None:
                K.tt('dve', pt[i % 4][:, c0:c0 + 128], pt[i % 4][:, c0:c0 + 128], m01[:, :], ALU.mult,
                     r=[bpt[i % 4], bm01], w=[bpt[i % 4]])

        def stage2(i):
            hi, h, qb, kt_, r_, ti, nt_ = items[i]
            hb = hi % 2
            pi = i % 4
            qidx = hi * NB + qb
            oi = qidx % 2
            c0 = 0 if r_ is None else 128 * r_
            K.mm(obT[oi][:, c0:512], v2[hb][:, kt_, :], pt[pi][:, c0:512], ti == 0, ti == nt_ - 1,
                 r=[bpt[pi], bv[hb]], w=[bobT[oi]])
            if ti == nt_ - 1:
                K.copy('act', X[oi][:, :], obT[oi][:, :], r=[bobT[oi]], w=[bX[oi]])
                K.mm(dn[0:64, :], sel[:, :], X[oi][:, :], True, True, r=[bsel, bX[oi]], w=[bdn])
                K.cop('dve', lambda e, o_=rcp[:, :], i_=dn[0:64, :]: e.reciprocal(o_, i_), r=[bdn], w=[brcp])
                K.tt('dve', oT_sb[hb][:, qb * 512:(qb + 1) * 512], X[oi][0:64, :], rcp[:, :], ALU.mult,
                     r=[bX[oi], brcp], w=[bo[hb]])
                if qb == NB - 1:
                    K.dma(OTS[h, :, :], oT_sb[hb][:, :], r=[bo[hb]])

        n_it = len(items)
        for i in range(n_it + 2):
            if i < n_it:
                stage1(i)
            if i >= 2:
                stage2(i - 2)

    def attn_stick_heads(es, heads, scale):
        kt_sb = [sb(es, f"skt{i}", [64, S], BF16) for i in range(2)]
        qt_sb = [sb(es, f"sqt{i}", [64, S], BF16) for i in range(2)]
        v_sb = [sb(es, f"sv{i}", [128, NT, 65], BF16) for i in range(2)]
        o_sb = [sb(es, f"so{i}", [128, NT, 64], BF16) for i in range(2)]
        bkt = [Buf(), Buf()]
        bqt = [Buf(), Buf()]
        bv = [Buf(), Buf()]
        bo = [Buf(), Buf()]
        tri = sb(es, "tri", [128, 2, 128], BF16)
        btri = Buf()
        K.dma(tri[:, 0, :], c_tri[0, :, :], w=[btri])
        K.dma(tri[:, 1, :], c_tri[1, :, :], w=[btri])
        msk = sb(es, "smsk", [128, 4, 512], BF16)
        bmsk = Buf()
        for r_ in range(4):
            K.dma(msk[:, r_, :], c_maskD[0, r_, :, :], w=[bmsk])
        NCH = 2
        ee = [[sb(es, f"ee{c}_{i}", [128, 512], F32) for i in range(4)] for c in range(NCH)]
        bee = [[Buf() for _ in range(4)] for c in range(NCH)]
        sp_ = [[sb(es, f"sp{c}_{i}", [128, 512], BF16) for i in range(4)] for c in range(NCH)]
        bsp = [[Buf() for _ in range(4)] for c in range(NCH)]
        xx = [[sb(es, f"xx{c}_{i}", [128, 512], F32) for i in range(2)] for c in range(NCH)]
        bxx = [[Buf() for _ in range(2)] for c in range(NCH)]
        pt = [[sb(es, f"spt{c}_{i}", [128, 512], BF16) for i in range(4)] for c in range(NCH)]
        bpt = [[Buf() for _ in range(4)] for c in range(NCH)]
        z = [ps(es, f"sz{c}", [128, 512], F32) for c in range(NCH)]
        bz = [Buf() for _ in range(NCH)]
        acc = [ps(es, f"sacc{c}", [128, 512], F32) for c in range(NCH)]
        bacc = [Buf() for _ in range(NCH)]
        ob = [ps(es, f"sob{c}", [128, 512], F32) for c in range(NCH)]
        bob = [Buf() for _ in range(NCH)]
        zer = sb(es, "szer", [128, 128], BF16)
        bzer = Buf()
        K.memset('pool', zer[:, :], 0.0, w=[bzer])
        items = []
        rot = [0] * NCH
        for hi, h in enumerate(heads):
            qbs = list(range(NB))
            for g0 in range(0, NB, NCH):
                grp = qbs[g0:g0 + NCH][::-1]
                chains = []
                for c, qb in enumerate(grp):
                    ch = []
                    for ti, kt_ in enumerate(range(qb * 4 + 3, -1, -1)):
                        ch.append((hi, h, qb, kt_, ti, qb * 4 + 4, c, rot[c]))
                        rot[c] += 1
                    chains.append(ch)
                mx = max(len(ch) for ch in chains)
                for k in range(mx):
                    for ch in chains:
                        if k < len(ch):
                            items.append(ch[k])

        def prologue(i):
            hi, h, qb, kt_, ti, nt_, c, rt = items[i]
            hb = hi % 2
            if i == 0 or items[i - 1][0] != hi:
                K.dma(kt_sb[hb][:, :], KT[h, 0:64, :], w=[bkt[hb]])
                K.dma(qt_sb[hb][:, :], QT[h, 0:64, :], w=[bqt[hb]])
                K.dma(v_sb[hb][:, :, :], VS[h, :, :, :], w=[bv[hb]])

        def pe_z(i):
            hi, h, qb, kt_, ti, nt_, c, rt = items[i]
            hb = hi % 2
            diag = kt_ >= qb * 4
            K.mm(z[c][:, :], kt_sb[hb][:, kt_ * 128:(kt_ + 1) * 128], qt_sb[hb][:, qb * 512:(qb + 1) * 512],
                 True, not diag, r=[bkt[hb], bqt[hb]], w=[bz[c]])
            if diag:
                K.mm(z[c][:, :], ident[:], msk[:, kt_ - qb * 4, :], False, True, r=[b_ident, bmsk], w=[bz[c]])

        def act_e(i):
            hi, h, qb, kt_, ti, nt_, c, rt = items[i]
            ei = rt % 4
            K.act(ee[c][ei][:, :], z[c][:, :], AF.Exp, r=[bz[c]], w=[bee[c][ei]], scale=scale)

        def act_ln(i):
            hi, h, qb, kt_, ti, nt_, c, rt = items[i]
            ei = rt % 4
            K.act(sp_[c][ei][:, :], ee[c][ei][:, :], AF.Ln, r=[bee[c][ei]], w=[bsp[c][ei]], bias=1.0, scale=1.0)

        def pe_tri(i):
            hi, h, qb, kt_, ti, nt_, c, rt = items[i]
            ei = rt % 4
            K.mm(acc[c][:, :], tri[:, 0, :], sp_[c][ei][:, :], ti == 0, True, r=[btri, bsp[c][ei]], w=[bacc[c]], skip=True)

        def act_x(i):
            hi, h, qb, kt_, ti, nt_, c, rt = items[i]
            xi = rt % 2
            K.act(xx[c][xi][:, :], acc[c][:, :], AF.Exp, r=[bacc[c]], w=[bxx[c][xi]], scale=-1.0)

        def pe_excl(i):
            hi, h, qb, kt_, ti, nt_, c, rt = items[i]
            ei = rt % 4
            if ti < nt_ - 1:
                K.mm(acc[c][:, :], tri[:, 1, :], sp_[c][ei][:, :], False, True, r=[btri, bsp[c][ei]], w=[bacc[c]], skip=True)

        def dve_pt(i):
            hi, h, qb, kt_, ti, nt_, c, rt = items[i]
            ei = rt % 4
            xi = rt % 2
            K.tt('dve', pt[c][ei][:, :], ee[c][ei][:, :], xx[c][xi][:, :], ALU.mult, r=[bee[c][ei], bxx[c][xi]],
                 w=[bpt[c][ei]])

        def pe_pv(i):
            hi, h, qb, kt_, ti, nt_, c, rt = items[i]
            hb = hi % 2
            ei = rt % 4
            if ti == 0:
                K.mm(ob[c][:, 0:256], zer[:, :], msk[:, 0, 0:256], True, False, r=[bzer, bmsk], w=[bob[c]], skip=True)
            for qs in range(4):
                K.mm(ob[c][:, qs * 64:(qs + 1) * 64], pt[c][ei][:, qs * 128:(qs + 1) * 128], v_sb[hb][:, kt_, 0:64],
                     False, ti == nt_ - 1, r=[bpt[c][ei], bv[hb]], w=[bob[c]], skip=True)
            if ti == nt_ - 1:
                K.copy('dve', o_sb[hb][:, qb * 4:qb * 4 + 4, :], ob[c][:, 0:256].rearrange("p (q d) -> p q d", d=64),
                       r=[bob[c]], w=[bo[hb]])
                if qb == NB - 1:
                    K.dma(OS[:, :, h * 64:(h + 1) * 64].rearrange("t p d -> p t d"), o_sb[hb][:, :, :], r=[bo[hb]])

        n_it = len(items)
        for i in range(n_it + 5):
            ok = lambda j: 0 <= j < n_it
            if ok(i):
                prologue(i)
            if ok(i - 3):
                pe_tri(i - 3)
                act_x(i - 3)
            if ok(i):
                pe_z(i)
            if ok(i - 1):
                act_ln(i - 1)
            if ok(i - 5):
                pe_pv(i - 5)
            if ok(i):
                act_e(i)
            if ok(i - 3):
                pe_excl(i - 3)
                dve_pt(i - 3)

    def phase_attn_ab():
        with contextlib.ExitStack() as es:
            attn_softmax_heads(es, list(range(8)), 64, 0.125, 'A', bias_src=relE)
            K.barrier()
        with contextlib.ExitStack() as es:
            attn_stick_heads(es, list(range(8, 16)), 0.125)
            K.barrier()

    def phase_outproj(xsrc, xdst, w_o, tsrc=False):
        with contextlib.ExitStack() as es:
            W = sb(es, "W_o", [128, 8, D], BF16)
            bW = Buf()
            load_weight_folded(es, W, bW, w_o, D, None, None, 8, D, "wo")
            xt = [sb(es, f"oxt{i}", [128, 4, D], F32) for i in range(2)]
            bxt = [[Buf() for _ in range(4)] for _ in range(2)]
            ot = [sb(es, f"oblk{i}", [128, 4, D], BF16) for i in range(2)]
            bot = [Buf(), Buf()]
            oT = [sb(es, f"oT{i}", [128, 8, 512], BF16) for i in range(2)]
            boT = [Buf(), Buf()]
            tp = [ps(es, f"otp{i}", [128, 512], BF16) for i in range(2)]
            btp = [Buf(), Buf()]
            pm = [ps(es, f"opm{i}", [128, 512], F32) for i in range(4)]
            bpm = [Buf() for _ in range(4)]
            ipm = 0
            for b in range(NB):
                j2 = b % 2
                src = xsrc[b * 512:(b + 1) * 512, :].rearrange("(t p) c -> p t c", p=128)
                dst = xdst[b * 512:(b + 1) * 512, :].rearrange("(t p) c -> p t c", p=128)
                for t in range(4):
                    K.dma(xt[j2][:, t, :], src[:, t, :], w=[bxt[j2][t]])
                if tsrc:
                    for fc in range(8):
                        for hh in range(2):
                            K.dma(oT[j2][hh * 64:(hh + 1) * 64, fc, :], OTS[2 * fc + hh, :, b * 512:(b + 1) * 512],
                                  w=[boT[j2]])
                else:
                    K.dma(ot[j2][:, :, :], OS[b * 4:(b + 1) * 4, :, :].rearrange("t p c -> p t c"), w=[bot[j2]])
                    for fc in range(8):
                        j = fc % 2
                        for t in range(4):
                            K.tr(tp[j][:, t * 128:(t + 1) * 128], ot[j2][:, t, fc * 128:(fc + 1) * 128], ident[:],
                                 r=[bot[j2], b_ident], w=[btp[j]])
                        K.copy('act' if fc % 2 == 0 else 'dve', oT[j2][:, fc, :], tp[j][:, :], r=[btp[j]], w=[boT[j2]])
                for t in range(4):
                    for half in range(2):
                        bk = ipm % 4
                        ipm += 1
                        for fc in range(8):
                            K.mm(pm[bk][:, :], oT[j2][:, fc, t * 128:(t + 1) * 128], W[:, fc, half * 512:(half + 1) * 512],
                                 fc == 0, fc == 7, r=[bW, boT[j2]], w=[bpm[bk]])
                        K.tt('dve', xt[j2][:, t, half * 512:(half + 1) * 512], pm[bk][:, :],
                             xt[j2][:, t, half * 512:(half + 1) * 512], ALU.add, r=[bpm[bk], bxt[j2][t]], w=[bxt[j2][t]])
                    K.dma(dst[:, t, :], xt[j2][:, t, :], r=[bxt[j2][t]])
            K.barrier()

    def phase_ffn(xsrc, xdst, layer, final):
        with contextlib.ExitStack() as es:
            Wg = sb(es, "Wg", [128, 8, DFF], BF16)
            Wu = sb(es, "Wu", [128, 8, DFF], BF16)
            Wd = sb(es, "Wd", [128, NFF, D], BF16)
            bWg, bWu, bWd = Buf(), Buf(), Buf()
            g_sb = sb(es, "g_ffn", [128, 8], F32)
            b_g = Buf()
            K.dma(g_sb[:], ffn_norm[layer, :, :], w=[b_g])
            cw = sb(es, "convw", [128, NFF, 4], F32)
            bcw = Buf()
            K.dma(cw[:], ffn_conv[layer, :, :, :], w=[bcw])
            with contextlib.ExitStack() as es2:
                load_weight_folded(es2, Wg, bWg, ffn_w_gate[layer], DFF, g_sb, b_g, 8, 1408, "wg")
                load_weight_folded(es2, Wu, bWu, ffn_w_up[layer], DFF, g_sb, b_g, 8, 1408, "wu")
                load_weight_folded(es2, Wd, bWd, ffn_w_down[layer], D, None, None, NFF, 1024, "wd")
                K.barrier()
            nx = 1 if final else 2
            xts = [sb(es, f"fxt{i}", [128, 4, D], F32) for i in range(nx)]
            bxts = [[Buf() for _ in range(4)] for _ in range(nx)]
            ss = sb(es, "fss", [128, 4], F32)
            lnv = sb(es, "flnv", [128, 4], F32)
            rstd = sb(es, "frstd", [128, 4], F32)
            bss = Buf()
            xn = [sb(es, f"fxn{i}", [128, D], BF16) for i in range(2)]
            bxn = [Buf(), Buf()]
            junk = xn[0]
            bjunk = bxn[0]
            hT = sb(es, "fhT", [128, 8, 512], BF16)
            bhT = [Buf() for _ in range(8)]
            yT = sb(es, "fyT", [128, NFF, 512], BF16)
            byT = [Buf() for _ in range(NFF)]
            gsb = [sb(es, f"fg{i}", [128, 514], F32) for i in range(2)]
            bgsb = [Buf(), Buf()]
            t1 = [sb(es, f"ft1{i}", [128, 512], F32) for i in range(2)]
            bt1 = [Buf(), Buf()]
            carry = sb(es, "fcarry", [128, NFF, 2], F32)
            bcarry = [Buf() for _ in range(NFF)]
            K.memset('pool', carry[:, :, :], 0.0, w=bcarry)
            if final:
                gfin = sb(es, "gfin", [128, D], F32)
                bgfin = Buf()
                K.dma(gfin[:, :], final_norm[0:1, :].broadcast_to([128, D]), w=[bgfin])
                fss = sb(es, "ffss", [128, 4], F32)
                flnv = sb(es, "fflnv", [128, 4], F32)
                frstd = sb(es, "ffrstd", [128, 4], F32)
                bfss = Buf()
            tp = [ps(es, f"ftp{i}", [128, 512], BF16) for i in range(2)]
            btp8 = [Buf() for _ in range(8)]
            pg = [ps(es, f"fpg{i}", [128, 512], F32) for i in range(2)]
            bpg = [Buf(), Buf()]
            pu = [ps(es, f"fpu{i}", [128, 512], F32) for i in range(2)]
            bpu = [Buf(), Buf()]
            pd = [ps(es, f"fpd{i}", [128, 512], F32) for i in range(2)]
            bpd = [Buf(), Buf()]
            ipd = 0
            for b in range(NB):
                src = xsrc[b * 512:(b + 1) * 512, :].rearrange("(t p) c -> p t c", p=128)
                dst = xdst[b * 512:(b + 1) * 512, :].rearrange("(t p) c -> p t c", p=128)
                xt = xts[b % nx]
                bxt = bxts[b % nx]
                for t in range(4):
                    K.dma(xt[:, t, :], src[:, t, :], w=[bxt[t]])
                norm_block(xt, bxt, 4, ss, lnv, rstd, junk, bjunk, bss)
                for t in range(4):
                    j = t % 2
                    K.ts('dve' if t % 2 == 0 else 'pool', xn[j][:, :], xt[:, t, :], rstd[:, t:t + 1], None, ALU.mult,
                         r=[bxt[t], bss], w=[bxn[j]])
                    for k4 in range(2):
                        for kq in range(4):
                            kc = k4 * 4 + kq
                            K.tr(tp[k4][:, kq * 128:(kq + 1) * 128], xn[j][:, kc * 128:(kc + 1) * 128], ident[:],
                                 r=[bxn[j], b_ident], w=[btp8[k4]])
                        K.copy('act' if k4 == 0 else 'dve', hT[:, k4 * 4:(k4 + 1) * 4, t * 128:(t + 1) * 128],
                               tp[k4][:, :].rearrange("p (k q) -> p k q", q=128), r=[btp8[k4]],
                               w=[bhT[k4 * 4 + i] for i in range(4)])
                for fc in range(NFF):
                    j = fc % 2
                    for kc in range(8):
                        K.mm(pg[j][:, :], Wg[:, kc, fc * 128:(fc + 1) * 128], hT[:, kc, :], kc == 0, kc == 7,
                             r=[bWg, bhT[kc]], w=[bpg[j]])
                    for kc in range(8):
                        K.mm(pu[j][:, :], Wu[:, kc, fc * 128:(fc + 1) * 128], hT[:, kc, :], kc == 0, kc == 7,
                             r=[bWu, bhT[kc]], w=[bpu[j]])
                    K.copy('pool', gsb[j][:, 0:2], carry[:, fc, :], r=[bcarry[fc]], w=[bgsb[j]])
                    K.copy('act', gsb[j][:, 2:514], pg[j][:, :], r=[bpg[j]], w=[bgsb[j]])
                    K.copy('pool', carry[:, fc, :], gsb[j][:, 512:514], r=[bgsb[j]], w=[bcarry[fc]])
                    K.ts('dve', t1[j][:, :], gsb[j][:, 2:514], cw[:, fc, 2:3], cw[:, fc, 3:4], ALU.mult, ALU.add,
                         r=[bgsb[j], bcw], w=[bt1[j]])
                    K.stt('dve', t1[j][:, :], gsb[j][:, 1:513], cw[:, fc, 1:2], t1[j][:, :], ALU.mult, ALU.add,
                          r=[bgsb[j], bcw, bt1[j]], w=[bt1[j]])
                    K.stt('dve', t1[j][:, :], gsb[j][:, 0:512], cw[:, fc, 0:1], t1[j][:, :], ALU.mult, ALU.add,
                          r=[bgsb[j], bcw, bt1[j]], w=[bt1[j]])
                    K.act(t1[j][:, :], t1[j][:, :], AF.Silu, r=[bt1[j]], w=[bt1[j]])
                    K.tt('dve', yT[:, fc, :], pu[j][:, :], t1[j][:, :], ALU.mult, r=[bpu[j], bt1[j]], w=[byT[fc]])
                for t in range(4):
                    for half in range(2):
                        bk = ipd % 2
                        ipd += 1
                        for fc in range(NFF):
                            K.mm(pd[bk][:, :], yT[:, fc, t * 128:(t + 1) * 128], Wd[:, fc, half * 512:(half + 1) * 512],
                                 fc == 0, fc == NFF - 1, r=[bWd, byT[fc]], w=[bpd[bk]])
                        K.tt('dve', xt[:, t, half * 512:(half + 1) * 512], pd[bk][:, :],
                             xt[:, t, half * 512:(half + 1) * 512], ALU.add, r=[bpd[bk], bxt[t]], w=[bxt[t]])
                    if final:
                        K.act(junk[:, :], xt[:, t, :], AF.Square, r=[bxt[t]], w=[bjunk, bfss], accum_out=fss[:, t:t + 1])
                        K.act(flnv[:, t:t + 1], fss[:, t:t + 1], AF.Ln, r=[bfss], w=[bfss], bias=EPS, scale=1.0 / D)
                        K.act(frstd[:, t:t + 1], flnv[:, t:t + 1], AF.Exp, r=[bfss], w=[bfss], scale=-0.5)
                        K.stt('dve', xt[:, t, :], xt[:, t, :], frstd[:, t:t + 1], gfin[:, :], ALU.mult, ALU.mult,
                              r=[bxt[t], bfss, bgfin], w=[bxt[t]])
                    K.dma(dst[:, t, :], xt[:, t, :], r=[bxt[t]])
            K.barrier()

    CUT = int(os.environ.get('DBGCUT', '99'))

    def phase_inproj_cd(xsrc):
        with contextlib.ExitStack() as es:
            NCOL = 2216
            W = sb(es, "W_cd", [128, 8, NCOL], BF16)
            bW = Buf()
            g_sb = sb(es, "g_cd", [128, 8], F32)
            gq_sb = sb(es, "gq_cd", [128, 3], F32)
            gkv_sb = sb(es, "gkv_cd", [128, 2], F32)
            b_g = Buf()
            K.dma(g_sb[:], cd_norm[:, :], w=[b_g])
            K.dma(gq_sb[:], cd_q_norm[:, :], w=[b_g])
            K.dma(gkv_sb[:], cd_kv_norm[:, :], w=[b_g])
            nbf = sb(es, "nbf", [8, 1], F32)
            bnbf = Buf()
            K.dma(nbf[:], cd_b_f[:, :], w=[bnbf])
            K.ts('dve', nbf[:], nbf[:], -1.0, None, ALU.mult, r=[bnbf], w=[bnbf])
            Wuq = sb(es, "Wuq", [128, 3, 768], BF16)
            Wuqr = sb(es, "Wuqr", [128, 3, 768], BF16)
            Wukv = sb(es, "Wukv", [128, 2, 1024], BF16)
            Wr96 = sb(es, "Wr96", [128, 8, 96], BF16)
            bWuq, bWuqr, bWukv, bWr96 = Buf(), Buf(), Buf(), Buf()
            with contextlib.ExitStack() as es2:
                load_weight_folded(es2, W, bW, cd_w_in, NCOL, g_sb, b_g, 8, 1108, "wcd")
                load_weight_folded(es2, Wuq, bWuq, cd_w_uq, 768, gq_sb, b_g, 3, 768, "wuq")
                load_weight_folded(es2, Wukv, bWukv, cd_w_ukv, 1024, gkv_sb, b_g, 2, 1024, "wukv")
                K.barrier()
            K.copy('dve', Wuqr[:, :, :], Wuq[:, :, :], r=[bWuq], w=[bWuqr])
            v4 = Wuq[:, :, :].rearrange("p c (h k) -> p c h k", k=96)
            v4r = Wuqr[:, :, :].rearrange("p c (h k) -> p c h k", k=96)
            for c in range(3):
                K.ts('dve', v4r[:, c, :, 64:80], v4[:, c, :, 80:96], -1.0, None, ALU.mult, r=[bWuq], w=[bWuqr])
                K.copy('dve', v4r[:, c, :, 80:96], v4[:, c, :, 64:80], r=[bWuq], w=[bWuqr])
            K.memset('pool', Wr96[:, :, :], 0.0, w=[bWr96])
            K.ts('dve', Wr96[:, :, 64:80], W[:, :, 656:672], -1.0, None, ALU.mult, r=[bW], w=[bWr96])
            K.copy('dve', Wr96[:, :, 80:96], W[:, :, 640:656], r=[bW], w=[bWr96])

            xt = [sb(es, f"cxt{i}", [128, 4, D], F32) for i in range(2)]
            bxt = [[Buf() for _ in range(4)] for _ in range(2)]
            junk = sb(es, "cjunk", [128, D], BF16)
            bjunk = Buf()
            ss = sb(es, "css", [128, 4], F32)
            lnv = sb(es, "clnv", [128, 4], F32)
            rstd = sb(es, "crstd", [128, 4], F32)
            bss = Buf()
            xn = sb(es, "cxn", [128, 4, D], BF16)
            bxn = [Buf() for _ in range(4)]
            hT = [sb(es, f"chT{i}", [128, 8, 512], BF16) for i in range(2)]
            bhT = [Buf(), Buf()]
            cqf = sb(es, "cqf", [128, 4, 640], F32)
            bcqf = [Buf() for _ in range(4)]
            ss2 = sb(es, "css2", [128, 8], F32)
            lnv2 = sb(es, "clnv2", [128, 8], F32)
            rstd2 = sb(es, "crstd2", [128, 8], F32)
            bss2 = Buf()
            cn = sb(es, "ccn", [128, 4, 640], BF16)
            bcn = [Buf() for _ in range(4)]
            cT = sb(es, "ccT", [128, 5, 512], BF16)
            bcT = Buf()
            qst = [sb(es, f"cqst{i}", [128, 512], BF16) for i in range(4)]
            bqst = [Buf() for _ in range(4)]
            tmpa = [sb(es, f"ctmpa{i}", [96, 512], F32) for i in range(2)]
            tmpb = [sb(es, f"ctmpb{i}", [96, 512], F32) for i in range(2)]
            btmp = [Buf(), Buf()]
            rope_sb = [sb(es, f"crope{i}", [96, 2, 512], F32) for i in range(2)]
            brope = [Buf(), Buf()]
            vst = [sb(es, f"cvst{i}", [128, 16, 4, 65], BF16) for i in range(2)]
            bvst = [Buf(), Buf()]
            kaug = sb(es, "kaug", [8, 6, 512], BF16)
            qaug = sb(es, "qaug", [8, 6, 512], BF16)
            bkaug, bqaug = Buf(), Buf()
            fe = sb(es, "fe", [8, 512], F32)
            fsp = sb(es, "fsp", [8, 512], F32)
            fones = sb(es, "fones", [8, 512], F32)
            fG = [sb(es, f"fG{i}", [8, 512], F32) for i in range(2)]
            fr1 = sb(es, "fr1", [8, 512], F32)
            fr2 = sb(es, "fr2", [8, 512], F32)
            bfe, bfsp, bfones, bfr = Buf(), Buf(), Buf(), Buf()
            bfG = [Buf(), Buf()]
            K.memset('pool', fones[:, :], 1.0, w=[bfones])
            K.memset('pool', kaug[:, 3:6, :], 1.0, w=[bkaug])
            K.memset('pool', qaug[:, 0:3, :], 1.0, w=[bqaug])
            K.memset('pool', fG[1][:, :], 0.0, w=[bfG[1]])
            tp = [ps(es, f"ctp{i}", [128, 512], BF16) for i in range(2)]
            btp = [Buf(), Buf()]
            pm = [ps(es, f"cpm{i}", [128, 512], F32) for i in range(5)]
            bpm = [Buf() for _ in range(5)]
            for i in range(2):
                K.memset('pool', vst[i][:, :, :, 64:65], 1.0, w=[bvst[i]])
            ipm = 0
            iq = 0
            itm = 0

            def nxt():
                nonlocal ipm
                bk = ipm % 5
                ipm += 1
                return bk

            def rope_rows(pa, pb, dst_ap, rb, deps_r, deps_w):
                nonlocal itm
                j = itm % 2
                itm += 1
                K.tt('dve', tmpa[j][64:96, :], pm[pa][64:96, :], rope_sb[rb][64:96, 0, :], ALU.mult,
                     r=[bpm[pa], brope[rb]], w=[btmp[j]])
                K.tt('dve', tmpb[j][64:96, :], pm[pb][64:96, :], rope_sb[rb][64:96, 1, :], ALU.mult,
                     r=[bpm[pb], brope[rb]], w=[btmp[j]])
                K.tt('pool', dst_ap, tmpa[j][64:96, :], tmpb[j][64:96, :], ALU.add, r=[btmp[j]] + deps_r, w=deps_w)

            for b in range(NB):
                xb_ = b % 2
                hb = b % 2
                src = xsrc[b * 512:(b + 1) * 512, :].rearrange("(t p) c -> p t c", p=128)
                for t in range(4):
                    K.dma(xt[xb_][:, t, :], src[:, t, :], w=[bxt[xb_][t]])
                for k2 in range(2):
                    K.dma(rope_sb[xb_][64:96, k2, :], c_rope[k2, :, b * 512:(b + 1) * 512], w=[brope[xb_]])
                norm_block(xt[xb_], bxt[xb_], 4, ss, lnv, rstd, junk, bjunk, bss)
                for t in range(4):
                    K.ts('dve' if t % 2 == 0 else 'pool', xn[:, t, :], xt[xb_][:, t, :], rstd[:, t:t + 1], None,
                         ALU.mult, r=[bxt[xb_][t], bss], w=[bxn[t]])
                for kc in range(8):
                    j = kc % 2
                    for t in range(4):
                        K.tr(tp[j][:, t * 128:(t + 1) * 128], xn[:, t, kc * 128:(kc + 1) * 128], ident[:],
                             r=[bxn[t], b_ident], w=[btp[j]])
                    K.copy('act' if kc % 2 == 0 else 'dve', hT[hb][:, kc, :], tp[j][:, :], r=[btp[j]], w=[bhT[hb]])
                if CUT < 1:
                    continue
                for t in range(4):
                    for (c0, c1, col) in ((0, 384, t), (384, 640, 4 + t)):
                        bk = nxt()
                        for kc in range(8):
                            K.mm(pm[bk][:, 0:c1 - c0], hT[hb][:, kc, t * 128:(t + 1) * 128], W[:, kc, c0:c1],
                                 kc == 0, kc == 7, r=[bW, bhT[hb]], w=[bpm[bk]])
                        K.copy('dve', cqf[:, t, c0:c1], pm[bk][:, 0:c1 - c0], r=[bpm[bk]], w=[bcqf[t]])
                        K.act(junk[:, 0:c1 - c0], cqf[:, t, c0:c1], AF.Square, r=[bcqf[t]], w=[bjunk, bss2],
                              accum_out=ss2[:, col:col + 1])
                K.act(lnv2[:, 0:4], ss2[:, 0:4], AF.Ln, r=[bss2], w=[bss2], bias=EPS, scale=1.0 / 384)
                K.act(lnv2[:, 4:8], ss2[:, 4:8], AF.Ln, r=[bss2], w=[bss2], bias=EPS, scale=1.0 / 256)
                K.act(rstd2[:, :], lnv2[:, :], AF.Exp, r=[bss2], w=[bss2], scale=-0.5)
                for t in range(4):
                    K.ts('dve', cn[:, t, 0:384], cqf[:, t, 0:384], rstd2[:, t:t + 1], None, ALU.mult,
                         r=[bcqf[t], bss2], w=[bcn[t]])
                    K.ts('pool', cn[:, t, 384:640], cqf[:, t, 384:640], rstd2[:, 4 + t:5 + t], None, ALU.mult,
                         r=[bcqf[t], bss2], w=[bcn[t]])
                for c in range(5):
                    j = c % 2
                    for t in range(4):
                        K.tr(tp[j][:, t * 128:(t + 1) * 128], cn[:, t, c * 128:(c + 1) * 128], ident[:],
                             r=[bcn[t], b_ident], w=[btp[j]])
                    K.copy('act' if c % 2 == 0 else 'dve', cT[:, c, :], tp[j][:, :], r=[btp[j]], w=[bcT])
                if CUT < 2:
                    continue
                for h in range(8):
                    pa = nxt()
                    for c in range(3):
                        K.mm(pm[pa][0:96, :], Wuq[:, c, h * 96:(h + 1) * 96], cT[:, c, :], c == 0, c == 2,
                             r=[bWuq, bcT], w=[bpm[pa]])
                    pb = nxt()
                    for c in range(3):
                        K.mm(pm[pb][0:96, :], Wuqr[:, c, h * 96:(h + 1) * 96], cT[:, c, :], c == 0, c == 2,
                             r=[bWuqr, bcT], w=[bpm[pb]])
                    qi = iq % 4
                    iq += 1
                    K.copy('dve', qst[qi][0:64, :], pm[pa][0:64, :], r=[bpm[pa]], w=[bqst[qi]])
                    rope_rows(pa, pb, qst[qi][64:96, :], xb_, [], [bqst[qi]])
                    K.dma(QT[h, 0:96, b * 512:(b + 1) * 512], qst[qi][0:96, :], r=[bqst[qi]])
                if CUT < 3:
                    continue
                for h in range(8):
                    pa = nxt()
                    for c in range(2):
                        K.mm(pm[pa][0:64, :], Wukv[:, c, h * 128:h * 128 + 64], cT[:, 3 + c, :], c == 0, c == 1,
                             r=[bWukv, bcT], w=[bpm[pa]])
                    qi = iq % 4
                    iq += 1
                    K.copy('act' if h % 2 == 0 else 'dve', qst[qi][0:64, :], pm[pa][0:64, :], r=[bpm[pa]], w=[bqst[qi]])
                    K.dma(KT[h, 0:64, b * 512:(b + 1) * 512], qst[qi][0:64, :], r=[bqst[qi]])
                pa = nxt()
                for kc in range(8):
                    K.mm(pm[pa][0:96, :], W[:, kc, 576:672], hT[hb][:, kc, :], kc == 0, kc == 7, r=[bW, bhT[hb]], w=[bpm[pa]])
                pb = nxt()
                for kc in range(8):
                    K.mm(pm[pb][0:96, :], Wr96[:, kc, :], hT[hb][:, kc, :], kc == 0, kc == 7, r=[bWr96, bhT[hb]], w=[bpm[pb]])
                qi = iq % 4
                iq += 1
                rope_rows(pa, pb, qst[qi][64:96, :], xb_, [], [bqst[qi]])
                K.dma(KR[:, b * 512:(b + 1) * 512], qst[qi][64:96, :], r=[bqst[qi]])
                if CUT < 4:
                    continue
                for (c0, dst, h0) in ((672, QT, 8), (1184, KT, 8)):
                    for hp in range(4):
                        bk = nxt()
                        for kc in range(8):
                            K.mm(pm[bk][:, :], W[:, kc, c0 + hp * 128:c0 + (hp + 1) * 128], hT[hb][:, kc, :],
                                 kc == 0, kc == 7, r=[bW, bhT[hb]], w=[bpm[bk]])
                        qi = iq % 4
                        iq += 1
                        K.copy('act' if qi % 2 == 0 else 'dve', qst[qi][:, :], pm[bk][:, :], r=[bpm[bk]], w=[bqst[qi]])
                        for hh in range(2):
                            K.dma(dst[h0 + hp * 2 + hh, 0:64, b * 512:(b + 1) * 512], qst[qi][hh * 64:(hh + 1) * 64, :],
                                  r=[bqst[qi]])
                if CUT < 5:
                    continue
                vb_ = b % 2
                for t in range(4):
                    bk = nxt()
                    for c in range(2):
                        K.mm(pm[bk][:, :].rearrange("p (h d) -> p h d", d=64), cT[:, 3 + c, t * 128:(t + 1) * 128],
                             Wukv[:, c, :].rearrange("p (h k) -> p h k", k=128)[:, :, 64:128],
                             c == 0, c == 1, r=[bWukv, bcT], w=[bpm[bk]])
                    K.copy('act' if t % 2 == 0 else 'dve', vst[vb_][:, 0:8, t, 0:64],
                           pm[bk][:, :].rearrange("p (h d) -> p h d", d=64), r=[bpm[bk]], w=[bvst[vb_]])
                for t in range(4):
                    bk = nxt()
                    for kc in range(8):
                        K.mm(pm[bk][:, :], hT[hb][:, kc, t * 128:(t + 1) * 128], W[:, kc, 1696:2208],
                             kc == 0, kc == 7, r=[bW, bhT[hb]], w=[bpm[bk]])
                    K.copy('act' if t % 2 == 0 else 'dve', vst[vb_][:, 8:16, t, 0:64],
                           pm[bk][:, :].rearrange("p (h d) -> p h d", d=64), r=[bpm[bk]], w=[bvst[vb_]])
                K.dma(VS[:, :, b * 4:(b + 1) * 4, :].rearrange("h p t c -> p h t c"), vst[vb_][:, :, :, :], r=[bvst[vb_]])
                if CUT < 6:
                    continue
                bk = nxt()
                for kc in range(8):
                    K.mm(pm[bk][0:8, :], W[:, kc, 2208:2216], hT[hb][:, kc, :], kc == 0, kc == 7, r=[bW, bhT[hb]], w=[bpm[bk]])
                K.act(fe[:, :], pm[bk][0:8, :], AF.Exp, r=[bpm[bk], bnbf], w=[bfe], bias=nbf[:, 0:1], scale=-1.0)
                K.act(fsp[:, :], fe[:, :], AF.Ln, r=[bfe], w=[bfsp], bias=1.0, scale=1.0)
                gi = b % 2
                gp = (b + 1) % 2
                K.cop('dve', lambda e, o_=fG[gi][:, :], d0=fones[:, :], d1=fsp[:, :], ini=fG[gp][:, 511:512]:
                      e.tensor_tensor_scan(o_, d0, d1, ini, ALU.mult, ALU.add),
                      r=[bfones, bfsp, bfG[gp]], w=[bfG[gi]])
                K.ts('dve', fr1[:, :], fG[gi][:, :], 8.0, None, ALU.mult, r=[bfG[gi]], w=[bfr])
                K.copy('dve', kaug[:, 0, :], fr1[:, :], r=[bfr], w=[bkaug])
                K.tt('dve', fr2[:, :], fr1[:, :], kaug[:, 0, :], ALU.subtract, r=[bfr, bkaug], w=[bfr])
                K.copy('dve', kaug[:, 1, :], fr2[:, :], r=[bfr], w=[bkaug])
                K.tt('dve', fr1[:, :], fr2[:, :], kaug[:, 1, :], ALU.subtract, r=[bfr, bkaug], w=[bfr])
                K.copy('dve', kaug[:, 2, :], fr1[:, :], r=[bfr], w=[bkaug])
                K.ts('dve', qaug[:, 3:6, :], kaug[:, 0:3, :], -1.0, None, ALU.mult, r=[bkaug], w=[bqaug])
                K.dma(KT[8:16, 64:70, b * 512:(b + 1) * 512], kaug[:, :, :], r=[bkaug])
                K.dma(QT[8:16, 64:70, b * 512:(b + 1) * 512], qaug[:, :, :], r=[bqaug])
            K.barrier()

    def phase_attn_cd():
        with contextlib.ExitStack() as es:
            attn_softmax_T(es, list(range(8)), 64, 96.0 ** -0.5, 1, True)
            K.barrier()
        with contextlib.ExitStack() as es:
            attn_softmax_T(es, list(range(8, 16)), 70, 0.125, 0, False)
            K.barrier()

    phase_inproj_ab(x_in)
    if stop_after != 'inproj0':
        phase_attn_ab()
        phase_outproj(x_in, xa, ab_w_o)
        phase_ffn(xa, out if stop_after == 'l0' else xb, 0, False)
    if stop_after is None or stop_after.startswith('dbg'):
        dbg = stop_after or 'dbg:proj,attnC,attnD,out,ffn'
        if stop_after is None or 'proj' in dbg:
            phase_inproj_cd(xb)
        if stop_after is None or 'attnC' in dbg:
            with contextlib.ExitStack() as es:
                attn_softmax_T(es, list(range(8)), 64, 96.0 ** -0.5, 1, True)
                K.barrier()
        if stop_after is None or 'attnD' in dbg:
            with contextlib.ExitStack() as es:
                attn_softmax_T(es, list(range(8, 16)), 70, 0.125, 0, False)
                K.barrier()
        if stop_after is None or 'out' in dbg:
            phase_outproj(xb, xa, cd_w_o, tsrc=True)
        if stop_after is None or 'ffn' in dbg:
            phase_ffn(xa, out, 1, True)
    K.emit(es_top)
    es_top.close()
    return nc


def host_consts(S):
    c = {}
    c["c_ident"] = np.eye(128, dtype=np.float32).astype(ml_dtypes.bfloat16)
    j = np.arange(128)[:, None]
    s = np.arange(128)[None, :]
    tri = np.stack([(j >= s), (j < s)]).astype(np.float32)
    c["c_tri"] = tri.astype(ml_dtypes.bfloat16)
    sk = np.arange(128)[:, None]
    tq = np.arange(512)[None, :]
    mA = np.zeros((8, 128, 512), np.float32)
    for r in range(8):
        kc_ = (128 * r + sk) // 64
        qc_ = tq // 64 + 8
        valid = (kc_ <= qc_) & (kc_ >= qc_ - 8)
        mA[r] = np.where(valid, 0.0, NEG * 8)
    c["c_maskA"] = mA
    mD = np.zeros((3, 4, 128, 512), np.float32)
    for r in range(4):
        ka = 128 * r + sk
        mD[0, r] = np.where(ka < tq, 0.0, NEG)
        mD[1, r] = np.where(ka <= tq, 0.0, NEG)
        mD[2, r] = np.where(ka // 64 <= tq // 64, 0.0, NEG)
    c["c_maskD"] = mD.astype(ml_dtypes.bfloat16)
    sel = np.zeros((128, 64), np.float32)
    sel[64 + np.arange(64), np.arange(64)] = 1.0
    c["c_sel"] = sel
    t1_ = np.arange(128)[None, :]
    c["c_m01"] = np.stack([(sk <= t1_), (sk // 64 <= t1_ // 64)]).astype(np.float32).astype(ml_dtypes.bfloat16)
    half = 16
    inv = 10000.0 ** (-np.arange(half, dtype=np.float32) / half)
    ang = np.arange(S, dtype=np.float32)[None, :] * inv[:, None]
    cos = np.cos(ang).astype(np.float32)
    sin = np.sin(ang).astype(np.float32)
    c["c_rope"] = np.stack([np.concatenate([cos, cos], 0), np.concatenate([sin, sin], 0)]).astype(np.float32)
    return c


def host_layout(inp, S):
    m = {}
    f = lambda a: np.ascontiguousarray(a, dtype=np.float32)
    m["ab_w_in"] = f(inp["ab_w_in"][0])
    m["ab_w_o"] = f(inp["ab_w_o"][0])
    m["ab_norm"] = f(inp["ab_norm"][0].reshape(8, 128).T)
    rb = np.asarray(inp["ab_rel_bias"][0], np.float32)
    sk = np.arange(128)[:, None]
    jj = np.arange(1408)[None, :] - 384
    idx = np.clip(jj - sk, -128, 128) + 128
    m["relE"] = f(rb[:, idx])
    m["cd_w_in"] = f(inp["cd_w_in"][0])
    m["cd_w_o"] = f(inp["cd_w_o"][0])
    m["cd_norm"] = f(inp["cd_norm"][0].reshape(8, 128).T)
    m["cd_q_norm"] = f(inp["cd_q_norm"][0].reshape(3, 128).T)
    m["cd_kv_norm"] = f(inp["cd_kv_norm"][0].reshape(2, 128).T)
    m["cd_w_uq"] = f(inp["cd_w_uq"][0])
    m["cd_w_ukv"] = f(inp["cd_w_ukv"][0])
    m["cd_b_f"] = f(inp["cd_b_f"][0].reshape(8, 1))
    m["ffn_norm"] = f(np.stack([inp["ffn_norm"][l].reshape(8, 128).T for l in range(2)]))
    m["ffn_w_gate"] = f(inp["ffn_w_gate"])
    m["ffn_w_up"] = f(inp["ffn_w_up"])
    m["ffn_w_down"] = f(inp["ffn_w_down"])
    cw = np.zeros((2, 128, NFF, 4), np.float32)
    for l in range(2):
        for k in range(3):
            cw[l, :, :, k] = np.asarray(inp["ffn_conv_w"][l, k]).reshape(NFF, 128).T
        cw[l, :, :, 3] = np.asarray(inp["ffn_conv_b"][l]).reshape(NFF, 128).T
    m["ffn_conv"] = cw
    m["final_norm"] = f(np.asarray(inp["final_norm"]).reshape(1, D))
    m.update(host_consts(S))
    return m


_CACHE = {}


def kernel(**inputs):
    x = np.asarray(inputs["x"], np.float32)
    B, S, _ = x.shape
    key = (S,)
    if key not in _CACHE:
        _CACHE[key] = build(S)
    nc = _CACHE[key]
    shared = host_layout(inputs, S)
    n = 8
    in_maps = []
    for c in range(n):
        mp = dict(shared)
        mp["x"] = np.ascontiguousarray(x[c % B])
        in_maps.append(mp)
    res = run_bass_kernel_spmd(nc, in_maps, core_ids=list(range(n)))
    return np.stack([res.results[b]["out"] for b in range(B)], axis=0).astype(np.float32)
```

```python
import contextlib
import os
import numpy as np
import ml_dtypes
import concourse.bass as bass
import concourse.mybir as mybir
from concourse.bass_utils import run_bass_kernel_spmd

F32 = mybir.dt.float32
BF16 = mybir.dt.bfloat16
AF = mybir.ActivationFunctionType
ALU = mybir.AluOpType

D = 1024
DFF = 2816
NFF = DFF // 128
EPS = 1e-6
NEG = -30000.0
ENGS = ['sp', 'act', 'dve', 'pool', 'pe']


class Buf:
    __slots__ = ('w', 'r', 'rd')

    def __init__(self):
        self.w = None
        self.r = {}
        self.rd = []


class Op:
    __slots__ = ('eng', 'fn', 'deps', 'flag', 'sem', 'val', 'dma', 'slot')


class Sched:
    def __init__(self, nc, nd=40):
        self.nc = nc
        self.q = {e: [] for e in ENGS}
        self.dmas = []
        self.ND = nd
        self.pending_dma = []
        self.last_real = {}

    def op(self, eng, fn, r=(), w=()):
        o = Op()
        o.eng = eng
        o.fn = fn
        o.flag = False
        o.dma = False
        o.sem = None
        o.val = 0
        deps = {}

        def add(d, strong):
            if d is None or d is o:
                return
            k = id(d)
            if k in deps:
                deps[k] = (d, deps[k][1] or strong)
            else:
                deps[k] = (d, strong)
        for b in r:
            add(b.w, True)
        for b in w:
            add(b.w, True)
            for d in b.r.values():
                add(d, False)
            for d in b.rd:
                add(d, True)
        o.deps = list(deps.values())
        self.q[eng].append(o)
        self.last_real[eng] = o
        return o

    def _post(self, o, r, w):
        for b in r:
            if o.dma:
                b.rd.append(o)
            else:
                b.r[o.eng] = o
        for b in w:
            b.w = o
            b.r = {}
            b.rd = []

    def cop(self, eng, fn, r=(), w=()):
        o = self.op(eng, fn, r, w)
        self._post(o, r, w)
        return o

    def dma(self, out, in_, r=(), w=(), eng='sp'):
        o = self.op(eng, lambda e: e.dma_start(out=out, in_=in_), r, w)
        o.dma = True
        o.flag = True
        n = len(self.dmas)
        o.slot = n % self.ND
        o.val = 16 * (n // self.ND + 1)
        if n >= self.ND:
            o.deps.append((self.dmas[n - self.ND], True))
        self.dmas.append(o)
        self.pending_dma.append(o)
        self._post(o, r, w)
        return o

    def barrier(self):
        last = list(self.last_real.values())
        self.last_real = {}
        pend = self.pending_dma
        self.pending_dma = []
        for e in ENGS:
            o = Op()
            o.eng = e
            o.fn = None
            o.flag = False
            o.dma = False
            o.deps = [(d, True) for d in last if d.eng != e or d.dma] + [(d, True) for d in pend]
            self.q[e].append(o)

    def mm(self, out, lhsT, rhs, start, stop, r=(), w=(), skip=False):
        return self.cop('pe', lambda e: e.matmul(out, lhsT, rhs, start=start, stop=stop, skip_group_check=skip), r, w)

    def tr(self, out, in_, ident, r=(), w=()):
        return self.cop('pe', lambda e: e.transpose(out, in_, ident), r, w)

    def act(self, out, in_, func, r=(), w=(), **kw):
        return self.cop('act', lambda e: e.activation(out, in_, func, **kw), r, w)

    def ts(self, eng, out, in0, s1, s2, op0, op1=None, r=(), w=()):
        if op1 is None and eng == 'pool' and op0 == ALU.mult:
            return self.cop(eng, lambda e: e.tensor_scalar(out, in0, s1, 0.0, ALU.mult, ALU.add), r, w)
        if op1 is None:
            return self.cop(eng, lambda e: e.tensor_scalar(out, in0, s1, None, op0), r, w)
        return self.cop(eng, lambda e: e.tensor_scalar(out, in0, s1, s2, op0, op1), r, w)

    def tt(self, eng, out, in0, in1, op, r=(), w=()):
        return self.cop(eng, lambda e: e.tensor_tensor(out, in0, in1, op), r, w)

    def stt(self, eng, out, in0, scalar, in1, op0, op1, r=(), w=()):
        return self.cop(eng, lambda e: e.scalar_tensor_tensor(out, in0, scalar, in1, op0, op1), r, w)

    def copy(self, eng, out, in_, r=(), w=()):
        if eng == 'act':
            return self.cop('act', lambda e: e.copy(out, in_), r, w)
        return self.cop(eng, lambda e: e.tensor_copy(out, in_), r, w)

    def memset(self, eng, ap, val, w=()):
        return self.cop(eng, lambda e: e.memset(ap, val), (), w)

    @staticmethod
    def _needs(o, d, strong):
        if d.dma:
            return True
        if d.eng != o.eng:
            return True
        if o.eng == 'pe':
            return False
        return strong

    def emit(self, es):
        nc = self.nc
        for e in ENGS:
            for o in self.q[e]:
                for (d, strong) in o.deps:
                    if self._needs(o, d, strong):
                        d.flag = True
        esem = {e: es.enter_context(nc.semaphore('S_' + e)) for e in ENGS}
        dsem = [es.enter_context(nc.semaphore('D%d' % i)) for i in range(self.ND)]
        for e in ENGS:
            cnt = 0
            for o in self.q[e]:
                if o.dma:
                    o.sem = dsem[o.slot]
                elif o.flag:
                    cnt += 1
                    o.sem = esem[e]
                    o.val = cnt
        block = es.enter_context(nc.Block())

        def run(e, eng):
            waited = {}
            for o in self.q[e]:
                for (d, strong) in o.deps:
                    if not self._needs(o, d, strong):
                        continue
                    k = id(d.sem)
                    if waited.get(k, 0) >= d.val:
                        continue
                    eng.wait_ge(d.sem, d.val)
                    waited[k] = d.val
                if o.fn is None:
                    continue
                inst = o.fn(eng)
                if o.flag:
                    inst.then_inc(o.sem, 16 if o.dma else 1)

        block.sync(lambda eng: run('sp', eng))
        block.scalar(lambda eng: run('act', eng))
        block.vector(lambda eng: run('dve', eng))
        block.gpsimd(lambda eng: run('pool', eng))
        block.tensor(lambda eng: run('pe', eng))


def build(S, stop_after=None):
    nc = bass.Bass("TRN2", target_bir_lowering=False)
    NB = S // 512
    NT = S // 128

    def din(name, shape, dt=F32):
        return nc.dram_tensor(name, list(shape), dt, kind="ExternalInput").ap()

    def dscr(name, shape, dt):
        return nc.dram_tensor(name, list(shape), dt).ap()

    x_in = din("x", [S, D])
    ab_w_in = din("ab_w_in", [D, 3072])
    ab_w_o = din("ab_w_o", [D, D])
    ab_norm = din("ab_norm", [128, 8])
    relE = din("relE", [8, 128, 1408])
    cd_w_in = din("cd_w_in", [D, 2216])
    cd_w_o = din("cd_w_o", [D, D])
    cd_norm = din("cd_norm", [128, 8])
    cd_q_norm = din("cd_q_norm", [128, 3])
    cd_kv_norm = din("cd_kv_norm", [128, 2])
    cd_w_uq = din("cd_w_uq", [384, 768])
    cd_w_ukv = din("cd_w_ukv", [256, 1024])
    cd_b_f = din("cd_b_f", [8, 1])
    ffn_norm = din("ffn_norm", [2, 128, 8])
    ffn_w_gate = din("ffn_w_gate", [2, D, DFF])
    ffn_w_up = din("ffn_w_up", [2, D, DFF])
    ffn_w_down = din("ffn_w_down", [2, DFF, D])
    ffn_conv = din("ffn_conv", [2, 128, NFF, 4])
    final_norm = din("final_norm", [1, D])
    c_ident = din("c_ident", [128, 128], BF16)
    c_tri = din("c_tri", [2, 128, 128], BF16)
    c_maskA = din("c_maskA", [8, 128, 512])
    c_maskD = din("c_maskD", [3, 4, 128, 512], BF16)
    c_m01 = din("c_m01", [2, 128, 128], BF16)
    c_rope = din("c_rope", [2, 32, S])
    out = nc.dram_tensor("out", [S, D], F32, kind="ExternalOutput").ap()

    QT = dscr("QT_scr", [16, 96, S], BF16)
    KT = dscr("KT_scr", [16, 96, S], BF16)
    VS = dscr("V_scr", [16, 128, NT, 65], BF16)
    OS = dscr("O_scr", [NT, 128, D], BF16)
    KR = dscr("KR_scr", [32, S], BF16)
    xa = dscr("xa", [S, D], F32)
    xb = dscr("xb", [S, D], F32)

    K = Sched(nc)
    es_top = contextlib.ExitStack()

    uid = [0]

    def sb(es, name, shape, dt):
        uid[0] += 1
        return es.enter_context(nc.sbuf_tensor(f"{name}_{uid[0]}", list(shape), dt))

    def ps(es, name, shape, dt):
        uid[0] += 1
        return es.enter_context(nc.psum_tensor(f"{name}_{uid[0]}", list(shape), dt))

    ident = sb(es_top, "ident", [128, 128], BF16)
    b_ident = Buf()
    K.dma(ident[:], c_ident[:, :], w=[b_ident])

    def load_weight_folded(es, Wt, wbuf, src, ncols, g_sb, b_g, kchunks, colchunk, name):
        st = [sb(es, f"{name}_st{i}", [128, colchunk], F32) for i in range(2)]
        bst = [Buf(), Buf()]
        i = 0
        for kc in range(kchunks):
            for c0 in range(0, ncols, colchunk):
                cw = min(colchunk, ncols - c0)
                j = i % 2
                K.dma(st[j][:, 0:cw], src[kc * 128:(kc + 1) * 128, c0:c0 + cw], w=[bst[j]])
                eng = 'dve' if i % 2 == 0 else 'pool'
                if g_sb is None:
                    K.copy(eng, Wt[:, kc, c0:c0 + cw], st[j][:, 0:cw], r=[bst[j]], w=[wbuf])
                else:
                    K.ts(eng, Wt[:, kc, c0:c0 + cw], st[j][:, 0:cw], g_sb[:, kc:kc + 1], None, ALU.mult,
                         r=[bst[j], b_g], w=[wbuf])
                i += 1

    def norm_block(xt, bx, ntile, ss, lnv, rstd, junk, bjunk, bss, width=D):
        for t in range(ntile):
            K.act(junk[:, 0:width], xt[:, t, 0:width], AF.Square, r=[bx[t]], w=[bjunk, bss],
                  accum_out=ss[:, t:t + 1])
        K.act(lnv[:, 0:ntile], ss[:, 0:ntile], AF.Ln, r=[bss], w=[bss], bias=EPS, scale=1.0 / width)
        K.act(rstd[:, 0:ntile], lnv[:, 0:ntile], AF.Exp, r=[bss], w=[bss], scale=-0.5)

    def phase_inproj_ab(xsrc):
        with contextlib.ExitStack() as es:
            W = sb(es, "W_ab", [128, 8, 3072], BF16)
            bW = Buf()
            g_sb = sb(es, "g_ab", [128, 8], F32)
            b_g = Buf()
            K.dma(g_sb[:], ab_norm[:, :], w=[b_g])
            load_weight_folded(es, W, bW, ab_w_in, 3072, g_sb, b_g, 8, 1536, "wab")
            xt = [sb(es, f"xt{i}", [128, 4, D], F32) for i in range(2)]
            bxt = [[Buf() for _ in range(4)] for _ in range(2)]
            junk = sb(es, "junk", [128, D], BF16)
            bjunk = Buf()
            ss = sb(es, "ss", [128, 4], F32)
            lnv = sb(es, "lnv", [128, 4], F32)
            rstd = sb(es, "rstd", [128, 4], F32)
            bss = Buf()
            xn = sb(es, "xn", [128, 4, D], BF16)
            bxn = [Buf() for _ in range(4)]
            hT = [sb(es, f"hT{i}", [128, 8, 512], BF16) for i in range(2)]
            bhT = [Buf(), Buf()]
            qst = [sb(es, f"qst{i}", [128, 512], BF16) for i in range(4)]
            bqst = [Buf() for _ in range(4)]
            vst = [sb(es, f"vst{i}", [128, 16, 4, 65], BF16) for i in range(2)]
            bvst = [Buf(), Buf()]
            tp = [ps(es, f"tp{i}", [128, 512], BF16) for i in range(2)]
            btp = [Buf(), Buf()]
            pm = [ps(es, f"pm{i}", [128, 512], F32) for i in range(4)]
            bpm = [Buf() for _ in range(4)]
            for i in range(2):
                K.memset('pool', vst[i][:, :, :, 64:65], 1.0, w=[bvst[i]])
            ipm = 0
            iq = 0
            for b in range(NB):
                xb_ = b % 2
                src = xsrc[b * 512:(b + 1) * 512, :].rearrange("(t p) c -> p t c", p=128)
                for t in range(4):
                    K.dma(xt[xb_][:, t, :], src[:, t, :], w=[bxt[xb_][t]])
                norm_block(xt[xb_], bxt[xb_], 4, ss, lnv, rstd, junk, bjunk, bss)
                for t in range(4):
                    K.ts('dve' if t % 2 == 0 else 'pool', xn[:, t, :], xt[xb_][:, t, :], rstd[:, t:t + 1], None,
                         ALU.mult, r=[bxt[xb_][t], bss], w=[bxn[t]])
                hb = b % 2
                for kc in range(8):
                    j = kc % 2
                    for t in range(4):
                        K.tr(tp[j][:, t * 128:(t + 1) * 128], xn[:, t, kc * 128:(kc + 1) * 128], ident[:],
                             r=[bxn[t], b_ident], w=[btp[j]])
                    K.copy('act' if kc % 2 == 0 else 'dve', hT[hb][:, kc, :], tp[j][:, :], r=[btp[j]], w=[bhT[hb]])
                for (c0, dst, h0) in ((0, QT, 0), (512, KT, 0), (1536, QT, 8), (2048, KT, 8)):
                    for hp in range(4):
                        bk = ipm % 4
                        ipm += 1
                        for kc in range(8):
                            K.mm(pm[bk][:, :], W[:, kc, c0 + hp * 128:c0 + (hp + 1) * 128], hT[hb][:, kc, :],
                                 kc == 0, kc == 7, r=[bW, bhT[hb]], w=[bpm[bk]])
                        qi = iq % 4
                        iq += 1
                        K.copy('act' if qi % 2 == 0 else 'dve', qst[qi][:, :], pm[bk][:, :], r=[bpm[bk]], w=[bqst[qi]])
                        for hh in range(2):
                            K.dma(dst[h0 + hp * 2 + hh, 0:64, b * 512:(b + 1) * 512], qst[qi][hh * 64:(hh + 1) * 64, :],
                                  r=[bqst[qi]])
                vb_ = b % 2
                for (c0, h0) in ((1024, 0), (2560, 8)):
                    for t in range(4):
                        bk = ipm % 4
                        ipm += 1
                        for kc in range(8):
                            K.mm(pm[bk][:, :], hT[hb][:, kc, t * 128:(t + 1) * 128], W[:, kc, c0:c0 + 512],
                                 kc == 0, kc == 7, r=[bW, bhT[hb]], w=[bpm[bk]])
                        K.copy('act' if t % 2 == 0 else 'dve', vst[vb_][:, h0:h0 + 8, t, 0:64],
                               pm[bk][:, :].rearrange("p (h d) -> p h d", d=64), r=[bpm[bk]], w=[bvst[vb_]])
                K.dma(VS[:, :, b * 4:(b + 1) * 4, :].rearrange("h p t c -> p h t c"), vst[vb_][:, :, :, :], r=[bvst[vb_]])
            K.barrier()

    def attn_softmax_heads(es, heads, krows, scale, kind, maskset=None, bias_src=None, kr_rows=False):
        kt_sb = [sb(es, f"kt{i}", [96, S], BF16) for i in range(2)]
        qt_sb = [sb(es, f"qt{i}", [96, S], BF16) for i in range(2)]
        v_sb = [sb(es, f"v{i}", [128, NT, 65], BF16) for i in range(2)]
        o_sb = [sb(es, f"o{i}", [128, NT, 64], BF16) for i in range(2)]
        bkt = [Buf(), Buf()]
        bqt = [Buf(), Buf()]
        bv = [Buf(), Buf()]
        bo = [Buf(), Buf()]
        pt = [sb(es, f"pt{i}", [128, 512], BF16) for i in range(4)]
        bpt = [Buf() for _ in range(4)]
        rc = sb(es, "rc", [128, 4], F32)
        brc = Buf()
        z = [ps(es, f"z{i}", [128, 512], F32) for i in range(3)]
        bz = [Buf() for _ in range(3)]
        ob = [ps(es, f"ob{i}", [128, 512], F32) for i in range(4)]
        bob = [Buf() for _ in range(4)]
        if kind == 'A':
            bia = [sb(es, f"bia{i}", [128, 8, 512], BF16) for i in range(2)]
            bbia = [Buf(), Buf()]
            bst = [sb(es, f"bst{i}", [128, 512], F32) for i in range(2)]
            bbst = [Buf(), Buf()]
            mA = sb(es, "mA", [128, 8, 512], F32)
            bmA = Buf()
            for r_ in range(8):
                K.dma(mA[:, r_, :], c_maskA[r_, :, :], w=[bmA])
        else:
            m01 = sb(es, "m01", [128, 128], BF16)
            bm01 = Buf()
            K.dma(m01[:, :], c_m01[maskset - 1, :, :], w=[bm01])
        if kr_rows:
            for i in range(2):
                K.dma(kt_sb[i][64:96, :], KR[:, :], w=[bkt[i]])
        items = []
        for hi, h in enumerate(heads):
            for qb in range(NB):
                if kind == 'A':
                    tiles = [(qb * 4 - 4 + r_, r_) for r_ in range(8) if qb * 4 - 4 + r_ >= 0]
                else:
                    tiles = [(kt_, (kt_ - qb * 4) if kt_ >= qb * 4 else None) for kt_ in range(qb * 4 + 4)]
                for ti, (kt_, r_) in enumerate(tiles):
                    items.append((hi, h, qb, kt_, r_, ti, len(tiles)))
        kk = 96 if kr_rows else krows

        def head_prologue(hi, h):
            hb = hi % 2
            K.dma(kt_sb[hb][0:krows, :], KT[h, 0:krows, :], w=[bkt[hb]])
            K.dma(qt_sb[hb][0:kk, :], QT[h, 0:kk, :], w=[bqt[hb]])
            K.dma(v_sb[hb][:, :, :], VS[h, :, :, :], w=[bv[hb]])
            if kind == 'A':
                for r_ in range(8):
                    j = r_ % 2
                    K.dma(bst[j][:, :], bias_src[hi, :, 896 - 128 * r_:896 - 128 * r_ + 512], w=[bbst[j]])
                    K.stt('dve', bia[hb][:, r_, :], bst[j][:, :], 1.0 / scale, mA[:, r_, :],
                          ALU.mult, ALU.add, r=[bbst[j], bmA], w=[bbia[hb]])

        def stage1(i):
            hi, h, qb, kt_, r_, ti, nt_ = items[i]
            hb = hi % 2
            if qb == 0 and ti == 0:
                head_prologue(hi, h)
            zi = i % 3
            if kind == 'A':
                K.mm(z[zi][:, :], kt_sb[hb][0:kk, kt_ * 128:(kt_ + 1) * 128], qt_sb[hb][0:kk, qb * 512:(qb + 1) * 512],
                     True, False, r=[bkt[hb], bqt[hb]], w=[bz[zi]])
                K.mm(z[zi][:, :], ident[:], bia[hb][:, r_, :], False, True, r=[b_ident, bbia[hb]], w=[bz[zi]])
                K.act(pt[i % 4][:, :], z[zi][:, :], AF.Exp, r=[bz[zi]], w=[bpt[i % 4]], scale=scale)
            else:
                c0 = 0 if r_ is None else 128 * r_
                K.mm(z[zi][:, c0:512], kt_sb[hb][0:kk, kt_ * 128:(kt_ + 1) * 128],
                     qt_sb[hb][0:kk, qb * 512 + c0:(qb + 1) * 512], True, True, r=[bkt[hb], bqt[hb]], w=[bz[zi]])
                K.act(pt[i % 4][:, c0:512], z[zi][:, c0:512], AF.Exp, r=[bz[zi]], w=[bpt[i % 4]], scale=scale)
                if r_ is not None:
                    K.tt('dve', pt[i % 4][:, c0:c0 + 128], pt[i % 4][:, c0:c0 + 128], m01[:, :], ALU.mult,
                         r=[bpt[i % 4], bm01], w=[bpt[i % 4]])

        def stage2(i):
            hi, h, qb, kt_, r_, ti, nt_ = items[i]
            hb = hi % 2
            pi = i % 4
            for qs in range(4):
                if kind != 'A' and r_ is not None and qs < r_:
                    continue
                last = (ti == nt_ - 1) if kind == 'A' else (r_ is not None and r_ == qs)
                K.mm(ob[qs][:, 0:65], pt[pi][:, qs * 128:(qs + 1) * 128], v_sb[hb][:, kt_, 0:65],
                     ti == 0, last, r=[bpt[pi], bv[hb]], w=[bob[qs]])
            if ti == nt_ - 1:
                for qs in range(4):
                    K.cop('dve', lambda e, o_=rc[:, qs:qs + 1], i_=ob[qs][:, 64:65]: e.reciprocal(o_, i_),
                          r=[bob[qs]], w=[brc])
                    K.ts('dve', o_sb[hb][:, qb * 4 + qs, :], ob[qs][:, 0:64], rc[:, qs:qs + 1], None, ALU.mult,
                         r=[bob[qs], brc], w=[bo[hb]])
                if qb == NB - 1:
                    K.dma(OS[:, :, h * 64:(h + 1) * 64].rearrange("t p d -> p t d"), o_sb[hb][:, :, :], r=[bo[hb]])

        n_it = len(items)
        for i in range(n_it + 2):
            if i < n_it:
                stage1(i)
            if i >= 2:
                stage2(i - 2)

    def attn_stick_heads(es, heads, scale):
        kt_sb = [sb(es, f"skt{i}", [64, S], BF16) for i in range(2)]
        qt_sb = [sb(es, f"sqt{i}", [64, S], BF16) for i in range(2)]
        v_sb = [sb(es, f"sv{i}", [128, NT, 65], BF16) for i in range(2)]
        o_sb = [sb(es, f"so{i}", [128, NT, 64], BF16) for i in range(2)]
        bkt = [Buf(), Buf()]
        bqt = [Buf(), Buf()]
        bv = [Buf(), Buf()]
        bo = [Buf(), Buf()]
        tri = sb(es, "tri", [128, 2, 128], BF16)
        btri = Buf()
        K.dma(tri[:, 0, :], c_tri[0, :, :], w=[btri])
        K.dma(tri[:, 1, :], c_tri[1, :, :], w=[btri])
        msk = sb(es, "smsk", [128, 4, 512], BF16)
        bmsk = Buf()
        for r_ in range(4):
            K.dma(msk[:, r_, :], c_maskD[0, r_, :, :], w=[bmsk])
        NCH = 2
        ee = [[sb(es, f"ee{c}_{i}", [128, 512], F32) for i in range(4)] for c in range(NCH)]
        bee = [[Buf() for _ in range(4)] for c in range(NCH)]
        sp_ = [[sb(es, f"sp{c}_{i}", [128, 512], BF16) for i in range(4)] for c in range(NCH)]
        bsp = [[Buf() for _ in range(4)] for c in range(NCH)]
        xx = [[sb(es, f"xx{c}_{i}", [128, 512], F32) for i in range(2)] for c in range(NCH)]
        bxx = [[Buf() for _ in range(2)] for c in range(NCH)]
        pt = [[sb(es, f"spt{c}_{i}", [128, 512], BF16) for i in range(4)] for c in range(NCH)]
        bpt = [[Buf() for _ in range(4)] for c in range(NCH)]
        z = [ps(es, f"sz{c}", [128, 512], F32) for c in range(NCH)]
        bz = [Buf() for _ in range(NCH)]
        acc = [ps(es, f"sacc{c}", [128, 512], F32) for c in range(NCH)]
        bacc = [Buf() for _ in range(NCH)]
        ob = [ps(es, f"sob{c}", [128, 512], F32) for c in range(NCH)]
        bob = [Buf() for _ in range(NCH)]
        zer = sb(es, "szer", [128, 128], BF16)
        bzer = Buf()
        K.memset('pool', zer[:, :], 0.0, w=[bzer])
        items = []
        rot = [0] * NCH
        for hi, h in enumerate(heads):
            qbs = list(range(NB))
            for g0 in range(0, NB, NCH):
                grp = qbs[g0:g0 + NCH][::-1]
                chains = []
                for c, qb in enumerate(grp):
                    ch = []
                    for ti, kt_ in enumerate(range(qb * 4 + 3, -1, -1)):
                        ch.append((hi, h, qb, kt_, ti, qb * 4 + 4, c, rot[c]))
                        rot[c] += 1
                    chains.append(ch)
                mx = max(len(ch) for ch in chains)
                for k in range(mx):
                    for ch in chains:
                        if k < len(ch):
                            items.append(ch[k])

        def prologue(i):
            hi, h, qb, kt_, ti, nt_, c, rt = items[i]
            hb = hi % 2
            if i == 0 or items[i - 1][0] != hi:
                K.dma(kt_sb[hb][:, :], KT[h, 0:64, :], w=[bkt[hb]])
                K.dma(qt_sb[hb][:, :], QT[h, 0:64, :], w=[bqt[hb]])
                K.dma(v_sb[hb][:, :, :], VS[h, :, :, :], w=[bv[hb]])

        def pe_z(i):
            hi, h, qb, kt_, ti, nt_, c, rt = items[i]
            hb = hi % 2
            diag = kt_ >= qb * 4
            K.mm(z[c][:, :], kt_sb[hb][:, kt_ * 128:(kt_ + 1) * 128], qt_sb[hb][:, qb * 512:(qb + 1) * 512],
                 True, not diag, r=[bkt[hb], bqt[hb]], w=[bz[c]])
            if diag:
                K.mm(z[c][:, :], ident[:], msk[:, kt_ - qb * 4, :], False, True, r=[b_ident, bmsk], w=[bz[c]])

        def act_e(i):
            hi, h, qb, kt_, ti, nt_, c, rt = items[i]
            ei = rt % 4
            K.act(ee[c][ei][:, :], z[c][:, :], AF.Exp, r=[bz[c]], w=[bee[c][ei]], scale=scale)

        def act_ln(i):
            hi, h, qb, kt_, ti, nt_, c, rt = items[i]
            ei = rt % 4
            K.act(sp_[c][ei][:, :], ee[c][ei][:, :], AF.Ln, r=[bee[c][ei]], w=[bsp[c][ei]], bias=1.0, scale=1.0)

        def pe_tri(i):
            hi, h, qb, kt_, ti, nt_, c, rt = items[i]
            ei = rt % 4
            K.mm(acc[c][:, :], tri[:, 0, :], sp_[c][ei][:, :], ti == 0, True, r=[btri, bsp[c][ei]], w=[bacc[c]], skip=True)

        def act_x(i):
            hi, h, qb, kt_, ti, nt_, c, rt = items[i]
            xi = rt % 2
            K.act(xx[c][xi][:, :], acc[c][:, :], AF.Exp, r=[bacc[c]], w=[bxx[c][xi]], scale=-1.0)

        def pe_excl(i):
            hi, h, qb, kt_, ti, nt_, c, rt = items[i]
            ei = rt % 4
            if ti < nt_ - 1:
                K.mm(acc[c][:, :], tri[:, 1, :], sp_[c][ei][:, :], False, True, r=[btri, bsp[c][ei]], w=[bacc[c]], skip=True)

        def dve_pt(i):
            hi, h, qb, kt_, ti, nt_, c, rt = items[i]
            ei = rt % 4
            xi = rt % 2
            K.tt('dve', pt[c][ei][:, :], ee[c][ei][:, :], xx[c][xi][:, :], ALU.mult, r=[bee[c][ei], bxx[c][xi]],
                 w=[bpt[c][ei]])

        def pe_pv(i):
            hi, h, qb, kt_, ti, nt_, c, rt = items[i]
            hb = hi % 2
            ei = rt % 4
            if ti == 0:
                K.mm(ob[c][:, 0:256], zer[:, :], msk[:, 0, 0:256], True, False, r=[bzer, bmsk], w=[bob[c]], skip=True)
            for qs in range(4):
                K.mm(ob[c][:, qs * 64:(qs + 1) * 64], pt[c][ei][:, qs * 128:(qs + 1) * 128], v_sb[hb][:, kt_, 0:64],
                     False, ti == nt_ - 1, r=[bpt[c][ei], bv[hb]], w=[bob[c]], skip=True)
            if ti == nt_ - 1:
                K.copy('dve', o_sb[hb][:, qb * 4:qb * 4 + 4, :], ob[c][:, 0:256].rearrange("p (q d) -> p q d", d=64),
                       r=[bob[c]], w=[bo[hb]])
                if qb == NB - 1:
                    K.dma(OS[:, :, h * 64:(h + 1) * 64].rearrange("t p d -> p t d"), o_sb[hb][:, :, :], r=[bo[hb]])

        n_it = len(items)
        for i in range(n_it + 5):
            ok = lambda j: 0 <= j < n_it
            if ok(i):
                prologue(i)
            if ok(i - 3):
                pe_tri(i - 3)
                act_x(i - 3)
            if ok(i):
                pe_z(i)
            if ok(i - 1):
                act_ln(i - 1)
            if ok(i - 5):
                pe_pv(i - 5)
            if ok(i):
                act_e(i)
            if ok(i - 3):
                pe_excl(i - 3)
                dve_pt(i - 3)

    def phase_attn_ab():
        with contextlib.ExitStack() as es:
            attn_softmax_heads(es, list(range(8)), 64, 0.125, 'A', bias_src=relE)
            K.barrier()
        with contextlib.ExitStack() as es:
            attn_stick_heads(es, list(range(8, 16)), 0.125)
            K.barrier()

    def phase_outproj(xsrc, xdst, w_o):
        with contextlib.ExitStack() as es:
            W = sb(es, "W_o", [128, 8, D], BF16)
            bW = Buf()
            load_weight_folded(es, W, bW, w_o, D, None, None, 8, D, "wo")
            xt = [sb(es, f"oxt{i}", [128, 4, D], F32) for i in range(2)]
            bxt = [[Buf() for _ in range(4)] for _ in range(2)]
            ot = [sb(es, f"oblk{i}", [128, 4, D], BF16) for i in range(2)]
            bot = [Buf(), Buf()]
            oT = [sb(es, f"oT{i}", [128, 8, 512], BF16) for i in range(2)]
            boT = [Buf(), Buf()]
            tp = [ps(es, f"otp{i}", [128, 512], BF16) for i in range(2)]
            btp = [Buf(), Buf()]
            pm = [ps(es, f"opm{i}", [128, 512], F32) for i in range(4)]
            bpm = [Buf() for _ in range(4)]
            ipm = 0
            for b in range(NB):
                j2 = b % 2
                src = xsrc[b * 512:(b + 1) * 512, :].rearrange("(t p) c -> p t c", p=128)
                dst = xdst[b * 512:(b + 1) * 512, :].rearrange("(t p) c -> p t c", p=128)
                for t in range(4):
                    K.dma(xt[j2][:, t, :], src[:, t, :], w=[bxt[j2][t]])
                K.dma(ot[j2][:, :, :], OS[b * 4:(b + 1) * 4, :, :].rearrange("t p c -> p t c"), w=[bot[j2]])
                for fc in range(8):
                    j = fc % 2
                    for t in range(4):
                        K.tr(tp[j][:, t * 128:(t + 1) * 128], ot[j2][:, t, fc * 128:(fc + 1) * 128], ident[:],
                             r=[bot[j2], b_ident], w=[btp[j]])
                    K.copy('act' if fc % 2 == 0 else 'dve', oT[j2][:, fc, :], tp[j][:, :], r=[btp[j]], w=[boT[j2]])
                for t in range(4):
                    for half in range(2):
                        bk = ipm % 4
                        ipm += 1
                        for fc in range(8):
                            K.mm(pm[bk][:, :], oT[j2][:, fc, t * 128:(t + 1) * 128], W[:, fc, half * 512:(half + 1) * 512],
                                 fc == 0, fc == 7, r=[bW, boT[j2]], w=[bpm[bk]])
                        K.tt('dve', xt[j2][:, t, half * 512:(half + 1) * 512], pm[bk][:, :],
                             xt[j2][:, t, half * 512:(half + 1) * 512], ALU.add, r=[bpm[bk], bxt[j2][t]], w=[bxt[j2][t]])
                    K.dma(dst[:, t, :], xt[j2][:, t, :], r=[bxt[j2][t]])
            K.barrier()

    def phase_ffn(xsrc, xdst, layer, final):
        with contextlib.ExitStack() as es:
            Wg = sb(es, "Wg", [128, 8, DFF], BF16)
            Wu = sb(es, "Wu", [128, 8, DFF], BF16)
            Wd = sb(es, "Wd", [128, NFF, D], BF16)
            bWg, bWu, bWd = Buf(), Buf(), Buf()
            g_sb = sb(es, "g_ffn", [128, 8], F32)
            b_g = Buf()
            K.dma(g_sb[:], ffn_norm[layer, :, :], w=[b_g])
            cw = sb(es, "convw", [128, NFF, 4], F32)
            bcw = Buf()
            K.dma(cw[:], ffn_conv[layer, :, :, :], w=[bcw])
            with contextlib.ExitStack() as es2:
                load_weight_folded(es2, Wg, bWg, ffn_w_gate[layer], DFF, g_sb, b_g, 8, 1408, "wg")
                load_weight_folded(es2, Wu, bWu, ffn_w_up[layer], DFF, g_sb, b_g, 8, 1408, "wu")
                load_weight_folded(es2, Wd, bWd, ffn_w_down[layer], D, None, None, NFF, 1024, "wd")
                K.barrier()
            nx = 1 if final else 2
            xts = [sb(es, f"fxt{i}", [128, 4, D], F32) for i in range(nx)]
            bxts = [[Buf() for _ in range(4)] for _ in range(nx)]
            ss = sb(es, "fss", [128, 4], F32)
            lnv = sb(es, "flnv", [128, 4], F32)
            rstd = sb(es, "frstd", [128, 4], F32)
            bss = Buf()
            xn = [sb(es, f"fxn{i}", [128, D], BF16) for i in range(2)]
            bxn = [Buf(), Buf()]
            junk = xn[0]
            bjunk = bxn[0]
            hT = sb(es, "fhT", [128, 8, 512], BF16)
            bhT = [Buf() for _ in range(8)]
            yT = sb(es, "fyT", [128, NFF, 512], BF16)
            byT = [Buf() for _ in range(NFF)]
            gsb = [sb(es, f"fg{i}", [128, 514], F32) for i in range(2)]
            bgsb = [Buf(), Buf()]
            t1 = [sb(es, f"ft1{i}", [128, 512], F32) for i in range(2)]
            bt1 = [Buf(), Buf()]
            carry = sb(es, "fcarry", [128, NFF, 2], F32)
            bcarry = [Buf() for _ in range(NFF)]
            K.memset('pool', carry[:, :, :], 0.0, w=bcarry)
            if final:
                gfin = sb(es, "gfin", [128, D], F32)
                bgfin = Buf()
                K.dma(gfin[:, :], final_norm[0:1, :].broadcast_to([128, D]), w=[bgfin])
                fss = sb(es, "ffss", [128, 4], F32)
                flnv = sb(es, "fflnv", [128, 4], F32)
                frstd = sb(es, "ffrstd", [128, 4], F32)
                bfss = Buf()
            tp = [ps(es, f"ftp{i}", [128, 512], BF16) for i in range(2)]
            btp8 = [Buf() for _ in range(8)]
            pg = [ps(es, f"fpg{i}", [128, 512], F32) for i in range(2)]
            bpg = [Buf(), Buf()]
            pu = [ps(es, f"fpu{i}", [128, 512], F32) for i in range(2)]
            bpu = [Buf(), Buf()]
            pd = [ps(es, f"fpd{i}", [128, 512], F32) for i in range(2)]
            bpd = [Buf(), Buf()]
            ipd = 0
            for b in range(NB):
                src = xsrc[b * 512:(b + 1) * 512, :].rearrange("(t p) c -> p t c", p=128)
                dst = xdst[b * 512:(b + 1) * 512, :].rearrange("(t p) c -> p t c", p=128)
                xt = xts[b % nx]
                bxt = bxts[b % nx]
                for t in range(4):
                    K.dma(xt[:, t, :], src[:, t, :], w=[bxt[t]])
                norm_block(xt, bxt, 4, ss, lnv, rstd, junk, bjunk, bss)
                for t in range(4):
                    j = t % 2
                    K.ts('dve' if t % 2 == 0 else 'pool', xn[j][:, :], xt[:, t, :], rstd[:, t:t + 1], None, ALU.mult,
                         r=[bxt[t], bss], w=[bxn[j]])
                    for k4 in range(2):
                        for kq in range(4):
                            kc = k4 * 4 + kq
                            K.tr(tp[k4][:, kq * 128:(kq + 1) * 128], xn[j][:, kc * 128:(kc + 1) * 128], ident[:],
                                 r=[bxn[j], b_ident], w=[btp8[k4]])
                        K.copy('act' if k4 == 0 else 'dve', hT[:, k4 * 4:(k4 + 1) * 4, t * 128:(t + 1) * 128],
                               tp[k4][:, :].rearrange("p (k q) -> p k q", q=128), r=[btp8[k4]],
                               w=[bhT[k4 * 4 + i] for i in range(4)])
                for fc in range(NFF):
                    j = fc % 2
                    for kc in range(8):
                        K.mm(pg[j][:, :], Wg[:, kc, fc * 128:(fc + 1) * 128], hT[:, kc, :], kc == 0, kc == 7,
                             r=[bWg, bhT[kc]], w=[bpg[j]])
                    for kc in range(8):
                        K.mm(pu[j][:, :], Wu[:, kc, fc * 128:(fc + 1) * 128], hT[:, kc, :], kc == 0, kc == 7,
                             r=[bWu, bhT[kc]], w=[bpu[j]])
                    K.copy('pool', gsb[j][:, 0:2], carry[:, fc, :], r=[bcarry[fc]], w=[bgsb[j]])
                    K.copy('act', gsb[j][:, 2:514], pg[j][:, :], r=[bpg[j]], w=[bgsb[j]])
                    K.copy('pool', carry[:, fc, :], gsb[j][:, 512:514], r=[bgsb[j]], w=[bcarry[fc]])
                    K.ts('dve', t1[j][:, :], gsb[j][:, 2:514], cw[:, fc, 2:3], cw[:, fc, 3:4], ALU.mult, ALU.add,
                         r=[bgsb[j], bcw], w=[bt1[j]])
                    K.stt('dve', t1[j][:, :], gsb[j][:, 1:513], cw[:, fc, 1:2], t1[j][:, :], ALU.mult, ALU.add,
                          r=[bgsb[j], bcw, bt1[j]], w=[bt1[j]])
                    K.stt('dve', t1[j][:, :], gsb[j][:, 0:512], cw[:, fc, 0:1], t1[j][:, :], ALU.mult, ALU.add,
                          r=[bgsb[j], bcw, bt1[j]], w=[bt1[j]])
                    K.act(t1[j][:, :], t1[j][:, :], AF.Silu, r=[bt1[j]], w=[bt1[j]])
                    K.tt('dve', yT[:, fc, :], pu[j][:, :], t1[j][:, :], ALU.mult, r=[bpu[j], bt1[j]], w=[byT[fc]])
                for t in range(4):
                    for half in range(2):
                        bk = ipd % 2
                        ipd += 1
                        for fc in range(NFF):
                            K.mm(pd[bk][:, :], yT[:, fc, t * 128:(t + 1) * 128], Wd[:, fc, half * 512:(half + 1) * 512],
                                 fc == 0, fc == NFF - 1, r=[bWd, byT[fc]], w=[bpd[bk]])
                        K.tt('dve', xt[:, t, half * 512:(half + 1) * 512], pd[bk][:, :],
                             xt[:, t, half * 512:(half + 1) * 512], ALU.add, r=[bpd[bk], bxt[t]], w=[bxt[t]])
                    if final:
                        K.act(junk[:, :], xt[:, t, :], AF.Square, r=[bxt[t]], w=[bjunk, bfss], accum_out=fss[:, t:t + 1])
                        K.act(flnv[:, t:t + 1], fss[:, t:t + 1], AF.Ln, r=[bfss], w=[bfss], bias=EPS, scale=1.0 / D)
                        K.act(frstd[:, t:t + 1], flnv[:, t:t + 1], AF.Exp, r=[bfss], w=[bfss], scale=-0.5)
                        K.stt('dve', xt[:, t, :], xt[:, t, :], frstd[:, t:t + 1], gfin[:, :], ALU.mult, ALU.mult,
                              r=[bxt[t], bfss, bgfin], w=[bxt[t]])
                    K.dma(dst[:, t, :], xt[:, t, :], r=[bxt[t]])
            K.barrier()

    CUT = int(os.environ.get('DBGCUT', '99'))

    def phase_inproj_cd(xsrc):
        with contextlib.ExitStack() as es:
            NCOL = 2216
            W = sb(es, "W_cd", [128, 8, NCOL], BF16)
            bW = Buf()
            g_sb = sb(es, "g_cd", [128, 8], F32)
            gq_sb = sb(es, "gq_cd", [128, 3], F32)
            gkv_sb = sb(es, "gkv_cd", [128, 2], F32)
            b_g = Buf()
            K.dma(g_sb[:], cd_norm[:, :], w=[b_g])
            K.dma(gq_sb[:], cd_q_norm[:, :], w=[b_g])
            K.dma(gkv_sb[:], cd_kv_norm[:, :], w=[b_g])
            nbf = sb(es, "nbf", [8, 1], F32)
            bnbf = Buf()
            K.dma(nbf[:], cd_b_f[:, :], w=[bnbf])
            K.ts('dve', nbf[:], nbf[:], -1.0, None, ALU.mult, r=[bnbf], w=[bnbf])
            Wuq = sb(es, "Wuq", [128, 3, 768], BF16)
            Wuqr = sb(es, "Wuqr", [128, 3, 768], BF16)
            Wukv = sb(es, "Wukv", [128, 2, 1024], BF16)
            Wr96 = sb(es, "Wr96", [128, 8, 96], BF16)
            bWuq, bWuqr, bWukv, bWr96 = Buf(), Buf(), Buf(), Buf()
            with contextlib.ExitStack() as es2:
                load_weight_folded(es2, W, bW, cd_w_in, NCOL, g_sb, b_g, 8, 1108, "wcd")
                load_weight_folded(es2, Wuq, bWuq, cd_w_uq, 768, gq_sb, b_g, 3, 768, "wuq")
                load_weight_folded(es2, Wukv, bWukv, cd_w_ukv, 1024, gkv_sb, b_g, 2, 1024, "wukv")
                K.barrier()
            K.copy('dve', Wuqr[:, :, :], Wuq[:, :, :], r=[bWuq], w=[bWuqr])
            v4 = Wuq[:, :, :].rearrange("p c (h k) -> p c h k", k=96)
            v4r = Wuqr[:, :, :].rearrange("p c (h k) -> p c h k", k=96)
            for c in range(3):
                K.ts('dve', v4r[:, c, :, 64:80], v4[:, c, :, 80:96], -1.0, None, ALU.mult, r=[bWuq], w=[bWuqr])
                K.copy('dve', v4r[:, c, :, 80:96], v4[:, c, :, 64:80], r=[bWuq], w=[bWuqr])
            K.memset('pool', Wr96[:, :, :], 0.0, w=[bWr96])
            K.ts('dve', Wr96[:, :, 64:80], W[:, :, 656:672], -1.0, None, ALU.mult, r=[bW], w=[bWr96])
            K.copy('dve', Wr96[:, :, 80:96], W[:, :, 640:656], r=[bW], w=[bWr96])

            xt = [sb(es, f"cxt{i}", [128, 4, D], F32) for i in range(2)]
            bxt = [[Buf() for _ in range(4)] for _ in range(2)]
            junk = sb(es, "cjunk", [128, D], BF16)
            bjunk = Buf()
            ss = sb(es, "css", [128, 4], F32)
            lnv = sb(es, "clnv", [128, 4], F32)
            rstd = sb(es, "crstd", [128, 4], F32)
            bss = Buf()
            xn = sb(es, "cxn", [128, 4, D], BF16)
            bxn = [Buf() for _ in range(4)]
            hT = [sb(es, f"chT{i}", [128, 8, 512], BF16) for i in range(2)]
            bhT = [Buf(), Buf()]
            cqf = sb(es, "cqf", [128, 4, 640], F32)
            bcqf = [Buf() for _ in range(4)]
            ss2 = sb(es, "css2", [128, 8], F32)
            lnv2 = sb(es, "clnv2", [128, 8], F32)
            rstd2 = sb(es, "crstd2", [128, 8], F32)
            bss2 = Buf()
            cn = sb(es, "ccn", [128, 4, 640], BF16)
            bcn = [Buf() for _ in range(4)]
            cT = sb(es, "ccT", [128, 5, 512], BF16)
            bcT = Buf()
            qst = [sb(es, f"cqst{i}", [128, 512], BF16) for i in range(4)]
            bqst = [Buf() for _ in range(4)]
            tmpa = [sb(es, f"ctmpa{i}", [96, 512], F32) for i in range(2)]
            tmpb = [sb(es, f"ctmpb{i}", [96, 512], F32) for i in range(2)]
            btmp = [Buf(), Buf()]
            rope_sb = [sb(es, f"crope{i}", [96, 2, 512], F32) for i in range(2)]
            brope = [Buf(), Buf()]
            vst = [sb(es, f"cvst{i}", [128, 16, 4, 65], BF16) for i in range(2)]
            bvst = [Buf(), Buf()]
            kaug = sb(es, "kaug", [8, 6, 512], BF16)
            qaug = sb(es, "qaug", [8, 6, 512], BF16)
            bkaug, bqaug = Buf(), Buf()
            fe = sb(es, "fe", [8, 512], F32)
            fsp = sb(es, "fsp", [8, 512], F32)
            fones = sb(es, "fones", [8, 512], F32)
            fG = [sb(es, f"fG{i}", [8, 512], F32) for i in range(2)]
            fr1 = sb(es, "fr1", [8, 512], F32)
            fr2 = sb(es, "fr2", [8, 512], F32)
            bfe, bfsp, bfones, bfr = Buf(), Buf(), Buf(), Buf()
            bfG = [Buf(), Buf()]
            K.memset('pool', fones[:, :], 1.0, w=[bfones])
            K.memset('pool', kaug[:, 3:6, :], 1.0, w=[bkaug])
            K.memset('pool', qaug[:, 0:3, :], 1.0, w=[bqaug])
            K.memset('pool', fG[1][:, :], 0.0, w=[bfG[1]])
            tp = [ps(es, f"ctp{i}", [128, 512], BF16) for i in range(2)]
            btp = [Buf(), Buf()]
            pm = [ps(es, f"cpm{i}", [128, 512], F32) for i in range(5)]
            bpm = [Buf() for _ in range(5)]
            for i in range(2):
                K.memset('pool', vst[i][:, :, :, 64:65], 1.0, w=[bvst[i]])
            ipm = 0
            iq = 0
            itm = 0

            def nxt():
                nonlocal ipm
                bk = ipm % 5
                ipm += 1
                return bk

            def rope_rows(pa, pb, dst_ap, rb, deps_r, deps_w):
                nonlocal itm
                j = itm % 2
                itm += 1
                K.tt('dve', tmpa[j][64:96, :], pm[pa][64:96, :], rope_sb[rb][64:96, 0, :], ALU.mult,
                     r=[bpm[pa], brope[rb]], w=[btmp[j]])
                K.tt('dve', tmpb[j][64:96, :], pm[pb][64:96, :], rope_sb[rb][64:96, 1, :], ALU.mult,
                     r=[bpm[pb], brope[rb]], w=[btmp[j]])
                K.tt('pool', dst_ap, tmpa[j][64:96, :], tmpb[j][64:96, :], ALU.add, r=[btmp[j]] + deps_r, w=deps_w)

            for b in range(NB):
                xb_ = b % 2
                hb = b % 2
                src = xsrc[b * 512:(b + 1) * 512, :].rearrange("(t p) c -> p t c", p=128)
                for t in range(4):
                    K.dma(xt[xb_][:, t, :], src[:, t, :], w=[bxt[xb_][t]])
                for k2 in range(2):
                    K.dma(rope_sb[xb_][64:96, k2, :], c_rope[k2, :, b * 512:(b + 1) * 512], w=[brope[xb_]])
                norm_block(xt[xb_], bxt[xb_], 4, ss, lnv, rstd, junk, bjunk, bss)
                for t in range(4):
                    K.ts('dve' if t % 2 == 0 else 'pool', xn[:, t, :], xt[xb_][:, t, :], rstd[:, t:t + 1], None,
                         ALU.mult, r=[bxt[xb_][t], bss], w=[bxn[t]])
                for kc in range(8):
                    j = kc % 2
                    for t in range(4):
                        K.tr(tp[j][:, t * 128:(t + 1) * 128], xn[:, t, kc * 128:(kc + 1) * 128], ident[:],
                             r=[bxn[t], b_ident], w=[btp[j]])
                    K.copy('act' if kc % 2 == 0 else 'dve', hT[hb][:, kc, :], tp[j][:, :], r=[btp[j]], w=[bhT[hb]])
                if CUT < 1:
                    continue
                for t in range(4):
                    for (c0, c1, col) in ((0, 384, t), (384, 640, 4 + t)):
                        bk = nxt()
                        for kc in range(8):
                            K.mm(pm[bk][:, 0:c1 - c0], hT[hb][:, kc, t * 128:(t + 1) * 128], W[:, kc, c0:c1],
                                 kc == 0, kc == 7, r=[bW, bhT[hb]], w=[bpm[bk]])
                        K.copy('dve', cqf[:, t, c0:c1], pm[bk][:, 0:c1 - c0], r=[bpm[bk]], w=[bcqf[t]])
                        K.act(junk[:, 0:c1 - c0], cqf[:, t, c0:c1], AF.Square, r=[bcqf[t]], w=[bjunk, bss2],
                              accum_out=ss2[:, col:col + 1])
                K.act(lnv2[:, 0:4], ss2[:, 0:4], AF.Ln, r=[bss2], w=[bss2], bias=EPS, scale=1.0 / 384)
                K.act(lnv2[:, 4:8], ss2[:, 4:8], AF.Ln, r=[bss2], w=[bss2], bias=EPS, scale=1.0 / 256)
                K.act(rstd2[:, :], lnv2[:, :], AF.Exp, r=[bss2], w=[bss2], scale=-0.5)
                for t in range(4):
                    K.ts('dve', cn[:, t, 0:384], cqf[:, t, 0:384], rstd2[:, t:t + 1], None, ALU.mult,
                         r=[bcqf[t], bss2], w=[bcn[t]])
                    K.ts('pool', cn[:, t, 384:640], cqf[:, t, 384:640], rstd2[:, 4 + t:5 + t], None, ALU.mult,
                         r=[bcqf[t], bss2], w=[bcn[t]])
                for c in range(5):
                    j = c % 2
                    for t in range(4):
                        K.tr(tp[j][:, t * 128:(t + 1) * 128], cn[:, t, c * 128:(c + 1) * 128], ident[:],
                             r=[bcn[t], b_ident], w=[btp[j]])
                    K.copy('act' if c % 2 == 0 else 'dve', cT[:, c, :], tp[j][:, :], r=[btp[j]], w=[bcT])
                if CUT < 2:
                    continue
                for h in range(8):
                    pa = nxt()
                    for c in range(3):
                        K.mm(pm[pa][0:96, :], Wuq[:, c, h * 96:(h + 1) * 96], cT[:, c, :], c == 0, c == 2,
                             r=[bWuq, bcT], w=[bpm[pa]])
                    pb = nxt()
                    for c in range(3):
                        K.mm(pm[pb][0:96, :], Wuqr[:, c, h * 96:(h + 1) * 96], cT[:, c, :], c == 0, c == 2,
                             r=[bWuqr, bcT], w=[bpm[pb]])
                    qi = iq % 4
                    iq += 1
                    K.copy('dve', qst[qi][0:64, :], pm[pa][0:64, :], r=[bpm[pa]], w=[bqst[qi]])
                    rope_rows(pa, pb, qst[qi][64:96, :], xb_, [], [bqst[qi]])
                    K.dma(QT[h, 0:96, b * 512:(b + 1) * 512], qst[qi][0:96, :], r=[bqst[qi]])
                if CUT < 3:
                    continue
                for h in range(8):
                    pa = nxt()
                    for c in range(2):
                        K.mm(pm[pa][0:64, :], Wukv[:, c, h * 128:h * 128 + 64], cT[:, 3 + c, :], c == 0, c == 1,
                             r=[bWukv, bcT], w=[bpm[pa]])
                    qi = iq % 4
                    iq += 1
                    K.copy('act' if h % 2 == 0 else 'dve', qst[qi][0:64, :], pm[pa][0:64, :], r=[bpm[pa]], w=[bqst[qi]])
                    K.dma(KT[h, 0:64, b * 512:(b + 1) * 512], qst[qi][0:64, :], r=[bqst[qi]])
                pa = nxt()
                for kc in range(8):
                    K.mm(pm[pa][0:96, :], W[:, kc, 576:672], hT[hb][:, kc, :], kc == 0, kc == 7, r=[bW, bhT[hb]], w=[bpm[pa]])
                pb = nxt()
                for kc in range(8):
                    K.mm(pm[pb][0:96, :], Wr96[:, kc, :], hT[hb][:, kc, :], kc == 0, kc == 7, r=[bWr96, bhT[hb]], w=[bpm[pb]])
                qi = iq % 4
                iq += 1
                rope_rows(pa, pb, qst[qi][64:96, :], xb_, [], [bqst[qi]])
                K.dma(KR[:, b * 512:(b + 1) * 512], qst[qi][64:96, :], r=[bqst[qi]])
                if CUT < 4:
                    continue
                for (c0, dst, h0) in ((672, QT, 8), (1184, KT, 8)):
                    for hp in range(4):
                        bk = nxt()
                        for kc in range(8):
                            K.mm(pm[bk][:, :], W[:, kc, c0 + hp * 128:c0 + (hp + 1) * 128], hT[hb][:, kc, :],
                                 kc == 0, kc == 7, r=[bW, bhT[hb]], w=[bpm[bk]])
                        qi = iq % 4
                        iq += 1
                        K.copy('act' if qi % 2 == 0 else 'dve', qst[qi][:, :], pm[bk][:, :], r=[bpm[bk]], w=[bqst[qi]])
                        for hh in range(2):
                            K.dma(dst[h0 + hp * 2 + hh, 0:64, b * 512:(b + 1) * 512], qst[qi][hh * 64:(hh + 1) * 64, :],
                                  r=[bqst[qi]])
                if CUT < 5:
                    continue
                vb_ = b % 2
                for t in range(4):
                    bk = nxt()
                    for c in range(2):
                        K.mm(pm[bk][:, :].rearrange("p (h d) -> p h d", d=64), cT[:, 3 + c, t * 128:(t + 1) * 128],
                             Wukv[:, c, :].rearrange("p (h k) -> p h k", k=128)[:, :, 64:128],
                             c == 0, c == 1, r=[bWukv, bcT], w=[bpm[bk]])
                    K.copy('act' if t % 2 == 0 else 'dve', vst[vb_][:, 0:8, t, 0:64],
                           pm[bk][:, :].rearrange("p (h d) -> p h d", d=64), r=[bpm[bk]], w=[bvst[vb_]])
                for t in range(4):
                    bk = nxt()
                    for kc in range(8):
                        K.mm(pm[bk][:, :], hT[hb][:, kc, t * 128:(t + 1) * 128], W[:, kc, 1696:2208],
                             kc == 0, kc == 7, r=[bW, bhT[hb]], w=[bpm[bk]])
                    K.copy('act' if t % 2 == 0 else 'dve', vst[vb_][:, 8:16, t, 0:64],
                           pm[bk][:, :].rearrange("p (h d) -> p h d", d=64), r=[bpm[bk]], w=[bvst[vb_]])
                K.dma(VS[:, :, b * 4:(b + 1) * 4, :].rearrange("h p t c -> p h t c"), vst[vb_][:, :, :, :], r=[bvst[vb_]])
                if CUT < 6:
                    continue
                bk = nxt()
                for kc in range(8):
                    K.mm(pm[bk][0:8, :], W[:, kc, 2208:2216], hT[hb][:, kc, :], kc == 0, kc == 7, r=[bW, bhT[hb]], w=[bpm[bk]])
                K.act(fe[:, :], pm[bk][0:8, :], AF.Exp, r=[bpm[bk], bnbf], w=[bfe], bias=nbf[:, 0:1], scale=-1.0)
                K.act(fsp[:, :], fe[:, :], AF.Ln, r=[bfe], w=[bfsp], bias=1.0, scale=1.0)
                gi = b % 2
                gp = (b + 1) % 2
                K.cop('dve', lambda e, o_=fG[gi][:, :], d0=fones[:, :], d1=fsp[:, :], ini=fG[gp][:, 511:512]:
                      e.tensor_tensor_scan(o_, d0, d1, ini, ALU.mult, ALU.add),
                      r=[bfones, bfsp, bfG[gp]], w=[bfG[gi]])
                K.ts('dve', fr1[:, :], fG[gi][:, :], 8.0, None, ALU.mult, r=[bfG[gi]], w=[bfr])
                K.copy('dve', kaug[:, 0, :], fr1[:, :], r=[bfr], w=[bkaug])
                K.tt('dve', fr2[:, :], fr1[:, :], kaug[:, 0, :], ALU.subtract, r=[bfr, bkaug], w=[bfr])
                K.copy('dve', kaug[:, 1, :], fr2[:, :], r=[bfr], w=[bkaug])
                K.tt('dve', fr1[:, :], fr2[:, :], kaug[:, 1, :], ALU.subtract, r=[bfr, bkaug], w=[bfr])
                K.copy('dve', kaug[:, 2, :], fr1[:, :], r=[bfr], w=[bkaug])
                K.ts('dve', qaug[:, 3:6, :], kaug[:, 0:3, :], -1.0, None, ALU.mult, r=[bkaug], w=[bqaug])
                K.dma(KT[8:16, 64:70, b * 512:(b + 1) * 512], kaug[:, :, :], r=[bkaug])
                K.dma(QT[8:16, 64:70, b * 512:(b + 1) * 512], qaug[:, :, :], r=[bqaug])
            K.barrier()

    def phase_attn_cd():
        with contextlib.ExitStack() as es:
            attn_softmax_heads(es, list(range(8)), 64, 96.0 ** -0.5, 'C', maskset=2, kr_rows=True)
            K.barrier()
        with contextlib.ExitStack() as es:
            attn_softmax_heads(es, list(range(8, 16)), 70, 0.125, 'D', maskset=1)
            K.barrier()

    phase_inproj_ab(x_in)
    if stop_after != 'inproj0':
        phase_attn_ab()
        phase_outproj(x_in, xa, ab_w_o)
        phase_ffn(xa, out if stop_after == 'l0' else xb, 0, False)
    if stop_after is None or stop_after.startswith('dbg'):
        dbg = stop_after or 'dbg:proj,attnC,attnD,out,ffn'
        if stop_after is None or 'proj' in dbg:
            phase_inproj_cd(xb)
        if stop_after is None or 'attnC' in dbg:
            with contextlib.ExitStack() as es:
                attn_softmax_heads(es, list(range(8)), 64, 96.0 ** -0.5, 'C', maskset=2, kr_rows=True)
                K.barrier()
        if stop_after is None or 'attnD' in dbg:
            with contextlib.ExitStack() as es:
                attn_softmax_heads(es, list(range(8, 16)), 70, 0.125, 'D', maskset=1)
                K.barrier()
        if stop_after is None or 'out' in dbg:
            phase_outproj(xb, xa, cd_w_o)
        if stop_after is None or 'ffn' in dbg:
            phase_ffn(xa, out, 1, True)
    K.emit(es_top)
    es_top.close()
    return nc


def host_consts(S):
    c = {}
    c["c_ident"] = np.eye(128, dtype=np.float32).astype(ml_dtypes.bfloat16)
    j = np.arange(128)[:, None]
    s = np.arange(128)[None, :]
    tri = np.stack([(j >= s), (j < s)]).astype(np.float32)
    c["c_tri"] = tri.astype(ml_dtypes.bfloat16)
    sk = np.arange(128)[:, None]
    tq = np.arange(512)[None, :]
    mA = np.zeros((8, 128, 512), np.float32)
    for r in range(8):
        kc_ = (128 * r + sk) // 64
        qc_ = tq // 64 + 8
        valid = (kc_ <= qc_) & (kc_ >= qc_ - 8)
        mA[r] = np.where(valid, 0.0, NEG * 8)
    c["c_maskA"] = mA
    mD = np.zeros((3, 4, 128, 512), np.float32)
    for r in range(4):
        ka = 128 * r + sk
        mD[0, r] = np.where(ka < tq, 0.0, NEG)
        mD[1, r] = np.where(ka <= tq, 0.0, NEG)
        mD[2, r] = np.where(ka // 64 <= tq // 64, 0.0, NEG)
    c["c_maskD"] = mD.astype(ml_dtypes.bfloat16)
    t1_ = np.arange(128)[None, :]
    c["c_m01"] = np.stack([(sk <= t1_), (sk // 64 <= t1_ // 64)]).astype(np.float32).astype(ml_dtypes.bfloat16)
    half = 16
    inv = 10000.0 ** (-np.arange(half, dtype=np.float32) / half)
    ang = np.arange(S, dtype=np.float32)[None, :] * inv[:, None]
    cos = np.cos(ang).astype(np.float32)
    sin = np.sin(ang).astype(np.float32)
    c["c_rope"] = np.stack([np.concatenate([cos, cos], 0), np.concatenate([sin, sin], 0)]).astype(np.float32)
    return c


def host_layout(inp, S):
    m = {}
    f = lambda a: np.ascontiguousarray(a, dtype=np.float32)
    m["ab_w_in"] = f(inp["ab_w_in"][0])
    m["ab_w_o"] = f(inp["ab_w_o"][0])
    m["ab_norm"] = f(inp["ab_norm"][0].reshape(8, 128).T)
    rb = np.asarray(inp["ab_rel_bias"][0], np.float32)
    sk = np.arange(128)[:, None]
    jj = np.arange(1408)[None, :] - 384
    idx = np.clip(jj - sk, -128, 128) + 128
    m["relE"] = f(rb[:, idx])
    m["cd_w_in"] = f(inp["cd_w_in"][0])
    m["cd_w_o"] = f(inp["cd_w_o"][0])
    m["cd_norm"] = f(inp["cd_norm"][0].reshape(8, 128).T)
    m["cd_q_norm"] = f(inp["cd_q_norm"][0].reshape(3, 128).T)
    m["cd_kv_norm"] = f(inp["cd_kv_norm"][0].reshape(2, 128).T)
    m["cd_w_uq"] = f(inp["cd_w_uq"][0])
    m["cd_w_ukv"] = f(inp["cd_w_ukv"][0])
    m["cd_b_f"] = f(inp["cd_b_f"][0].reshape(8, 1))
    m["ffn_norm"] = f(np.stack([inp["ffn_norm"][l].reshape(8, 128).T for l in range(2)]))
    m["ffn_w_gate"] = f(inp["ffn_w_gate"])
    m["ffn_w_up"] = f(inp["ffn_w_up"])
    m["ffn_w_down"] = f(inp["ffn_w_down"])
    cw = np.zeros((2, 128, NFF, 4), np.float32)
    for l in range(2):
        for k in range(3):
            cw[l, :, :, k] = np.asarray(inp["ffn_conv_w"][l, k]).reshape(NFF, 128).T
        cw[l, :, :, 3] = np.asarray(inp["ffn_conv_b"][l]).reshape(NFF, 128).T
    m["ffn_conv"] = cw
    m["final_norm"] = f(np.asarray(inp["final_norm"]).reshape(1, D))
    m.update(host_consts(S))
    return m


_CACHE = {}


def kernel(**inputs):
    x = np.asarray(inputs["x"], np.float32)
    B, S, _ = x.shape
    key = (S,)
    if key not in _CACHE:
        _CACHE[key] = build(S)
    nc = _CACHE[key]
    shared = host_layout(inputs, S)
    n = B
    in_maps = []
    for c in range(n):
        mp = dict(shared)
        mp["x"] = np.ascontiguousarray(x[c % B])
        in_maps.append(mp)
    res = run_bass_kernel_spmd(nc, in_maps, core_ids=list(range(n)))
    return np.stack([res.results[b]["out"] for b in range(B)], axis=0).astype(np.float32)
```

```python
import contextlib
import os
import numpy as np
import ml_dtypes
import concourse.bass as bass
import concourse.mybir as mybir
from concourse.bass_utils import run_bass_kernel_spmd

F32 = mybir.dt.float32
BF16 = mybir.dt.bfloat16
AF = mybir.ActivationFunctionType
ALU = mybir.AluOpType

D = 1024
DFF = 2816
NFF = DFF // 128
EPS = 1e-6
NEG = -30000.0
ENGS = ['sp', 'act', 'dve', 'pool', 'pe']


class Buf:
    __slots__ = ('w', 'r', 'rd')

    def __init__(self):
        self.w = None
        self.r = {}
        self.rd = []


class Op:
    __slots__ = ('eng', 'fn', 'deps', 'flag', 'sem', 'val', 'dma', 'slot')


class Sched:
    def __init__(self, nc, nd=40):
        self.nc = nc
        self.q = {e: [] for e in ENGS}
        self.dmas = []
        self.ND = nd
        self.pending_dma = []
        self.last_real = {}

    def op(self, eng, fn, r=(), w=()):
        o = Op()
        o.eng = eng
        o.fn = fn
        o.flag = False
        o.dma = False
        o.sem = None
        o.val = 0
        deps = {}

        def add(d, strong):
            if d is None or d is o:
                return
            k = id(d)
            if k in deps:
                deps[k] = (d, deps[k][1] or strong)
            else:
                deps[k] = (d, strong)
        for b in r:
            add(b.w, True)
        for b in w:
            add(b.w, True)
            for d in b.r.values():
                add(d, False)
            for d in b.rd:
                add(d, True)
        o.deps = list(deps.values())
        self.q[eng].append(o)
        self.last_real[eng] = o
        return o

    def _post(self, o, r, w):
        for b in r:
            if o.dma:
                b.rd.append(o)
            else:
                b.r[o.eng] = o
        for b in w:
            b.w = o
            b.r = {}
            b.rd = []

    def cop(self, eng, fn, r=(), w=()):
        o = self.op(eng, fn, r, w)
        self._post(o, r, w)
        return o

    def dma(self, out, in_, r=(), w=(), eng='sp'):
        o = self.op(eng, lambda e: e.dma_start(out=out, in_=in_), r, w)
        o.dma = True
        o.flag = True
        n = len(self.dmas)
        o.slot = n % self.ND
        o.val = 16 * (n // self.ND + 1)
        if n >= self.ND:
            o.deps.append((self.dmas[n - self.ND], True))
        self.dmas.append(o)
        self.pending_dma.append(o)
        self._post(o, r, w)
        return o

    def barrier(self):
        last = list(self.last_real.values())
        self.last_real = {}
        pend = self.pending_dma
        self.pending_dma = []
        for e in ENGS:
            o = Op()
            o.eng = e
            o.fn = None
            o.flag = False
            o.dma = False
            o.deps = [(d, True) for d in last if d.eng != e or d.dma] + [(d, True) for d in pend]
            self.q[e].append(o)

    def mm(self, out, lhsT, rhs, start, stop, r=(), w=(), skip=False):
        return self.cop('pe', lambda e: e.matmul(out, lhsT, rhs, start=start, stop=stop, skip_group_check=skip), r, w)

    def tr(self, out, in_, ident, r=(), w=()):
        return self.cop('pe', lambda e: e.transpose(out, in_, ident), r, w)

    def act(self, out, in_, func, r=(), w=(), **kw):
        return self.cop('act', lambda e: e.activation(out, in_, func, **kw), r, w)

    def ts(self, eng, out, in0, s1, s2, op0, op1=None, r=(), w=()):
        if op1 is None and eng == 'pool' and op0 == ALU.mult:
            return self.cop(eng, lambda e: e.tensor_scalar(out, in0, s1, 0.0, ALU.mult, ALU.add), r, w)
        if op1 is None:
            return self.cop(eng, lambda e: e.tensor_scalar(out, in0, s1, None, op0), r, w)
        return self.cop(eng, lambda e: e.tensor_scalar(out, in0, s1, s2, op0, op1), r, w)

    def tt(self, eng, out, in0, in1, op, r=(), w=()):
        return self.cop(eng, lambda e: e.tensor_tensor(out, in0, in1, op), r, w)

    def stt(self, eng, out, in0, scalar, in1, op0, op1, r=(), w=()):
        return self.cop(eng, lambda e: e.scalar_tensor_tensor(out, in0, scalar, in1, op0, op1), r, w)

    def copy(self, eng, out, in_, r=(), w=()):
        if eng == 'act':
            return self.cop('act', lambda e: e.copy(out, in_), r, w)
        return self.cop(eng, lambda e: e.tensor_copy(out, in_), r, w)

    def memset(self, eng, ap, val, w=()):
        return self.cop(eng, lambda e: e.memset(ap, val), (), w)

    @staticmethod
    def _needs(o, d, strong):
        if d.dma:
            return True
        if d.eng != o.eng:
            return True
        if o.eng == 'pe':
            return False
        return strong

    def emit(self, es):
        nc = self.nc
        for e in ENGS:
            for o in self.q[e]:
                for (d, strong) in o.deps:
                    if self._needs(o, d, strong):
                        d.flag = True
        esem = {e: es.enter_context(nc.semaphore('S_' + e)) for e in ENGS}
        dsem = [es.enter_context(nc.semaphore('D%d' % i)) for i in range(self.ND)]
        for e in ENGS:
            cnt = 0
            for o in self.q[e]:
                if o.dma:
                    o.sem = dsem[o.slot]
                elif o.flag:
                    cnt += 1
                    o.sem = esem[e]
                    o.val = cnt
        block = es.enter_context(nc.Block())

        def run(e, eng):
            waited = {}
            for o in self.q[e]:
                for (d, strong) in o.deps:
                    if not self._needs(o, d, strong):
                        continue
                    k = id(d.sem)
                    if waited.get(k, 0) >= d.val:
                        continue
                    eng.wait_ge(d.sem, d.val)
                    waited[k] = d.val
                if o.fn is None:
                    continue
                inst = o.fn(eng)
                if o.flag:
                    inst.then_inc(o.sem, 16 if o.dma else 1)

        block.sync(lambda eng: run('sp', eng))
        block.scalar(lambda eng: run('act', eng))
        block.vector(lambda eng: run('dve', eng))
        block.gpsimd(lambda eng: run('pool', eng))
        block.tensor(lambda eng: run('pe', eng))


def build(S, stop_after=None):
    nc = bass.Bass("TRN2", target_bir_lowering=False)
    NB = S // 512
    NT = S // 128

    def din(name, shape, dt=F32):
        return nc.dram_tensor(name, list(shape), dt, kind="ExternalInput").ap()

    def dscr(name, shape, dt):
        return nc.dram_tensor(name, list(shape), dt).ap()

    x_in = din("x", [S, D])
    ab_w_in = din("ab_w_in", [D, 3072])
    ab_w_o = din("ab_w_o", [D, D])
    ab_norm = din("ab_norm", [128, 8])
    relE = din("relE", [8, 128, 1408])
    cd_w_in = din("cd_w_in", [D, 2216])
    cd_w_o = din("cd_w_o", [D, D])
    cd_norm = din("cd_norm", [128, 8])
    cd_q_norm = din("cd_q_norm", [128, 3])
    cd_kv_norm = din("cd_kv_norm", [128, 2])
    cd_w_uq = din("cd_w_uq", [384, 768])
    cd_w_ukv = din("cd_w_ukv", [256, 1024])
    cd_b_f = din("cd_b_f", [8, 1])
    ffn_norm = din("ffn_norm", [2, 128, 8])
    ffn_w_gate = din("ffn_w_gate", [2, D, DFF])
    ffn_w_up = din("ffn_w_up", [2, D, DFF])
    ffn_w_down = din("ffn_w_down", [2, DFF, D])
    ffn_conv = din("ffn_conv", [2, 128, NFF, 4])
    final_norm = din("final_norm", [1, D])
    c_ident = din("c_ident", [128, 128], BF16)
    c_tri = din("c_tri", [2, 128, 128], BF16)
    c_maskA = din("c_maskA", [8, 128, 512])
    c_maskD = din("c_maskD", [3, 4, 128, 512], BF16)
    c_m01 = din("c_m01", [2, 128, 128], BF16)
    c_rope = din("c_rope", [2, 32, S])
    out = nc.dram_tensor("out", [S, D], F32, kind="ExternalOutput").ap()

    QT = dscr("QT_scr", [16, 96, S], BF16)
    KT = dscr("KT_scr", [16, 96, S], BF16)
    VS = dscr("V_scr", [16, 128, NT, 65], BF16)
    OS = dscr("O_scr", [NT, 128, D], BF16)
    KR = dscr("KR_scr", [32, S], BF16)
    xa = dscr("xa", [S, D], F32)
    xb = dscr("xb", [S, D], F32)

    K = Sched(nc)
    es_top = contextlib.ExitStack()

    uid = [0]

    def sb(es, name, shape, dt):
        uid[0] += 1
        return es.enter_context(nc.sbuf_tensor(f"{name}_{uid[0]}", list(shape), dt))

    def ps(es, name, shape, dt):
        uid[0] += 1
        return es.enter_context(nc.psum_tensor(f"{name}_{uid[0]}", list(shape), dt))

    ident = sb(es_top, "ident", [128, 128], BF16)
    b_ident = Buf()
    K.dma(ident[:], c_ident[:, :], w=[b_ident])

    def load_weight_folded(es, Wt, wbuf, src, ncols, g_sb, b_g, kchunks, colchunk, name):
        st = [sb(es, f"{name}_st{i}", [128, colchunk], F32) for i in range(2)]
        bst = [Buf(), Buf()]
        i = 0
        for kc in range(kchunks):
            for c0 in range(0, ncols, colchunk):
                cw = min(colchunk, ncols - c0)
                j = i % 2
                K.dma(st[j][:, 0:cw], src[kc * 128:(kc + 1) * 128, c0:c0 + cw], w=[bst[j]])
                eng = 'dve' if i % 2 == 0 else 'pool'
                if g_sb is None:
                    K.copy(eng, Wt[:, kc, c0:c0 + cw], st[j][:, 0:cw], r=[bst[j]], w=[wbuf])
                else:
                    K.ts(eng, Wt[:, kc, c0:c0 + cw], st[j][:, 0:cw], g_sb[:, kc:kc + 1], None, ALU.mult,
                         r=[bst[j], b_g], w=[wbuf])
                i += 1

    def norm_block(xt, bx, ntile, ss, lnv, rstd, junk, bjunk, bss, width=D):
        for t in range(ntile):
            K.act(junk[:, 0:width], xt[:, t, 0:width], AF.Square, r=[bx[t]], w=[bjunk, bss],
                  accum_out=ss[:, t:t + 1])
        K.act(lnv[:, 0:ntile], ss[:, 0:ntile], AF.Ln, r=[bss], w=[bss], bias=EPS, scale=1.0 / width)
        K.act(rstd[:, 0:ntile], lnv[:, 0:ntile], AF.Exp, r=[bss], w=[bss], scale=-0.5)

    def phase_inproj_ab(xsrc):
        with contextlib.ExitStack() as es:
            W = sb(es, "W_ab", [128, 8, 3072], BF16)
            bW = Buf()
            g_sb = sb(es, "g_ab", [128, 8], F32)
            b_g = Buf()
            K.dma(g_sb[:], ab_norm[:, :], w=[b_g])
            load_weight_folded(es, W, bW, ab_w_in, 3072, g_sb, b_g, 8, 1536, "wab")
            xt = [sb(es, f"xt{i}", [128, 4, D], F32) for i in range(2)]
            bxt = [[Buf() for _ in range(4)] for _ in range(2)]
            junk = sb(es, "junk", [128, D], BF16)
            bjunk = Buf()
            ss = sb(es, "ss", [128, 4], F32)
            lnv = sb(es, "lnv", [128, 4], F32)
            rstd = sb(es, "rstd", [128, 4], F32)
            bss = Buf()
            xn = sb(es, "xn", [128, 4, D], BF16)
            bxn = [Buf() for _ in range(4)]
            hT = [sb(es, f"hT{i}", [128, 8, 512], BF16) for i in range(2)]
            bhT = [Buf(), Buf()]
            qst = [sb(es, f"qst{i}", [128, 512], BF16) for i in range(4)]
            bqst = [Buf() for _ in range(4)]
            vst = [sb(es, f"vst{i}", [128, 16, 4, 65], BF16) for i in range(2)]
            bvst = [Buf(), Buf()]
            tp = [ps(es, f"tp{i}", [128, 512], BF16) for i in range(2)]
            btp = [Buf(), Buf()]
            pm = [ps(es, f"pm{i}", [128, 512], F32) for i in range(4)]
            bpm = [Buf() for _ in range(4)]
            for i in range(2):
                K.memset('pool', vst[i][:, :, :, 64:65], 1.0, w=[bvst[i]])
            ipm = 0
            iq = 0
            for b in range(NB):
                xb_ = b % 2
                src = xsrc[b * 512:(b + 1) * 512, :].rearrange("(t p) c -> p t c", p=128)
                for t in range(4):
                    K.dma(xt[xb_][:, t, :], src[:, t, :], w=[bxt[xb_][t]])
                norm_block(xt[xb_], bxt[xb_], 4, ss, lnv, rstd, junk, bjunk, bss)
                for t in range(4):
                    K.ts('dve' if t % 2 == 0 else 'pool', xn[:, t, :], xt[xb_][:, t, :], rstd[:, t:t + 1], None,
                         ALU.mult, r=[bxt[xb_][t], bss], w=[bxn[t]])
                hb = b % 2
                for kc in range(8):
                    j = kc % 2
                    for t in range(4):
                        K.tr(tp[j][:, t * 128:(t + 1) * 128], xn[:, t, kc * 128:(kc + 1) * 128], ident[:],
                             r=[bxn[t], b_ident], w=[btp[j]])
                    K.copy('act' if kc % 2 == 0 else 'dve', hT[hb][:, kc, :], tp[j][:, :], r=[btp[j]], w=[bhT[hb]])
                for (c0, dst, h0) in ((0, QT, 0), (512, KT, 0), (1536, QT, 8), (2048, KT, 8)):
                    for hp in range(4):
                        bk = ipm % 4
                        ipm += 1
                        for kc in range(8):
                            K.mm(pm[bk][:, :], W[:, kc, c0 + hp * 128:c0 + (hp + 1) * 128], hT[hb][:, kc, :],
                                 kc == 0, kc == 7, r=[bW, bhT[hb]], w=[bpm[bk]])
                        qi = iq % 4
                        iq += 1
                        K.copy('act' if qi % 2 == 0 else 'dve', qst[qi][:, :], pm[bk][:, :], r=[bpm[bk]], w=[bqst[qi]])
                        for hh in range(2):
                            K.dma(dst[h0 + hp * 2 + hh, 0:64, b * 512:(b + 1) * 512], qst[qi][hh * 64:(hh + 1) * 64, :],
                                  r=[bqst[qi]])
                vb_ = b % 2
                for (c0, h0) in ((1024, 0), (2560, 8)):
                    for t in range(4):
                        bk = ipm % 4
                        ipm += 1
                        for kc in range(8):
                            K.mm(pm[bk][:, :], hT[hb][:, kc, t * 128:(t + 1) * 128], W[:, kc, c0:c0 + 512],
                                 kc == 0, kc == 7, r=[bW, bhT[hb]], w=[bpm[bk]])
                        K.copy('act' if t % 2 == 0 else 'dve', vst[vb_][:, h0:h0 + 8, t, 0:64],
                               pm[bk][:, :].rearrange("p (h d) -> p h d", d=64), r=[bpm[bk]], w=[bvst[vb_]])
                K.dma(VS[:, :, b * 4:(b + 1) * 4, :].rearrange("h p t c -> p h t c"), vst[vb_][:, :, :, :], r=[bvst[vb_]])
            K.barrier()

    def attn_softmax_heads(es, heads, krows, scale, kind, maskset=None, bias_src=None, kr_rows=False):
        kt_sb = [sb(es, f"kt{i}", [96, S], BF16) for i in range(2)]
        qt_sb = [sb(es, f"qt{i}", [96, S], BF16) for i in range(2)]
        v_sb = [sb(es, f"v{i}", [128, NT, 65], BF16) for i in range(2)]
        o_sb = [sb(es, f"o{i}", [128, NT, 64], BF16) for i in range(2)]
        bkt = [Buf(), Buf()]
        bqt = [Buf(), Buf()]
        bv = [Buf(), Buf()]
        bo = [Buf(), Buf()]
        pt = [sb(es, f"pt{i}", [128, 512], BF16) for i in range(4)]
        bpt = [Buf() for _ in range(4)]
        rc = sb(es, "rc", [128, 4], F32)
        brc = Buf()
        z = [ps(es, f"z{i}", [128, 512], F32) for i in range(3)]
        bz = [Buf() for _ in range(3)]
        ob = [ps(es, f"ob{i}", [128, 512], F32) for i in range(4)]
        bob = [Buf() for _ in range(4)]
        if kind == 'A':
            bia = [sb(es, f"bia{i}", [128, 8, 512], BF16) for i in range(2)]
            bbia = [Buf(), Buf()]
            bst = [sb(es, f"bst{i}", [128, 512], F32) for i in range(2)]
            bbst = [Buf(), Buf()]
            mA = sb(es, "mA", [128, 8, 512], F32)
            bmA = Buf()
            for r_ in range(8):
                K.dma(mA[:, r_, :], c_maskA[r_, :, :], w=[bmA])
        else:
            m01 = sb(es, "m01", [128, 128], BF16)
            bm01 = Buf()
            K.dma(m01[:, :], c_m01[maskset - 1, :, :], w=[bm01])
        if kr_rows:
            for i in range(2):
                K.dma(kt_sb[i][64:96, :], KR[:, :], w=[bkt[i]])
        items = []
        for hi, h in enumerate(heads):
            for qb in range(NB):
                if kind == 'A':
                    tiles = [(qb * 4 - 4 + r_, r_) for r_ in range(8) if qb * 4 - 4 + r_ >= 0]
                else:
                    tiles = [(kt_, (kt_ - qb * 4) if kt_ >= qb * 4 else None) for kt_ in range(qb * 4 + 4)]
                for ti, (kt_, r_) in enumerate(tiles):
                    items.append((hi, h, qb, kt_, r_, ti, len(tiles)))
        kk = 96 if kr_rows else krows

        def head_prologue(hi, h):
            hb = hi % 2
            K.dma(kt_sb[hb][0:krows, :], KT[h, 0:krows, :], w=[bkt[hb]])
            K.dma(qt_sb[hb][0:kk, :], QT[h, 0:kk, :], w=[bqt[hb]])
            K.dma(v_sb[hb][:, :, :], VS[h, :, :, :], w=[bv[hb]])
            if kind == 'A':
                for r_ in range(8):
                    j = r_ % 2
                    K.dma(bst[j][:, :], bias_src[hi, :, 896 - 128 * r_:896 - 128 * r_ + 512], w=[bbst[j]])
                    K.stt('dve', bia[hb][:, r_, :], bst[j][:, :], 1.0 / scale, mA[:, r_, :],
                          ALU.mult, ALU.add, r=[bbst[j], bmA], w=[bbia[hb]])

        def stage1(i):
            hi, h, qb, kt_, r_, ti, nt_ = items[i]
            hb = hi % 2
            if qb == 0 and ti == 0:
                head_prologue(hi, h)
            zi = i % 3
            if kind == 'A':
                a0, a1 = (0, 128 * (r_ + 1)) if r_ < 4 else (128 * (r_ - 4), 512)
                K.mm(z[zi][:, a0:a1], kt_sb[hb][0:kk, kt_ * 128:(kt_ + 1) * 128],
                     qt_sb[hb][0:kk, qb * 512 + a0:qb * 512 + a1], True, False, r=[bkt[hb], bqt[hb]], w=[bz[zi]])
                K.mm(z[zi][:, a0:a1], ident[:], bia[hb][:, r_, a0:a1], False, True, r=[b_ident, bbia[hb]], w=[bz[zi]])
                K.act(pt[i % 4][:, a0:a1], z[zi][:, a0:a1], AF.Exp, r=[bz[zi]], w=[bpt[i % 4]], scale=scale)
            else:
                c0 = 0 if r_ is None else 128 * r_
                K.mm(z[zi][:, c0:512], kt_sb[hb][0:kk, kt_ * 128:(kt_ + 1) * 128],
                     qt_sb[hb][0:kk, qb * 512 + c0:(qb + 1) * 512], True, True, r=[bkt[hb], bqt[hb]], w=[bz[zi]])
                K.act(pt[i % 4][:, c0:512], z[zi][:, c0:512], AF.Exp, r=[bz[zi]], w=[bpt[i % 4]], scale=scale)
                if r_ is not None:
                    K.tt('dve', pt[i % 4][:, c0:c0 + 128], pt[i % 4][:, c0:c0 + 128], m01[:, :], ALU.mult,
                         r=[bpt[i % 4], bm01], w=[bpt[i % 4]])

        def stage2(i):
            hi, h, qb, kt_, r_, ti, nt_ = items[i]
            hb = hi % 2
            pi = i % 4
            for qs in range(4):
                if kind != 'A' and r_ is not None and qs < r_:
                    continue
                if kind == 'A':
                    if (r_ < 4 and qs > r_) or (r_ >= 4 and qs < r_ - 4):
                        continue
                    first = (r_ == qs) if qb > 0 else (r_ == 4)
                    last = (r_ == qs + 4)
                else:
                    first = (ti == 0)
                    last = (r_ is not None and r_ == qs)
                K.mm(ob[qs][:, 0:65], pt[pi][:, qs * 128:(qs + 1) * 128], v_sb[hb][:, kt_, 0:65],
                     first, last, r=[bpt[pi], bv[hb]], w=[bob[qs]])
            if ti == nt_ - 1:
                for qs in range(4):
                    K.cop('dve', lambda e, o_=rc[:, qs:qs + 1], i_=ob[qs][:, 64:65]: e.reciprocal(o_, i_),
                          r=[bob[qs]], w=[brc])
                    K.ts('dve', o_sb[hb][:, qb * 4 + qs, :], ob[qs][:, 0:64], rc[:, qs:qs + 1], None, ALU.mult,
                         r=[bob[qs], brc], w=[bo[hb]])
                if qb == NB - 1:
                    K.dma(OS[:, :, h * 64:(h + 1) * 64].rearrange("t p d -> p t d"), o_sb[hb][:, :, :], r=[bo[hb]])

        n_it = len(items)
        for i in range(n_it + 2):
            if i < n_it:
                stage1(i)
            if i >= 2:
                stage2(i - 2)

    def attn_stick_heads(es, heads, scale):
        kt_sb = [sb(es, f"skt{i}", [64, S], BF16) for i in range(2)]
        qt_sb = [sb(es, f"sqt{i}", [64, S], BF16) for i in range(2)]
        v_sb = [sb(es, f"sv{i}", [128, NT, 65], BF16) for i in range(2)]
        o_sb = [sb(es, f"so{i}", [128, NT, 64], BF16) for i in range(2)]
        bkt = [Buf(), Buf()]
        bqt = [Buf(), Buf()]
        bv = [Buf(), Buf()]
        bo = [Buf(), Buf()]
        tri = sb(es, "tri", [128, 2, 128], BF16)
        btri = Buf()
        K.dma(tri[:, 0, :], c_tri[0, :, :], w=[btri])
        K.dma(tri[:, 1, :], c_tri[1, :, :], w=[btri])
        msk = sb(es, "smsk", [128, 4, 512], BF16)
        bmsk = Buf()
        for r_ in range(4):
            K.dma(msk[:, r_, :], c_maskD[0, r_, :, :], w=[bmsk])
        NCH = 2
        ee = [[sb(es, f"ee{c}_{i}", [128, 512], F32) for i in range(4)] for c in range(NCH)]
        bee = [[Buf() for _ in range(4)] for c in range(NCH)]
        sp_ = [[sb(es, f"sp{c}_{i}", [128, 512], BF16) for i in range(4)] for c in range(NCH)]
        bsp = [[Buf() for _ in range(4)] for c in range(NCH)]
        xx = [[sb(es, f"xx{c}_{i}", [128, 512], F32) for i in range(2)] for c in range(NCH)]
        bxx = [[Buf() for _ in range(2)] for c in range(NCH)]
        pt = [[sb(es, f"spt{c}_{i}", [128, 512], BF16) for i in range(4)] for c in range(NCH)]
        bpt = [[Buf() for _ in range(4)] for c in range(NCH)]
        z = [ps(es, f"sz{c}", [128, 512], F32) for c in range(NCH)]
        bz = [Buf() for _ in range(NCH)]
        acc = [ps(es, f"sacc{c}", [128, 512], F32) for c in range(NCH)]
        bacc = [Buf() for _ in range(NCH)]
        ob = [ps(es, f"sob{c}", [128, 512], F32) for c in range(NCH)]
        bob = [Buf() for _ in range(NCH)]
        zer = sb(es, "szer", [128, 128], BF16)
        bzer = Buf()
        K.memset('pool', zer[:, :], 0.0, w=[bzer])
        items = []
        rot = [0] * NCH
        for hi, h in enumerate(heads):
            qbs = list(range(NB))
            for g0 in range(0, NB, NCH):
                grp = qbs[g0:g0 + NCH][::-1]
                chains = []
                for c, qb in enumerate(grp):
                    ch = []
                    for ti, kt_ in enumerate(range(qb * 4 + 3, -1, -1)):
                        ch.append((hi, h, qb, kt_, ti, qb * 4 + 4, c, rot[c]))
                        rot[c] += 1
                    chains.append(ch)
                mx = max(len(ch) for ch in chains)
                for k in range(mx):
                    for ch in chains:
                        if k < len(ch):
                            items.append(ch[k])

        def prologue(i):
            hi, h, qb, kt_, ti, nt_, c, rt = items[i]
            hb = hi % 2
            if i == 0 or items[i - 1][0] != hi:
                K.dma(kt_sb[hb][:, :], KT[h, 0:64, :], w=[bkt[hb]])
                K.dma(qt_sb[hb][:, :], QT[h, 0:64, :], w=[bqt[hb]])
                K.dma(v_sb[hb][:, :, :], VS[h, :, :, :], w=[bv[hb]])

        def pe_z(i):
            hi, h, qb, kt_, ti, nt_, c, rt = items[i]
            hb = hi % 2
            diag = kt_ >= qb * 4
            K.mm(z[c][:, :], kt_sb[hb][:, kt_ * 128:(kt_ + 1) * 128], qt_sb[hb][:, qb * 512:(qb + 1) * 512],
                 True, not diag, r=[bkt[hb], bqt[hb]], w=[bz[c]])
            if diag:
                K.mm(z[c][:, :], ident[:], msk[:, kt_ - qb * 4, :], False, True, r=[b_ident, bmsk], w=[bz[c]])

        def act_e(i):
            hi, h, qb, kt_, ti, nt_, c, rt = items[i]
            ei = rt % 4
            K.act(ee[c][ei][:, :], z[c][:, :], AF.Exp, r=[bz[c]], w=[bee[c][ei]], scale=scale)

        def act_ln(i):
            hi, h, qb, kt_, ti, nt_, c, rt = items[i]
            ei = rt % 4
            K.act(sp_[c][ei][:, :], ee[c][ei][:, :], AF.Ln, r=[bee[c][ei]], w=[bsp[c][ei]], bias=1.0, scale=1.0)

        def pe_tri(i):
            hi, h, qb, kt_, ti, nt_, c, rt = items[i]
            ei = rt % 4
            K.mm(acc[c][:, :], tri[:, 0, :], sp_[c][ei][:, :], ti == 0, True, r=[btri, bsp[c][ei]], w=[bacc[c]], skip=True)

        def act_x(i):
            hi, h, qb, kt_, ti, nt_, c, rt = items[i]
            xi = rt % 2
            K.act(xx[c][xi][:, :], acc[c][:, :], AF.Exp, r=[bacc[c]], w=[bxx[c][xi]], scale=-1.0)

        def pe_excl(i):
            hi, h, qb, kt_, ti, nt_, c, rt = items[i]
            ei = rt % 4
            if ti < nt_ - 1:
                K.mm(acc[c][:, :], tri[:, 1, :], sp_[c][ei][:, :], False, True, r=[btri, bsp[c][ei]], w=[bacc[c]], skip=True)

        def dve_pt(i):
            hi, h, qb, kt_, ti, nt_, c, rt = items[i]
            ei = rt % 4
            xi = rt % 2
            K.tt('dve', pt[c][ei][:, :], ee[c][ei][:, :], xx[c][xi][:, :], ALU.mult, r=[bee[c][ei], bxx[c][xi]],
                 w=[bpt[c][ei]])

        def pe_pv(i):
            hi, h, qb, kt_, ti, nt_, c, rt = items[i]
            hb = hi % 2
            ei = rt % 4
            if ti == 0:
                K.mm(ob[c][:, 0:256], zer[:, :], msk[:, 0, 0:256], True, False, r=[bzer, bmsk], w=[bob[c]], skip=True)
            for qs in range(4):
                K.mm(ob[c][:, qs * 64:(qs + 1) * 64], pt[c][ei][:, qs * 128:(qs + 1) * 128], v_sb[hb][:, kt_, 0:64],
                     False, ti == nt_ - 1, r=[bpt[c][ei], bv[hb]], w=[bob[c]], skip=True)
            if ti == nt_ - 1:
                K.copy('dve', o_sb[hb][:, qb * 4:qb * 4 + 4, :], ob[c][:, 0:256].rearrange("p (q d) -> p q d", d=64),
                       r=[bob[c]], w=[bo[hb]])
                if qb == NB - 1:
                    K.dma(OS[:, :, h * 64:(h + 1) * 64].rearrange("t p d -> p t d"), o_sb[hb][:, :, :], r=[bo[hb]])

        n_it = len(items)
        for i in range(n_it + 5):
            ok = lambda j: 0 <= j < n_it
            if ok(i):
                prologue(i)
            if ok(i - 3):
                pe_tri(i - 3)
                act_x(i - 3)
            if ok(i):
                pe_z(i)
            if ok(i - 1):
                act_ln(i - 1)
            if ok(i - 5):
                pe_pv(i - 5)
            if ok(i):
                act_e(i)
            if ok(i - 3):
                pe_excl(i - 3)
                dve_pt(i - 3)

    def phase_attn_ab():
        with contextlib.ExitStack() as es:
            attn_softmax_heads(es, list(range(8)), 64, 0.125, 'A', bias_src=relE)
            K.barrier()
        with contextlib.ExitStack() as es:
            attn_stick_heads(es, list(range(8, 16)), 0.125)
            K.barrier()

    def phase_outproj(xsrc, xdst, w_o):
        with contextlib.ExitStack() as es:
            W = sb(es, "W_o", [128, 8, D], BF16)
            bW = Buf()
            load_weight_folded(es, W, bW, w_o, D, None, None, 8, D, "wo")
            xt = [sb(es, f"oxt{i}", [128, 4, D], F32) for i in range(2)]
            bxt = [[Buf() for _ in range(4)] for _ in range(2)]
            ot = [sb(es, f"oblk{i}", [128, 4, D], BF16) for i in range(2)]
            bot = [Buf(), Buf()]
            oT = [sb(es, f"oT{i}", [128, 8, 512], BF16) for i in range(2)]
            boT = [Buf(), Buf()]
            tp = [ps(es, f"otp{i}", [128, 512], BF16) for i in range(2)]
            btp = [Buf(), Buf()]
            pm = [ps(es, f"opm{i}", [128, 512], F32) for i in range(4)]
            bpm = [Buf() for _ in range(4)]
            ipm = 0
            for b in range(NB):
                j2 = b % 2
                src = xsrc[b * 512:(b + 1) * 512, :].rearrange("(t p) c -> p t c", p=128)
                dst = xdst[b * 512:(b + 1) * 512, :].rearrange("(t p) c -> p t c", p=128)
                for t in range(4):
                    K.dma(xt[j2][:, t, :], src[:, t, :], w=[bxt[j2][t]])
                K.dma(ot[j2][:, :, :], OS[b * 4:(b + 1) * 4, :, :].rearrange("t p c -> p t c"), w=[bot[j2]])
                for fc in range(8):
                    j = fc % 2
                    for t in range(4):
                        K.tr(tp[j][:, t * 128:(t + 1) * 128], ot[j2][:, t, fc * 128:(fc + 1) * 128], ident[:],
                             r=[bot[j2], b_ident], w=[btp[j]])
                    K.copy('act' if fc % 2 == 0 else 'dve', oT[j2][:, fc, :], tp[j][:, :], r=[btp[j]], w=[boT[j2]])
                for t in range(4):
                    for half in range(2):
                        bk = ipm % 4
                        ipm += 1
                        for fc in range(8):
                            K.mm(pm[bk][:, :], oT[j2][:, fc, t * 128:(t + 1) * 128], W[:, fc, half * 512:(half + 1) * 512],
                                 fc == 0, fc == 7, r=[bW, boT[j2]], w=[bpm[bk]])
                        K.tt('dve', xt[j2][:, t, half * 512:(half + 1) * 512], pm[bk][:, :],
                             xt[j2][:, t, half * 512:(half + 1) * 512], ALU.add, r=[bpm[bk], bxt[j2][t]], w=[bxt[j2][t]])
                    K.dma(dst[:, t, :], xt[j2][:, t, :], r=[bxt[j2][t]])
            K.barrier()

    def phase_ffn(xsrc, xdst, layer, final):
        with contextlib.ExitStack() as es:
            Wg = sb(es, "Wg", [128, 8, DFF], BF16)
            Wu = sb(es, "Wu", [128, 8, DFF], BF16)
            Wd = sb(es, "Wd", [128, NFF, D], BF16)
            bWg, bWu, bWd = Buf(), Buf(), Buf()
            g_sb = sb(es, "g_ffn", [128, 8], F32)
            b_g = Buf()
            K.dma(g_sb[:], ffn_norm[layer, :, :], w=[b_g])
            cw = sb(es, "convw", [128, NFF, 4], F32)
            bcw = Buf()
            K.dma(cw[:], ffn_conv[layer, :, :, :], w=[bcw])
            with contextlib.ExitStack() as es2:
                load_weight_folded(es2, Wg, bWg, ffn_w_gate[layer], DFF, g_sb, b_g, 8, 1408, "wg")
                load_weight_folded(es2, Wu, bWu, ffn_w_up[layer], DFF, g_sb, b_g, 8, 1408, "wu")
                load_weight_folded(es2, Wd, bWd, ffn_w_down[layer], D, None, None, NFF, 1024, "wd")
                K.barrier()
            nx = 1 if final else 2
            xts = [sb(es, f"fxt{i}", [128, 4, D], F32) for i in range(nx)]
            bxts = [[Buf() for _ in range(4)] for _ in range(nx)]
            ss = sb(es, "fss", [128, 4], F32)
            lnv = sb(es, "flnv", [128, 4], F32)
            rstd = sb(es, "frstd", [128, 4], F32)
            bss = Buf()
            xn = [sb(es, f"fxn{i}", [128, D], BF16) for i in range(2)]
            bxn = [Buf(), Buf()]
            junk = xn[0]
            bjunk = bxn[0]
            hT = sb(es, "fhT", [128, 8, 512], BF16)
            bhT = [Buf() for _ in range(8)]
            yT = sb(es, "fyT", [128, NFF, 512], BF16)
            byT = [Buf() for _ in range(NFF)]
            gsb = [sb(es, f"fg{i}", [128, 514], F32) for i in range(2)]
            bgsb = [Buf(), Buf()]
            t1 = [sb(es, f"ft1{i}", [128, 512], F32) for i in range(2)]
            bt1 = [Buf(), Buf()]
            carry = sb(es, "fcarry", [128, NFF, 2], F32)
            bcarry = [Buf() for _ in range(NFF)]
            K.memset('pool', carry[:, :, :], 0.0, w=bcarry)
            if final:
                gfin = sb(es, "gfin", [128, D], F32)
                bgfin = Buf()
                K.dma(gfin[:, :], final_norm[0:1, :].broadcast_to([128, D]), w=[bgfin])
                fss = sb(es, "ffss", [128, 4], F32)
                flnv = sb(es, "fflnv", [128, 4], F32)
                frstd = sb(es, "ffrstd", [128, 4], F32)
                bfss = Buf()
            tp = [ps(es, f"ftp{i}", [128, 512], BF16) for i in range(2)]
            btp8 = [Buf() for _ in range(8)]
            pg = [ps(es, f"fpg{i}", [128, 512], F32) for i in range(2)]
            bpg = [Buf(), Buf()]
            pu = [ps(es, f"fpu{i}", [128, 512], F32) for i in range(2)]
            bpu = [Buf(), Buf()]
            pd = [ps(es, f"fpd{i}", [128, 512], F32) for i in range(2)]
            bpd = [Buf(), Buf()]
            ipd = 0
            for b in range(NB):
                src = xsrc[b * 512:(b + 1) * 512, :].rearrange("(t p) c -> p t c", p=128)
                dst = xdst[b * 512:(b + 1) * 512, :].rearrange("(t p) c -> p t c", p=128)
                xt = xts[b % nx]
                bxt = bxts[b % nx]
                for t in range(4):
                    K.dma(xt[:, t, :], src[:, t, :], w=[bxt[t]])
                norm_block(xt, bxt, 4, ss, lnv, rstd, junk, bjunk, bss)
                for t in range(4):
                    j = t % 2
                    K.ts('dve' if t % 2 == 0 else 'pool', xn[j][:, :], xt[:, t, :], rstd[:, t:t + 1], None, ALU.mult,
                         r=[bxt[t], bss], w=[bxn[j]])
                    for k4 in range(2):
                        for kq in range(4):
                            kc = k4 * 4 + kq
                            K.tr(tp[k4][:, kq * 128:(kq + 1) * 128], xn[j][:, kc * 128:(kc + 1) * 128], ident[:],
                                 r=[bxn[j], b_ident], w=[btp8[k4]])
                        K.copy('act' if k4 == 0 else 'dve', hT[:, k4 * 4:(k4 + 1) * 4, t * 128:(t + 1) * 128],
                               tp[k4][:, :].rearrange("p (k q) -> p k q", q=128), r=[btp8[k4]],
                               w=[bhT[k4 * 4 + i] for i in range(4)])
                for fc in range(NFF):
                    j = fc % 2
                    for kc in range(8):
                        K.mm(pg[j][:, :], Wg[:, kc, fc * 128:(fc + 1) * 128], hT[:, kc, :], kc == 0, kc == 7,
                             r=[bWg, bhT[kc]], w=[bpg[j]])
                    for kc in range(8):
                        K.mm(pu[j][:, :], Wu[:, kc, fc * 128:(fc + 1) * 128], hT[:, kc, :], kc == 0, kc == 7,
                             r=[bWu, bhT[kc]], w=[bpu[j]])
                    K.copy('pool', gsb[j][:, 0:2], carry[:, fc, :], r=[bcarry[fc]], w=[bgsb[j]])
                    K.copy('act', gsb[j][:, 2:514], pg[j][:, :], r=[bpg[j]], w=[bgsb[j]])
                    K.copy('pool', carry[:, fc, :], gsb[j][:, 512:514], r=[bgsb[j]], w=[bcarry[fc]])
                    K.ts('dve', t1[j][:, :], gsb[j][:, 2:514], cw[:, fc, 2:3], cw[:, fc, 3:4], ALU.mult, ALU.add,
                         r=[bgsb[j], bcw], w=[bt1[j]])
                    K.stt('dve', t1[j][:, :], gsb[j][:, 1:513], cw[:, fc, 1:2], t1[j][:, :], ALU.mult, ALU.add,
                          r=[bgsb[j], bcw, bt1[j]], w=[bt1[j]])
                    K.stt('dve', t1[j][:, :], gsb[j][:, 0:512], cw[:, fc, 0:1], t1[j][:, :], ALU.mult, ALU.add,
                          r=[bgsb[j], bcw, bt1[j]], w=[bt1[j]])
                    K.act(t1[j][:, :], t1[j][:, :], AF.Silu, r=[bt1[j]], w=[bt1[j]])
                    K.tt('dve', yT[:, fc, :], pu[j][:, :], t1[j][:, :], ALU.mult, r=[bpu[j], bt1[j]], w=[byT[fc]])
                for t in range(4):
                    for half in range(2):
                        bk = ipd % 2
                        ipd += 1
                        for fc in range(NFF):
                            K.mm(pd[bk][:, :], yT[:, fc, t * 128:(t + 1) * 128], Wd[:, fc, half * 512:(half + 1) * 512],
                                 fc == 0, fc == NFF - 1, r=[bWd, byT[fc]], w=[bpd[bk]])
                        K.tt('dve', xt[:, t, half * 512:(half + 1) * 512], pd[bk][:, :],
                             xt[:, t, half * 512:(half + 1) * 512], ALU.add, r=[bpd[bk], bxt[t]], w=[bxt[t]])
                    if final:
                        K.act(junk[:, :], xt[:, t, :], AF.Square, r=[bxt[t]], w=[bjunk, bfss], accum_out=fss[:, t:t + 1])
                        K.act(flnv[:, t:t + 1], fss[:, t:t + 1], AF.Ln, r=[bfss], w=[bfss], bias=EPS, scale=1.0 / D)
                        K.act(frstd[:, t:t + 1], flnv[:, t:t + 1], AF.Exp, r=[bfss], w=[bfss], scale=-0.5)
                        K.stt('dve', xt[:, t, :], xt[:, t, :], frstd[:, t:t + 1], gfin[:, :], ALU.mult, ALU.mult,
                              r=[bxt[t], bfss, bgfin], w=[bxt[t]])
                    K.dma(dst[:, t, :], xt[:, t, :], r=[bxt[t]])
            K.barrier()

    CUT = int(os.environ.get('DBGCUT', '99'))

    def phase_inproj_cd(xsrc):
        with contextlib.ExitStack() as es:
            NCOL = 2216
            W = sb(es, "W_cd", [128, 8, NCOL], BF16)
            bW = Buf()
            g_sb = sb(es, "g_cd", [128, 8], F32)
            gq_sb = sb(es, "gq_cd", [128, 3], F32)
            gkv_sb = sb(es, "gkv_cd", [128, 2], F32)
            b_g = Buf()
            K.dma(g_sb[:], cd_norm[:, :], w=[b_g])
            K.dma(gq_sb[:], cd_q_norm[:, :], w=[b_g])
            K.dma(gkv_sb[:], cd_kv_norm[:, :], w=[b_g])
            nbf = sb(es, "nbf", [8, 1], F32)
            bnbf = Buf()
            K.dma(nbf[:], cd_b_f[:, :], w=[bnbf])
            K.ts('dve', nbf[:], nbf[:], -1.0, None, ALU.mult, r=[bnbf], w=[bnbf])
            Wuq = sb(es, "Wuq", [128, 3, 768], BF16)
            Wuqr = sb(es, "Wuqr", [128, 3, 768], BF16)
            Wukv = sb(es, "Wukv", [128, 2, 1024], BF16)
            Wr96 = sb(es, "Wr96", [128, 8, 96], BF16)
            bWuq, bWuqr, bWukv, bWr96 = Buf(), Buf(), Buf(), Buf()
            with contextlib.ExitStack() as es2:
                load_weight_folded(es2, W, bW, cd_w_in, NCOL, g_sb, b_g, 8, 1108, "wcd")
                load_weight_folded(es2, Wuq, bWuq, cd_w_uq, 768, gq_sb, b_g, 3, 768, "wuq")
                load_weight_folded(es2, Wukv, bWukv, cd_w_ukv, 1024, gkv_sb, b_g, 2, 1024, "wukv")
                K.barrier()
            K.copy('dve', Wuqr[:, :, :], Wuq[:, :, :], r=[bWuq], w=[bWuqr])
            v4 = Wuq[:, :, :].rearrange("p c (h k) -> p c h k", k=96)
            v4r = Wuqr[:, :, :].rearrange("p c (h k) -> p c h k", k=96)
            for c in range(3):
                K.ts('dve', v4r[:, c, :, 64:80], v4[:, c, :, 80:96], -1.0, None, ALU.mult, r=[bWuq], w=[bWuqr])
                K.copy('dve', v4r[:, c, :, 80:96], v4[:, c, :, 64:80], r=[bWuq], w=[bWuqr])
            K.memset('pool', Wr96[:, :, :], 0.0, w=[bWr96])
            K.ts('dve', Wr96[:, :, 64:80], W[:, :, 656:672], -1.0, None, ALU.mult, r=[bW], w=[bWr96])
            K.copy('dve', Wr96[:, :, 80:96], W[:, :, 640:656], r=[bW], w=[bWr96])

            xt = [sb(es, f"cxt{i}", [128, 4, D], F32) for i in range(2)]
            bxt = [[Buf() for _ in range(4)] for _ in range(2)]
            junk = sb(es, "cjunk", [128, D], BF16)
            bjunk = Buf()
            ss = sb(es, "css", [128, 4], F32)
            lnv = sb(es, "clnv", [128, 4], F32)
            rstd = sb(es, "crstd", [128, 4], F32)
            bss = Buf()
            xn = sb(es, "cxn", [128, 4, D], BF16)
            bxn = [Buf() for _ in range(4)]
            hT = [sb(es, f"chT{i}", [128, 8, 512], BF16) for i in range(2)]
            bhT = [Buf(), Buf()]
            cqf = sb(es, "cqf", [128, 4, 640], F32)
            bcqf = [Buf() for _ in range(4)]
            ss2 = sb(es, "css2", [128, 8], F32)
            lnv2 = sb(es, "clnv2", [128, 8], F32)
            rstd2 = sb(es, "crstd2", [128, 8], F32)
            bss2 = Buf()
            cn = sb(es, "ccn", [128, 4, 640], BF16)
            bcn = [Buf() for _ in range(4)]
            cT = sb(es, "ccT", [128, 5, 512], BF16)
            bcT = Buf()
            qst = [sb(es, f"cqst{i}", [128, 512], BF16) for i in range(4)]
            bqst = [Buf() for _ in range(4)]
            tmpa = [sb(es, f"ctmpa{i}", [96, 512], F32) for i in range(2)]
            tmpb = [sb(es, f"ctmpb{i}", [96, 512], F32) for i in range(2)]
            btmp = [Buf(), Buf()]
            rope_sb = [sb(es, f"crope{i}", [96, 2, 512], F32) for i in range(2)]
            brope = [Buf(), Buf()]
            vst = [sb(es, f"cvst{i}", [128, 16, 4, 65], BF16) for i in range(2)]
            bvst = [Buf(), Buf()]
            kaug = sb(es, "kaug", [8, 6, 512], BF16)
            qaug = sb(es, "qaug", [8, 6, 512], BF16)
            bkaug, bqaug = Buf(), Buf()
            fe = sb(es, "fe", [8, 512], F32)
            fsp = sb(es, "fsp", [8, 512], F32)
            fones = sb(es, "fones", [8, 512], F32)
            fG = [sb(es, f"fG{i}", [8, 512], F32) for i in range(2)]
            fr1 = sb(es, "fr1", [8, 512], F32)
            fr2 = sb(es, "fr2", [8, 512], F32)
            bfe, bfsp, bfones, bfr = Buf(), Buf(), Buf(), Buf()
            bfG = [Buf(), Buf()]
            K.memset('pool', fones[:, :], 1.0, w=[bfones])
            K.memset('pool', kaug[:, 3:6, :], 1.0, w=[bkaug])
            K.memset('pool', qaug[:, 0:3, :], 1.0, w=[bqaug])
            K.memset('pool', fG[1][:, :], 0.0, w=[bfG[1]])
            tp = [ps(es, f"ctp{i}", [128, 512], BF16) for i in range(2)]
            btp = [Buf(), Buf()]
            pm = [ps(es, f"cpm{i}", [128, 512], F32) for i in range(5)]
            bpm = [Buf() for _ in range(5)]
            for i in range(2):
                K.memset('pool', vst[i][:, :, :, 64:65], 1.0, w=[bvst[i]])
            ipm = 0
            iq = 0
            itm = 0

            def nxt():
                nonlocal ipm
                bk = ipm % 5
                ipm += 1
                return bk

            def rope_rows(pa, pb, dst_ap, rb, deps_r, deps_w):
                nonlocal itm
                j = itm % 2
                itm += 1
                K.tt('dve', tmpa[j][64:96, :], pm[pa][64:96, :], rope_sb[rb][64:96, 0, :], ALU.mult,
                     r=[bpm[pa], brope[rb]], w=[btmp[j]])
                K.tt('dve', tmpb[j][64:96, :], pm[pb][64:96, :], rope_sb[rb][64:96, 1, :], ALU.mult,
                     r=[bpm[pb], brope[rb]], w=[btmp[j]])
                K.tt('pool', dst_ap, tmpa[j][64:96, :], tmpb[j][64:96, :], ALU.add, r=[btmp[j]] + deps_r, w=deps_w)

            for b in range(NB):
                xb_ = b % 2
                hb = b % 2
                src = xsrc[b * 512:(b + 1) * 512, :].rearrange("(t p) c -> p t c", p=128)
                for t in range(4):
                    K.dma(xt[xb_][:, t, :], src[:, t, :], w=[bxt[xb_][t]])
                for k2 in range(2):
                    K.dma(rope_sb[xb_][64:96, k2, :], c_rope[k2, :, b * 512:(b + 1) * 512], w=[brope[xb_]])
                norm_block(xt[xb_], bxt[xb_], 4, ss, lnv, rstd, junk, bjunk, bss)
                for t in range(4):
                    K.ts('dve' if t % 2 == 0 else 'pool', xn[:, t, :], xt[xb_][:, t, :], rstd[:, t:t + 1], None,
                         ALU.mult, r=[bxt[xb_][t], bss], w=[bxn[t]])
                for kc in range(8):
                    j = kc % 2
                    for t in range(4):
                        K.tr(tp[j][:, t * 128:(t + 1) * 128], xn[:, t, kc * 128:(kc + 1) * 128], ident[:],
                             r=[bxn[t], b_ident], w=[btp[j]])
                    K.copy('act' if kc % 2 == 0 else 'dve', hT[hb][:, kc, :], tp[j][:, :], r=[btp[j]], w=[bhT[hb]])
                if CUT < 1:
                    continue
                for t in range(4):
                    for (c0, c1, col) in ((0, 384, t), (384, 640, 4 + t)):
                        bk = nxt()
                        for kc in range(8):
                            K.mm(pm[bk][:, 0:c1 - c0], hT[hb][:, kc, t * 128:(t + 1) * 128], W[:, kc, c0:c1],
                                 kc == 0, kc == 7, r=[bW, bhT[hb]], w=[bpm[bk]])
                        K.copy('dve', cqf[:, t, c0:c1], pm[bk][:, 0:c1 - c0], r=[bpm[bk]], w=[bcqf[t]])
                        K.act(junk[:, 0:c1 - c0], cqf[:, t, c0:c1], AF.Square, r=[bcqf[t]], w=[bjunk, bss2],
                              accum_out=ss2[:, col:col + 1])
                K.act(lnv2[:, 0:4], ss2[:, 0:4], AF.Ln, r=[bss2], w=[bss2], bias=EPS, scale=1.0 / 384)
                K.act(lnv2[:, 4:8], ss2[:, 4:8], AF.Ln, r=[bss2], w=[bss2], bias=EPS, scale=1.0 / 256)
                K.act(rstd2[:, :], lnv2[:, :], AF.Exp, r=[bss2], w=[bss2], scale=-0.5)
                for t in range(4):
                    K.ts('dve', cn[:, t, 0:384], cqf[:, t, 0:384], rstd2[:, t:t + 1], None, ALU.mult,
                         r=[bcqf[t], bss2], w=[bcn[t]])
                    K.ts('pool', cn[:, t, 384:640], cqf[:, t, 384:640], rstd2[:, 4 + t:5 + t], None, ALU.mult,
                         r=[bcqf[t], bss2], w=[bcn[t]])
                for c in range(5):
                    j = c % 2
                    for t in range(4):
                        K.tr(tp[j][:, t * 128:(t + 1) * 128], cn[:, t, c * 128:(c + 1) * 128], ident[:],
                             r=[bcn[t], b_ident], w=[btp[j]])
                    K.copy('act' if c % 2 == 0 else 'dve', cT[:, c, :], tp[j][:, :], r=[btp[j]], w=[bcT])
                if CUT < 2:
                    continue
                for h in range(8):
                    pa = nxt()
                    for c in range(3):
                        K.mm(pm[pa][0:96, :], Wuq[:, c, h * 96:(h + 1) * 96], cT[:, c, :], c == 0, c == 2,
                             r=[bWuq, bcT], w=[bpm[pa]])
                    pb = nxt()
                    for c in range(3):
                        K.mm(pm[pb][0:96, :], Wuqr[:, c, h * 96:(h + 1) * 96], cT[:, c, :], c == 0, c == 2,
                             r=[bWuqr, bcT], w=[bpm[pb]])
                    qi = iq % 4
                    iq += 1
                    K.copy('dve', qst[qi][0:64, :], pm[pa][0:64, :], r=[bpm[pa]], w=[bqst[qi]])
                    rope_rows(pa, pb, qst[qi][64:96, :], xb_, [], [bqst[qi]])
                    K.dma(QT[h, 0:96, b * 512:(b + 1) * 512], qst[qi][0:96, :], r=[bqst[qi]])
                if CUT < 3:
                    continue
                for h in range(8):
                    pa = nxt()
                    for c in range(2):
                        K.mm(pm[pa][0:64, :], Wukv[:, c, h * 128:h * 128 + 64], cT[:, 3 + c, :], c == 0, c == 1,
                             r=[bWukv, bcT], w=[bpm[pa]])
                    qi = iq % 4
                    iq += 1
                    K.copy('act' if h % 2 == 0 else 'dve', qst[qi][0:64, :], pm[pa][0:64, :], r=[bpm[pa]], w=[bqst[qi]])
                    K.dma(KT[h, 0:64, b * 512:(b + 1) * 512], qst[qi][0:64, :], r=[bqst[qi]])
                pa = nxt()
                for kc in range(8):
                    K.mm(pm[pa][0:96, :], W[:, kc, 576:672], hT[hb][:, kc, :], kc == 0, kc == 7, r=[bW, bhT[hb]], w=[bpm[pa]])
                pb = nxt()
                for kc in range(8):
                    K.mm(pm[pb][0:96, :], Wr96[:, kc, :], hT[hb][:, kc, :], kc == 0, kc == 7, r=[bWr96, bhT[hb]], w=[bpm[pb]])
                qi = iq % 4
                iq += 1
                rope_rows(pa, pb, qst[qi][64:96, :], xb_, [], [bqst[qi]])
                K.dma(KR[:, b * 512:(b + 1) * 512], qst[qi][64:96, :], r=[bqst[qi]])
                if CUT < 4:
                    continue
                for (c0, dst, h0) in ((672, QT, 8), (1184, KT, 8)):
                    for hp in range(4):
                        bk = nxt()
                        for kc in range(8):
                            K.mm(pm[bk][:, :], W[:, kc, c0 + hp * 128:c0 + (hp + 1) * 128], hT[hb][:, kc, :],
                                 kc == 0, kc == 7, r=[bW, bhT[hb]], w=[bpm[bk]])
                        qi = iq % 4
                        iq += 1
                        K.copy('act' if qi % 2 == 0 else 'dve', qst[qi][:, :], pm[bk][:, :], r=[bpm[bk]], w=[bqst[qi]])
                        for hh in range(2):
                            K.dma(dst[h0 + hp * 2 + hh, 0:64, b * 512:(b + 1) * 512], qst[qi][hh * 64:(hh + 1) * 64, :],
                                  r=[bqst[qi]])
                if CUT < 5:
                    continue
                vb_ = b % 2
                for t in range(4):
                    bk = nxt()
                    for c in range(2):
                        K.mm(pm[bk][:, :].rearrange("p (h d) -> p h d", d=64), cT[:, 3 + c, t * 128:(t + 1) * 128],
                             Wukv[:, c, :].rearrange("p (h k) -> p h k", k=128)[:, :, 64:128],
                             c == 0, c == 1, r=[bWukv, bcT], w=[bpm[bk]])
                    K.copy('act' if t % 2 == 0 else 'dve', vst[vb_][:, 0:8, t, 0:64],
                           pm[bk][:, :].rearrange("p (h d) -> p h d", d=64), r=[bpm[bk]], w=[bvst[vb_]])
                for t in range(4):
                    bk = nxt()
                    for kc in range(8):
                        K.mm(pm[bk][:, :], hT[hb][:, kc, t * 128:(t + 1) * 128], W[:, kc, 1696:2208],
                             kc == 0, kc == 7, r=[bW, bhT[hb]], w=[bpm[bk]])
                    K.copy('act' if t % 2 == 0 else 'dve', vst[vb_][:, 8:16, t, 0:64],
                           pm[bk][:, :].rearrange("p (h d) -> p h d", d=64), r=[bpm[bk]], w=[bvst[vb_]])
                K.dma(VS[:, :, b * 4:(b + 1) * 4, :].rearrange("h p t c -> p h t c"), vst[vb_][:, :, :, :], r=[bvst[vb_]])
                if CUT < 6:
                    continue
                bk = nxt()
                for kc in range(8):
                    K.mm(pm[bk][0:8, :], W[:, kc, 2208:2216], hT[hb][:, kc, :], kc == 0, kc == 7, r=[bW, bhT[hb]], w=[bpm[bk]])
                K.act(fe[:, :], pm[bk][0:8, :], AF.Exp, r=[bpm[bk], bnbf], w=[bfe], bias=nbf[:, 0:1], scale=-1.0)
                K.act(fsp[:, :], fe[:, :], AF.Ln, r=[bfe], w=[bfsp], bias=1.0, scale=1.0)
                gi = b % 2
                gp = (b + 1) % 2
                K.cop('dve', lambda e, o_=fG[gi][:, :], d0=fones[:, :], d1=fsp[:, :], ini=fG[gp][:, 511:512]:
                      e.tensor_tensor_scan(o_, d0, d1, ini, ALU.mult, ALU.add),
                      r=[bfones, bfsp, bfG[gp]], w=[bfG[gi]])
                K.ts('dve', fr1[:, :], fG[gi][:, :], 8.0, None, ALU.mult, r=[bfG[gi]], w=[bfr])
                K.copy('dve', kaug[:, 0, :], fr1[:, :], r=[bfr], w=[bkaug])
                K.tt('dve', fr2[:, :], fr1[:, :], kaug[:, 0, :], ALU.subtract, r=[bfr, bkaug], w=[bfr])
                K.copy('dve', kaug[:, 1, :], fr2[:, :], r=[bfr], w=[bkaug])
                K.tt('dve', fr1[:, :], fr2[:, :], kaug[:, 1, :], ALU.subtract, r=[bfr, bkaug], w=[bfr])
                K.copy('dve', kaug[:, 2, :], fr1[:, :], r=[bfr], w=[bkaug])
                K.ts('dve', qaug[:, 3:6, :], kaug[:, 0:3, :], -1.0, None, ALU.mult, r=[bkaug], w=[bqaug])
                K.dma(KT[8:16, 64:70, b * 512:(b + 1) * 512], kaug[:, :, :], r=[bkaug])
                K.dma(QT[8:16, 64:70, b * 512:(b + 1) * 512], qaug[:, :, :], r=[bqaug])
            K.barrier()

    def phase_attn_cd():
        with contextlib.ExitStack() as es:
            attn_softmax_heads(es, list(range(8)), 64, 96.0 ** -0.5, 'C', maskset=2, kr_rows=True)
            K.barrier()
        with contextlib.ExitStack() as es:
            attn_softmax_heads(es, list(range(8, 16)), 70, 0.125, 'D', maskset=1)
            K.barrier()

    phase_inproj_ab(x_in)
    if stop_after != 'inproj0':
        phase_attn_ab()
        phase_outproj(x_in, xa, ab_w_o)
        phase_ffn(xa, out if stop_after == 'l0' else xb, 0, False)
    if stop_after is None or stop_after.startswith('dbg'):
        dbg = stop_after or 'dbg:proj,attnC,attnD,out,ffn'
        if stop_after is None or 'proj' in dbg:
            phase_inproj_cd(xb)
        if stop_after is None or 'attnC' in dbg:
            with contextlib.ExitStack() as es:
                attn_softmax_heads(es, list(range(8)), 64, 96.0 ** -0.5, 'C', maskset=2, kr_rows=True)
                K.barrier()
        if stop_after is None or 'attnD' in dbg:
            with contextlib.ExitStack() as es:
                attn_softmax_heads(es, list(range(8, 16)), 70, 0.125, 'D', maskset=1)
                K.barrier()
        if stop_after is None or 'out' in dbg:
            phase_outproj(xb, xa, cd_w_o)
        if stop_after is None or 'ffn' in dbg:
            phase_ffn(xa, out, 1, True)
    K.emit(es_top)
    es_top.close()
    return nc


def host_consts(S):
    c = {}
    c["c_ident"] = np.eye(128, dtype=np.float32).astype(ml_dtypes.bfloat16)
    j = np.arange(128)[:, None]
    s = np.arange(128)[None, :]
    tri = np.stack([(j >= s), (j < s)]).astype(np.float32)
    c["c_tri"] = tri.astype(ml_dtypes.bfloat16)
    sk = np.arange(128)[:, None]
    tq = np.arange(512)[None, :]
    mA = np.zeros((8, 128, 512), np.float32)
    for r in range(8):
        kc_ = (128 * r + sk) // 64
        qc_ = tq // 64 + 8
        valid = (kc_ <= qc_) & (kc_ >= qc_ - 8)
        mA[r] = np.where(valid, 0.0, NEG * 8)
    c["c_maskA"] = mA
    mD = np.zeros((3, 4, 128, 512), np.float32)
    for r in range(4):
        ka = 128 * r + sk
        mD[0, r] = np.where(ka < tq, 0.0, NEG)
        mD[1, r] = np.where(ka <= tq, 0.0, NEG)
        mD[2, r] = np.where(ka // 64 <= tq // 64, 0.0, NEG)
    c["c_maskD"] = mD.astype(ml_dtypes.bfloat16)
    t1_ = np.arange(128)[None, :]
    c["c_m01"] = np.stack([(sk <= t1_), (sk // 64 <= t1_ // 64)]).astype(np.float32).astype(ml_dtypes.bfloat16)
    half = 16
    inv = 10000.0 ** (-np.arange(half, dtype=np.float32) / half)
    ang = np.arange(S, dtype=np.float32)[None, :] * inv[:, None]
    cos = np.cos(ang).astype(np.float32)
    sin = np.sin(ang).astype(np.float32)
    c["c_rope"] = np.stack([np.concatenate([cos, cos], 0), np.concatenate([sin, sin], 0)]).astype(np.float32)
    return c


def host_layout(inp, S):
    m = {}
    f = lambda a: np.ascontiguousarray(a, dtype=np.float32)
    m["ab_w_in"] = f(inp["ab_w_in"][0])
    m["ab_w_o"] = f(inp["ab_w_o"][0])
    m["ab_norm"] = f(inp["ab_norm"][0].reshape(8, 128).T)
    rb = np.asarray(inp["ab_rel_bias"][0], np.float32)
    sk = np.arange(128)[:, None]
    jj = np.arange(1408)[None, :] - 384
    idx = np.clip(jj - sk, -128, 128) + 128
    m["relE"] = f(rb[:, idx])
    m["cd_w_in"] = f(inp["cd_w_in"][0])
    m["cd_w_o"] = f(inp["cd_w_o"][0])
    m["cd_norm"] = f(inp["cd_norm"][0].reshape(8, 128).T)
    m["cd_q_norm"] = f(inp["cd_q_norm"][0].reshape(3, 128).T)
    m["cd_kv_norm"] = f(inp["cd_kv_norm"][0].reshape(2, 128).T)
    m["cd_w_uq"] = f(inp["cd_w_uq"][0])
    m["cd_w_ukv"] = f(inp["cd_w_ukv"][0])
    m["cd_b_f"] = f(inp["cd_b_f"][0].reshape(8, 1))
    m["ffn_norm"] = f(np.stack([inp["ffn_norm"][l].reshape(8, 128).T for l in range(2)]))
    m["ffn_w_gate"] = f(inp["ffn_w_gate"])
    m["ffn_w_up"] = f(inp["ffn_w_up"])
    m["ffn_w_down"] = f(inp["ffn_w_down"])
    cw = np.zeros((2, 128, NFF, 4), np.float32)
    for l in range(2):
        for k in range(3):
            cw[l, :, :, k] = np.asarray(inp["ffn_conv_w"][l, k]).reshape(NFF, 128).T
        cw[l, :, :, 3] = np.asarray(inp["ffn_conv_b"][l]).reshape(NFF, 128).T
    m["ffn_conv"] = cw
    m["final_norm"] = f(np.asarray(inp["final_norm"]).reshape(1, D))
    m.update(host_consts(S))
    return m


_CACHE = {}


def kernel(**inputs):
    x = np.asarray(inputs["x"], np.float32)
    B, S, _ = x.shape
    key = (S,)
    if key not in _CACHE:
        _CACHE[key] = build(S)
    nc = _CACHE[key]
    shared = host_layout(inputs, S)
    n = 8
    in_maps = []
    for c in range(n):
        mp = dict(shared)
        mp["x"] = np.ascontiguousarray(x[c % B])
        in_maps.append(mp)
    res = run_bass_kernel_spmd(nc, in_maps, core_ids=list(range(n)))
    return np.stack([res.results[b]["out"] for b in range(B)], axis=0).astype(np.float32)
```
